# Optimizing a Trainium2 kernel written in Bass

```python
import math
import jax
import jax.numpy as jnp
from jax import lax
import numpy as np

D_MODEL = 1024
BATCH = 2
SEQ = 16384
DEPTH = 2

ML_HEADS = 4
ML_DIM = 64
ML_WIDTH = ML_HEADS * ML_DIM
ML_CHUNK = 64
CONV_W = 4
DA_HEADS = 4
DA_QK_DIM = 32
DA_V_DIM = 2 * DA_QK_DIM
DA_WIDTH = DA_HEADS * DA_V_DIM
NSA_HEADS = 8
NSA_GROUPS = 2
NSA_REP = NSA_HEADS // NSA_GROUPS
NSA_DIM = 64
NSA_WIDTH = NSA_HEADS * NSA_DIM
NSA_KV = NSA_GROUPS * NSA_DIM
CMP_BLOCK = 32
CMP_STRIDE = 16
CMP_HIDDEN = 4 * NSA_DIM
SEL_BLOCK = 64
SEL_TOPK = 16
WINDOW = 512
Q_BLOCK = 128
D_MIX = ML_WIDTH + DA_WIDTH + NSA_WIDTH
D_FF = 4 * D_MODEL
EPS = 1e-6
FORCE_SCORE = 1e4
IN_SIZES = (ML_WIDTH, ML_WIDTH, ML_WIDTH, ML_WIDTH, ML_HEADS, ML_HEADS,
            2 * DA_HEADS * DA_QK_DIM, 2 * DA_HEADS * DA_QK_DIM, DA_WIDTH,
            NSA_WIDTH, NSA_KV, NSA_KV, NSA_KV, NSA_KV, NSA_KV, NSA_KV, 3 * NSA_HEADS)
IN_COLS = sum(IN_SIZES)

kernel_name = 'hybrid_mlstm_diffattn_nsa_block'

F32 = jnp.float32


def rmsnorm(x, g):
    xf = x.astype(F32)
    y = xf * lax.rsqrt(jnp.mean(xf * xf, axis=-1, keepdims=True) + EPS)
    return (y * g).astype(x.dtype)


def head_rmsnorm(x, g, heads):
    B, S, W = x.shape
    xf = x.astype(F32).reshape(B, S, heads, W // heads)
    y = xf * lax.rsqrt(jnp.mean(xf * xf, axis=-1, keepdims=True) + EPS)
    return y.reshape(B, S, W) * g


def masked_softmax(s, mask):
    s = jnp.where(mask, s.astype(F32), -jnp.inf)
    m = jnp.max(s, axis=-1, keepdims=True)
    m = jnp.where(jnp.isfinite(m), m, 0.0)
    p = jnp.exp(s - m)
    den = jnp.sum(p, axis=-1, keepdims=True)
    return p / jnp.where(den > 0, den, 1.0)


def causal_dwconv(x, w):
    return lax.conv_general_dilated(
        x, w[:, None, :].astype(x.dtype), window_strides=(1,), padding=[(CONV_W - 1, 0)],
        dimension_numbers=('NWC', 'WIO', 'NWC'), feature_group_count=x.shape[-1])


def mlstm(q, k, v, i_pre, f_pre):
    B, S, H, d = q.shape
    L = ML_CHUNK
    NC = S // L

    def chunk(t):
        return t.astype(F32).reshape(B, NC, L, H, -1).transpose(0, 3, 1, 2, 4)

    q = chunk(q)
    k = chunk(k) * (d ** -0.5)
    v = chunk(v)
    ig = i_pre.astype(F32).reshape(B, NC, L, H).transpose(0, 3, 1, 2)
    logf = jax.nn.log_sigmoid(f_pre.astype(F32)).reshape(B, NC, L, H).transpose(0, 3, 1, 2)
    b = jnp.cumsum(logf, axis=-1)
    g = b[..., -1]
    a = g[..., None] - b + ig
    a_max = jnp.max(a, axis=-1)
    w = jnp.exp(a - a_max[..., None])
    c_loc = jnp.einsum('bhcl,bhclv,bhclk->bhcvk', w, v, k)
    n_loc = jnp.einsum('bhcl,bhclk->bhck', w, k)

    def step(carry, xs):
        C, n, m = carry
        gc, amc, Cl, nl = xs
        m_new = jnp.maximum(gc + m, amc)
        s_prev = jnp.exp(gc + m - m_new)
        s_loc = jnp.exp(amc - m_new)
        C_new = s_prev[..., None, None] * C + s_loc[..., None, None] * Cl
        n_new = s_prev[..., None] * n + s_loc[..., None] * nl
        return (C_new, n_new, m_new), (C, n, m)

    init = (jnp.zeros((B, H, d, d), F32), jnp.zeros((B, H, d), F32), jnp.zeros((B, H), F32))
    xs = (jnp.moveaxis(g, 2, 0), jnp.moveaxis(a_max, 2, 0),
          jnp.moveaxis(c_loc, 2, 0), jnp.moveaxis(n_loc, 2, 0))
    _, (C_prev, n_prev, m_prev) = lax.scan(step, init, xs)
    C_prev = jnp.moveaxis(C_prev, 0, 2)
    n_prev = jnp.moveaxis(n_prev, 0, 2)
    m_prev = jnp.moveaxis(m_prev, 0, 2)

    causal = jnp.tril(jnp.ones((L, L), dtype=bool))
    D = jnp.where(causal, b[..., :, None] - b[..., None, :] + ig[..., None, :], -jnp.inf)
    inter_log = b + m_prev[..., None]
    m_t = jnp.maximum(inter_log, jnp.max(D, axis=-1))
    wts = jnp.exp(D - m_t[..., None]) * jnp.einsum('bhctd,bhcsd->bhcts', q, k)
    s_inter = jnp.exp(inter_log - m_t)
    num = (jnp.einsum('bhcts,bhcsv->bhctv', wts, v)
           + s_inter[..., None] * jnp.einsum('bhcvk,bhctk->bhctv', C_prev, q))
    den = jnp.sum(wts, axis=-1) + s_inter * jnp.einsum('bhck,bhctk->bhct', n_prev, q)
    h = num / jnp.maximum(jnp.abs(den), jnp.exp(-m_t))[..., None]
    return h.transpose(0, 2, 3, 1, 4).reshape(B, S, H * d)


def diff_attention(q, k, v, lam, lam_init, gain):
    B, S, _ = q.shape
    H, d = DA_HEADS, DA_QK_DIM
    q = q.reshape(B, S, H, 2, d).transpose(0, 2, 3, 1, 4) * (d ** -0.5)
    k = k.reshape(B, S, H, 2, d).transpose(0, 2, 3, 1, 4)
    v = v.reshape(B, S, H, DA_V_DIM).transpose(0, 2, 1, 3)
    kpos = jnp.arange(S)

    def block(qb):
        q0 = qb * Q_BLOCK
        qblk = lax.dynamic_slice_in_dim(q, q0, Q_BLOCK, axis=3)
        s = jnp.einsum('bhmqd,bhmkd->bhmqk', qblk, k, preferred_element_type=F32)
        qpos = q0 + jnp.arange(Q_BLOCK)
        s = jnp.where(kpos[None, :] <= qpos[:, None], s, -jnp.inf)
        p = jax.nn.softmax(s, axis=-1)
        att = p[:, :, 0] - lam * p[:, :, 1]
        return jnp.einsum('bhqk,bhkv->bhqv', att, v.astype(F32))

    o = lax.map(block, jnp.arange(S // Q_BLOCK))
    o = o.transpose(1, 0, 3, 2, 4).reshape(B, S, DA_WIDTH)
    return head_rmsnorm(o, gain, DA_HEADS) * (1.0 - lam_init)


def nsa_attention(q, kc_raw, vc_raw, ks_raw, vs_raw, kw_raw, vw_raw, gate_pre, pe, w1, w2):
    B, S, _ = q.shape
    G, R, d = NSA_GROUPS, NSA_REP, NSA_DIM
    q = q.reshape(B, S, G, R, d).transpose(0, 2, 3, 1, 4) * (d ** -0.5)
    gates = jax.nn.sigmoid(gate_pre.reshape(B, S, G, R, 3).transpose(0, 2, 3, 1, 4))

    def to_g(t):
        return t.reshape(B, S, G, d).transpose(0, 2, 1, 3)

    ncb = (S - CMP_BLOCK) // CMP_STRIDE + 1
    tok = (jnp.arange(ncb) * CMP_STRIDE)[:, None] + jnp.arange(CMP_BLOCK)[None, :]

    def compress(t, pe_, w1_, w2_):
        blk = to_g(t)[:, :, tok] + pe_
        hid = jax.nn.gelu(blk.reshape(B, G, ncb, CMP_BLOCK * d) @ w1_)
        return hid @ w2_

    k_cmp = compress(kc_raw, pe[0], w1[0], w2[0])
    v_cmp = compress(vc_raw, pe[1], w1[1], w2[1])
    cmp_end = jnp.arange(ncb) * CMP_STRIDE + CMP_BLOCK - 1

    nsb = S // SEL_BLOCK
    n_sel = min(SEL_TOPK, nsb)
    k_sel = to_g(ks_raw).reshape(B, G, nsb, SEL_BLOCK, d)
    v_sel = to_g(vs_raw).reshape(B, G, nsb, SEL_BLOCK, d)
    ratio = SEL_BLOCK // CMP_STRIDE
    lead = CMP_BLOCK // CMP_STRIDE - 1
    blk_ids = jnp.arange(nsb)
    bi = jnp.arange(B)[:, None, None, None]
    gi = jnp.arange(G)[None, :, None, None]

    pad = ((0, 0), (0, 0), (WINDOW, 0), (0, 0))
    k_win = jnp.pad(to_g(kw_raw), pad)
    v_win = jnp.pad(to_g(vw_raw), pad)

    def block(qb):
        q0 = qb * Q_BLOCK
        qblk = lax.dynamic_slice_in_dim(q, q0, Q_BLOCK, axis=3)
        gblk = lax.dynamic_slice_in_dim(gates, q0, Q_BLOCK, axis=3)
        qpos = q0 + jnp.arange(Q_BLOCK)

        s = jnp.einsum('bgrqd,bgcd->bgrqc', qblk, k_cmp, preferred_element_type=F32)
        p_cmp = masked_softmax(s, cmp_end[None, :] <= qpos[:, None])
        o_cmp = jnp.einsum('bgrqc,bgcd->bgrqd', p_cmp, v_cmp.astype(F32))

        imp = jnp.pad(jnp.sum(p_cmp, axis=2), ((0, 0), (0, 0), (0, 0), (lead, ratio + lead)))
        p_slc = jnp.zeros((B, G, Q_BLOCK, nsb), F32)
        for o in range(-lead, ratio):
            st = lead + o
            p_slc = p_slc + imp[..., st:st + ratio * nsb:ratio]
        cur = qpos // SEL_BLOCK
        forced = ((blk_ids[None, :] == 0) | (blk_ids[None, :] == cur[:, None])
                  | (blk_ids[None, :] == cur[:, None] - 1))
        causal_blk = blk_ids[None, :] * SEL_BLOCK <= qpos[:, None]
        score = jnp.where(forced, FORCE_SCORE, jnp.where(causal_blk, p_slc, -1.0))
        _, idx = lax.top_k(score, n_sel)
        ksb = k_sel[bi, gi, idx]
        vsb = v_sel[bi, gi, idx]
        pos = idx[..., None] * SEL_BLOCK + jnp.arange(SEL_BLOCK)
        smask = (pos <= qpos[:, None, None]).reshape(B, G, 1, Q_BLOCK, n_sel * SEL_BLOCK)
        s = jnp.einsum('bgrqd,bgqnld->bgrqnl', qblk, ksb, preferred_element_type=F32)
        p = masked_softmax(s.reshape(B, G, R, Q_BLOCK, n_sel * SEL_BLOCK), smask)
        o_sel = jnp.einsum('bgrqm,bgqmd->bgrqd', p,
                           vsb.reshape(B, G, Q_BLOCK, n_sel * SEL_BLOCK, d).astype(F32))

        kwb = lax.dynamic_slice_in_dim(k_win, q0, Q_BLOCK + WINDOW, axis=2)
        vwb = lax.dynamic_slice_in_dim(v_win, q0, Q_BLOCK + WINDOW, axis=2)
        wpos = q0 - WINDOW + jnp.arange(Q_BLOCK + WINDOW)
        rel = qpos[:, None] - wpos[None, :]
        wmask = (rel >= 0) & (rel < WINDOW) & (wpos[None, :] >= 0)
        s = jnp.einsum('bgrqd,bgkd->bgrqk', qblk, kwb, preferred_element_type=F32)
        o_win = jnp.einsum('bgrqk,bgkd->bgrqd', masked_softmax(s, wmask), vwb.astype(F32))

        gf = gblk.astype(F32)
        return gf[..., 0:1] * o_cmp + gf[..., 1:2] * o_sel + gf[..., 2:3] * o_win

    o = lax.map(block, jnp.arange(S // Q_BLOCK))
    return o.transpose(1, 0, 4, 2, 3, 5).reshape(B, S, NSA_WIDTH)


def token_mixer(h, layer_idx, w_in, ml_conv, ml_gate_bias, ml_norm, da_lambda, da_norm,
                nsa_pe, nsa_w1, nsa_w2, w_out):
    B, S, _ = h.shape
    z = h @ w_in
    splits = [int(c) for c in np.cumsum(IN_SIZES)[:-1]]
    (ml_q, ml_k, ml_v, ml_o, ml_i, ml_f, da_q, da_k, da_v,
     ns_q, ns_kc, ns_vc, ns_ks, ns_vs, ns_kw, ns_vw, ns_g) = jnp.split(z, splits, axis=-1)

    qk = jax.nn.silu(causal_dwconv(jnp.concatenate([ml_q, ml_k], axis=-1), ml_conv))
    ml_q, ml_k = jnp.split(qk, 2, axis=-1)
    i_pre = ml_i + ml_gate_bias[:ML_HEADS]
    f_pre = ml_f + ml_gate_bias[ML_HEADS:]
    hd = (B, S, ML_HEADS, ML_DIM)
    y_ml = mlstm(ml_q.reshape(hd), ml_k.reshape(hd), ml_v.reshape(hd), i_pre, f_pre)
    y_ml = head_rmsnorm(jax.nn.sigmoid(ml_o.astype(F32)) * y_ml, ml_norm, ML_HEADS)

    lam_init = 0.8 - 0.6 * math.exp(-0.3 * layer_idx)
    lf = da_lambda.astype(F32)
    lam = jnp.exp(jnp.sum(lf[0] * lf[1])) - jnp.exp(jnp.sum(lf[2] * lf[3])) + lam_init
    y_da = diff_attention(da_q, da_k, da_v, lam, lam_init, da_norm)

    y_ns = nsa_attention(ns_q, ns_kc, ns_vc, ns_ks, ns_vs, ns_kw, ns_vw, ns_g, nsa_pe, nsa_w1, nsa_w2)

    mixed = jnp.concatenate([y_ml.astype(h.dtype), y_da.astype(h.dtype), y_ns.astype(h.dtype)], axis=-1)
    return mixed @ w_out


def squared_relu_mlp(h, w1, w2):
    return jnp.square(jax.nn.relu(h @ w1)) @ w2


def setup_inputs(seed: int = 0) -> dict:
    key = jax.random.key(seed)
    ks = jax.random.split(key, 18)

    def nrm(k, shape, scale):
        return jax.random.normal(k, shape, F32) * scale

    x = nrm(ks[0], (BATCH, SEQ, D_MODEL), 1.0)
    norm1 = 1.0 + nrm(ks[1], (DEPTH, D_MODEL), 0.02)
    w_in = nrm(ks[2], (DEPTH, D_MODEL, IN_COLS), D_MODEL ** -0.5)
    ml_conv = nrm(ks[3], (DEPTH, CONV_W, 2 * ML_WIDTH), CONV_W ** -0.5)
    ig_bias = nrm(ks[4], (DEPTH, ML_HEADS), 0.1)
    fg_bias = jnp.linspace(3.0, 6.0, ML_HEADS, dtype=F32)[None, :] + nrm(ks[5], (DEPTH, ML_HEADS), 0.01)
    ml_gate_bias = jnp.concatenate([ig_bias, fg_bias], axis=-1)
    ml_norm = 1.0 + nrm(ks[6], (DEPTH, ML_WIDTH), 0.02)
    da_lambda = nrm(ks[7], (DEPTH, 4, DA_QK_DIM), 0.1)
    da_norm = 1.0 + nrm(ks[8], (DEPTH, DA_WIDTH), 0.02)
    nsa_pe = nrm(ks[9], (DEPTH, 2, CMP_BLOCK, NSA_DIM), 0.1)
    nsa_w1 = nrm(ks[10], (DEPTH, 2, CMP_BLOCK * NSA_DIM, CMP_HIDDEN), (CMP_BLOCK * NSA_DIM) ** -0.5)
    nsa_w2 = nrm(ks[11], (DEPTH, 2, CMP_HIDDEN, NSA_DIM), CMP_HIDDEN ** -0.5)
    w_out = nrm(ks[12], (DEPTH, D_MIX, D_MODEL), D_MIX ** -0.5)
    norm2 = 1.0 + nrm(ks[13], (DEPTH, D_MODEL), 0.02)
    w_ff1 = nrm(ks[14], (DEPTH, D_MODEL, D_FF), D_MODEL ** -0.5)
    w_ff2 = nrm(ks[15], (DEPTH, D_FF, D_MODEL), D_FF ** -0.5)
    final_norm = 1.0 + nrm(ks[16], (D_MODEL,), 0.02)
    return {'x': x, 'norm1': norm1, 'w_in': w_in, 'ml_conv': ml_conv, 'ml_gate_bias': ml_gate_bias,
            'ml_norm': ml_norm, 'da_lambda': da_lambda, 'da_norm': da_norm, 'nsa_pe': nsa_pe,
            'nsa_w1': nsa_w1, 'nsa_w2': nsa_w2, 'w_out': w_out, 'norm2': norm2,
            'w_ff1': w_ff1, 'w_ff2': w_ff2, 'final_norm': final_norm}


def reference(x, norm1, w_in, ml_conv, ml_gate_bias, ml_norm, da_lambda, da_norm, nsa_pe,
              nsa_w1, nsa_w2, w_out, norm2, w_ff1, w_ff2, final_norm):
    for l in range(DEPTH):
        h = rmsnorm(x, norm1[l])
        x = x + token_mixer(h, l, w_in[l], ml_conv[l], ml_gate_bias[l], ml_norm[l], da_lambda[l],
                            da_norm[l], nsa_pe[l], nsa_w1[l], nsa_w2[l], w_out[l])
        x = x + squared_relu_mlp(rmsnorm(x, norm2[l]), w_ff1[l], w_ff2[l])
    return rmsnorm(x, final_norm)
```

```python
import contextlib
import math
import numpy as np
import concourse.bass as bass
import concourse.mybir as mybir
from concourse.bass_utils import run_bass_kernel_spmd

F32 = mybir.dt.float32
BF16 = mybir.dt.bfloat16
AF = mybir.ActivationFunctionType
ALU = mybir.AluOpType
AX = mybir.AxisListType

D_MODEL = 1024
DEPTH = 2
ML_HEADS, ML_DIM = 4, 64
DA_HEADS, DA_QK, DA_V = 4, 32, 64
NSA_GROUPS, NSA_REP, NSA_DIM = 2, 4, 64
CMP_BLOCK, CMP_STRIDE, CMP_HIDDEN = 32, 16, 256
SEL_BLOCK, SEL_TOPK, WINDOW = 64, 16, 512
D_FF = 4096
IN_COLS = 3104
EPS = 1e-6
NEG = -30000.0

EPOCH_ENG = 30000
EPOCH_DMA = 2000


class Buf:
    def __init__(self, prog, ap, name):
        self.p = prog
        self.ap = ap
        self.name = name
        self.lw = None
        self.rd = {}
        self.dma = None

    def __getitem__(self, k):
        return self.ap[k]


class Counter:
    def __init__(self, prog, name, step):
        self.p = prog
        self.name = name
        self.step = step
        self.epoch = EPOCH_ENG if step == 1 else EPOCH_DMA
        self.n = 0
        self.sems = []

    def _sem(self, i):
        while len(self.sems) <= i:
            self.sems.append(self.p.new_sem(f"{self.name}_{len(self.sems)}"))
        return self.sems[i]

    def next(self):
        self.n += 1
        return self._sem((self.n - 1) // self.epoch), self.step

    def wait_specs(self, v):
        out = []
        last = (v - 1) // self.epoch
        for ep in range(last + 1):
            hv = (min(v, (ep + 1) * self.epoch) - ep * self.epoch) * self.step
            out.append((self._sem(ep), hv, ep))
        return out


class Prog:
    ENG = ("pe", "act", "dve", "pool", "sp")

    def __init__(self, nc, stack, same_engine_sync=True):
        self.nc = nc
        self.stack = stack
        self.ops = {e: [] for e in self.ENG}
        self.cnt = {e: Counter(self, "c" + e, 1) for e in self.ENG}
        self.waited = {e: {} for e in self.ENG}
        self.same = same_engine_sync
        self.nsem = 0
        self.out_waits = None
        self.ninst = 0

    def new_sem(self, name):
        self.nsem += 1
        return self.stack.enter_context(self.nc.semaphore(name))

    def sb(self, name, shape, dt):
        t = self.stack.enter_context(self.nc.sbuf_tensor(name, list(shape), dt))
        return Buf(self, t, name)

    def ps(self, name, shape, dt=F32):
        t = self.stack.enter_context(self.nc.psum_tensor(name, list(shape), dt))
        return Buf(self, t, name)

    def dram(self, name, shape, dt, kind):
        t = self.nc.dram_tensor(name, list(shape), dt, kind=kind).ap()
        return Buf(self, t, name)

    def _deps(self, eng, reads, writes):
        deps = []
        for b in reads:
            if b.lw is not None:
                deps.append(b.lw)
        for b in writes:
            if b.lw is not None:
                deps.append(b.lw)
            deps.extend(b.rd.items())
        out = []
        w = self.waited[eng]
        for c, v in deps:
            if c is self.cnt[eng] and (eng == "pe" or not self.same):
                continue
            if c.step == 16:
                v = c.n
            for sem, hv, ep in c.wait_specs(v):
                key = (id(c), ep)
                if w.get(key, 0) >= hv:
                    continue
                w[key] = hv
                out.append((sem, hv))
        return out

    def op(self, eng, fn, reads=(), writes=()):
        waits = self._deps(eng, reads, writes)
        c = self.cnt[eng]
        sem, amt = c.next()
        v = c.n
        for b in reads:
            b.rd[c] = v
        for b in writes:
            b.lw = (c, v)
            b.rd = {}
        self.ops[eng].append((waits, fn, sem, amt))
        self.ninst += 1 + len(waits)

    def dma(self, q, out_ap, in_ap, reads=(), writes=(), **kw):
        waits = self._deps(q, reads, writes)
        owner = writes[0]
        if owner.dma is None:
            owner.dma = Counter(self, "d" + owner.name, 16)
        c = owner.dma
        sem, amt = c.next()
        v = c.n
        for b in reads:
            b.rd[c] = v
        for b in writes:
            b.lw = (c, v)
            b.rd = {}
        self.ops[q].append((waits, lambda e: e.dma_start(out=out_ap, in_=in_ap, **kw), sem, amt))
        self.ninst += 1 + len(waits)

    def finish_outputs(self, bufs, eng="sp"):
        waits = []
        for b in bufs:
            c = b.dma
            for sem, hv, ep in c.wait_specs(c.n):
                waits.append((sem, hv))
        self.out_waits = (eng, waits)

    def emit(self):
        nc = self.nc
        amap = {"pe": "tensor", "act": "scalar", "dve": "vector", "pool": "gpsimd", "sp": "sync"}
        with nc.Block() as block:
            for ename in self.ENG:
                ops = self.ops[ename]
                final = self.out_waits[1] if (self.out_waits and self.out_waits[0] == ename) else []

                def body(e, ops=ops, final=final):
                    for waits, fn, sem, amt in ops:
                        for s, v in waits:
                            e.wait_ge(s, v)
                        fn(e).then_inc(sem, amt)
                    for s, v in final:
                        e.wait_ge(s, v)

                if not ops and not final:
                    continue
                getattr(block, amap[ename])(body)


def rsqrt(P, ob, o_ap, ib, i_ap, scale, bias_buf):
    bp = o_ap.base_partition()
    np_ = o_ap.shape[0]
    P.op("act", lambda e: e.activation(out=o_ap, in_=i_ap, func=AF.Sqrt, bias=bias_buf[bp:bp + np_, 0:1], scale=scale),
         reads=[ib, bias_buf], writes=[ob])
    P.op("dve", lambda e: e.reciprocal(out=o_ap, in_=o_ap), reads=[ob], writes=[ob])


def const_col(P, name, val, parts=128):
    b = P.sb(name, [parts, 1], F32)
    P.op("pool", lambda e: e.memset(b[:, :], val), writes=[b])
    return b


def _mm(out, lhsT, rhs, start=True, stop=True):
    return lambda e: e.matmul(out, lhsT=lhsT, rhs=rhs, start=start, stop=stop)


def build_l1(T):
    nc = bass.Bass("TRN2", target_bir_lowering=False)
    with contextlib.ExitStack() as st:
        P = Prog(nc, st)
        x = P.dram("x", [T, D_MODEL], F32, "ExternalInput")
        xT = P.dram("xT", [D_MODEL, T], F32, "ExternalInput")
        g = P.dram("g", [D_MODEL, 1], F32, "ExternalInput")
        w = P.dram("w", [D_MODEL, IN_COLS], F32, "ExternalInput")
        z = P.dram("z", [T, IN_COLS], F32, "ExternalOutput")
        KC = D_MODEL // 128
        wsb = P.sb("wsb", [128, KC, IN_COLS], BF16)
        gsb = P.sb("gsb", [128, KC], F32)
        for k in range(KC):
            P.dma("pool", wsb[:, k, :], w[k * 128:(k + 1) * 128, :], writes=[wsb])
        P.dma("sp", gsb[:, :], g.ap.rearrange("(k p) o -> p (k o)", p=128), writes=[gsb],
              allow_slow_non_contiguous=True)
        epsb = const_col(P, "epsb", EPS)
        TT = 512
        nbuf = 2
        xts = [P.sb(f"xt{i}", [128, TT // 128, D_MODEL], F32) for i in range(nbuf)]
        xTs = [P.sb(f"xTs{i}", [128, KC, TT], F32) for i in range(nbuf)]
        hTs = [P.sb(f"hT{i}", [128, KC, TT], BF16) for i in range(nbuf)]
        junk = P.sb("junk", [128, D_MODEL], F32)
        sss = [P.sb(f"ss{i}", [128, TT // 128], F32) for i in range(nbuf)]
        rstds = [P.sb(f"rstd{i}", [128, TT // 128], F32) for i in range(nbuf)]
        zsb = [P.sb(f"zsb{i}", [128, IN_COLS], F32) for i in range(2)]
        pss = [P.ps(f"ps{i}", [128, 512]) for i in range(4)]
        chunks = [(c, min(512, IN_COLS - c)) for c in range(0, IN_COLS, 512)]
        pi = 0
        zi = 0
        def load(it):
            b = it % nbuf
            t0 = it * TT
            P.dma("sp", xts[b][:, :, :], x.ap[t0:t0 + TT, :].rearrange("(s p) d -> p s d", p=128), writes=[xts[b]])
            P.dma("sp", xTs[b][:, :, :], xT.ap[:, t0:t0 + TT].rearrange("(k p) t -> p k t", p=128), writes=[xTs[b]])

        load(0)
        for it in range(T // TT):
            b = it % nbuf
            t0 = it * TT
            xt, xTt, hT, ss, rstd = xts[b], xTs[b], hTs[b], sss[b], rstds[b]
            if it + 1 < T // TT:
                load(it + 1)
            for s in range(TT // 128):
                P.op("act", lambda e, s=s, xt=xt, ss=ss: e.activation(
                    out=junk[:, :], in_=xt[:, s, :], func=AF.Square, accum_out=ss[:, s:s + 1]),
                    reads=[xt], writes=[junk, ss])
            rsqrt(P, rstd, rstd[:, :], ss, ss[:, :], 1.0 / D_MODEL, epsb)
            for k in range(KC):
                P.op("pool", lambda e, k=k, hT=hT, xTt=xTt: e.tensor_scalar(
                    out=hT[:, k, :], in0=xTt[:, k, :], scalar1=gsb[:, k:k + 1], scalar2=None, op0=ALU.mult),
                    reads=[xTt, gsb], writes=[hT])
            for s in range(TT // 128):
                zt = zsb[zi % 2]
                zi += 1
                for (c0, cw) in chunks:
                    ps = pss[pi % 4]
                    pi += 1
                    for k in range(KC):
                        P.op("pe", _mm(ps[:, :cw], hT[:, k, s * 128:(s + 1) * 128], wsb[:, k, c0:c0 + cw],
                                       start=(k == 0), stop=(k == KC - 1)),
                             reads=[hT, wsb], writes=[ps])
                    P.op("act", lambda e, ps=ps, zt=zt, c0=c0, cw=cw, s=s, rstd=rstd: e.activation(
                        out=zt[:, c0:c0 + cw], in_=ps[:, :cw], func=AF.Copy, scale=rstd[:, s:s + 1]),
                        reads=[ps, rstd], writes=[zt])
                P.dma("sp", z.ap[t0 + s * 128:t0 + (s + 1) * 128, :], zt[:, :], reads=[zt], writes=[z])
        P.finish_outputs([z])
        P.emit()
    return nc


def make_ident(P, name="ident"):
    it = P.sb(name + "_i", [128, 128], mybir.dt.int32)
    idf = P.sb(name, [128, 128], F32)
    P.op("pool", lambda e: e.iota(it[:, :], pattern=[[1, 128]], base=0, channel_multiplier=-1), writes=[it])
    P.op("dve", lambda e: e.tensor_copy(out=idf[:, :], in_=it[:, :]), reads=[it], writes=[idf])
    P.op("dve", lambda e: e.tensor_single_scalar(out=idf[:, :], in_=idf[:, :], scalar=0.0, op=ALU.is_equal),
         reads=[idf], writes=[idf])
    return idf


def build_l3(T, final):
    nc = bass.Bass("TRN2", target_bir_lowering=False)
    with contextlib.ExitStack() as st:
        P = Prog(nc, st)
        mixT = P.dram("mixT", [D_MODEL, T], F32, "ExternalInput")
        x = P.dram("x", [T, D_MODEL], F32, "ExternalInput")
        wo = P.dram("wo", [D_MODEL, D_MODEL], F32, "ExternalInput")
        g2 = P.dram("g2", [D_MODEL, 1], F32, "ExternalInput")
        w1 = P.dram("w1", [D_MODEL, D_FF], F32, "ExternalInput")
        w2 = P.dram("w2", [D_FF, D_MODEL], F32, "ExternalInput")
        gf = P.dram("gf", [1, D_MODEL], F32, "ExternalInput")
        out = P.dram("out", [T, D_MODEL], F32, "ExternalOutput")
        KC = D_MODEL // 128
        FC = D_FF // 128
        TT = 256
        NS = TT // 128
        wos = [P.sb(f"wo{k}", [128, D_MODEL], BF16) for k in range(KC)]
        w1s = [P.sb(f"w1_{k}", [128, D_FF], BF16) for k in range(KC)]
        w2s = [P.sb(f"w2_{k}", [128, 4, D_MODEL], BF16) for k in range(FC // 4)]
        g2sb = P.sb("g2sb", [128, KC], F32)
        epsb = const_col(P, "epsb", EPS)
        ident = make_ident(P)
        mixs = [P.sb(f"mix{i}", [128, KC, TT], BF16) for i in range(2)]
        xts = [P.sb(f"xt{i}", [128, NS, D_MODEL], F32) for i in range(2)]
        x1T = P.sb("x1T", [128, KC, TT], BF16)
        uT = P.sb("uT", [128, FC, TT], BF16)
        rl = [P.sb(f"rl{i}", [128, 512], F32) for i in range(2)]
        junk = P.sb("junk", [128, D_MODEL], F32)
        ss = P.sb("ss", [128, NS], F32)
        rstd = P.sb("rstd", [128, NS], F32)
        r2 = P.sb("r2", [128, NS], F32)
        pss = [P.ps(f"ps{i}", [128, 512]) for i in range(6)]
        pi = [0]

        def nps():
            pi[0] += 1
            return pss[pi[0] % len(pss)]

        def load(it):
            b = it % 2
            t0 = it * TT
            P.dma("pool", mixs[b][:, :, :], mixT.ap[:, t0:t0 + TT].rearrange("(k p) t -> p k t", p=128),
                  writes=[mixs[b]])
            P.dma("sp", xts[b][:, :, :], x.ap[t0:t0 + TT, :].rearrange("(s p) d -> p s d", p=128), writes=[xts[b]])

        for k in range(KC):
            P.dma("pool", wos[k][:, :], wo.ap[k * 128:(k + 1) * 128, :], writes=[wos[k]])
        load(0)
        P.dma("sp", g2sb[:, :], g2.ap.rearrange("(k p) o -> p (k o)", p=128), writes=[g2sb],
              allow_slow_non_contiguous=True)
        if final:
            gfb = P.sb("gfb", [128, D_MODEL], F32)
            P.dma("sp", gfb[:, :], gf.ap.partition_broadcast(128), writes=[gfb])
        for k in range(KC):
            P.dma("pool", w1s[k][:, :], w1.ap[k * 128:(k + 1) * 128, :], writes=[w1s[k]])
        for k in range(FC // 4):
            P.dma("pool", w2s[k][:, :, :], w2.ap[k * 512:(k + 1) * 512, :].rearrange("(c p) d -> p c d", p=128),
                  writes=[w2s[k]])

        for it in range(T // TT):
            b = it % 2
            t0 = it * TT
            mix, xt = mixs[b], xts[b]
            if it + 1 < T // TT:
                load(it + 1)
            for s in range(NS):
                for hf in range(2):
                    ps = nps()
                    for k in range(KC):
                        P.op("pe", _mm(ps[:, :], mix[:, k, s * 128:(s + 1) * 128], wos[k][:, hf * 512:(hf + 1) * 512],
                                       start=(k == 0), stop=(k == KC - 1)), reads=[mix, wos[k]], writes=[ps])
                    P.op("dve", lambda e, ps=ps, xt=xt, s=s, hf=hf: e.tensor_tensor(
                        out=xt[:, s, hf * 512:(hf + 1) * 512], in0=ps[:, :], in1=xt[:, s, hf * 512:(hf + 1) * 512],
                        op=ALU.add), reads=[ps, xt], writes=[xt])
            for s in range(NS):
                P.op("act", lambda e, s=s, xt=xt: e.activation(
                    out=junk[:, :], in_=xt[:, s, :], func=AF.Square, accum_out=ss[:, s:s + 1]),
                    reads=[xt], writes=[junk, ss])
            rsqrt(P, rstd, rstd[:, :], ss, ss[:, :], 1.0 / D_MODEL, epsb)
            P.op("dve", lambda e: e.tensor_tensor(out=r2[:, :], in0=rstd[:, :], in1=rstd[:, :], op=ALU.mult),
                 reads=[rstd], writes=[r2])
            for k in range(KC):
                ps = nps()
                for s in range(NS):
                    P.op("pe", lambda e, ps=ps, xt=xt, s=s, k=k: e.transpose(
                        out=ps[:, s * 128:(s + 1) * 128], in_=xt[:, s, k * 128:(k + 1) * 128], identity=ident[:, :]),
                        reads=[xt, ident], writes=[ps])
                P.op("act", lambda e, ps=ps, k=k: e.activation(
                    out=x1T[:, k, :], in_=ps[:, :TT], func=AF.Copy, scale=g2sb[:, k:k + 1]),
                    reads=[ps, g2sb], writes=[x1T])
            for f2 in range(FC // 2):
                ps = nps()
                for j in range(2):
                    f = f2 * 2 + j
                    for k in range(KC):
                        P.op("pe", _mm(ps[:, j * TT:(j + 1) * TT], w1s[k][:, f * 128:(f + 1) * 128], x1T[:, k, :],
                                       start=(k == 0), stop=(k == KC - 1)), reads=[w1s[k], x1T], writes=[ps])
                r = rl[f2 % 2]
                P.op("act", lambda e, ps=ps, r=r: e.activation(out=r[:, :], in_=ps[:, :], func=AF.Relu),
                     reads=[ps], writes=[r])
                P.op("pool", lambda e, r=r, f2=f2: e.tensor_tensor(
                    out=uT[:, 2 * f2:2 * f2 + 2, :], in0=r[:, :].rearrange("p (j t) -> p j t", j=2),
                    in1=r[:, :].rearrange("p (j t) -> p j t", j=2), op=ALU.mult), reads=[r], writes=[uT])
            for s in range(NS):
                for hf in range(2):
                    ps = nps()
                    for f in range(FC):
                        P.op("pe", _mm(ps[:, :], uT[:, f, s * 128:(s + 1) * 128],
                                       w2s[f // 4][:, f % 4, hf * 512:(hf + 1) * 512],
                                       start=(f == 0), stop=(f == FC - 1)), reads=[uT, w2s[f // 4]], writes=[ps])
                    P.op("dve", lambda e, ps=ps, xt=xt, s=s, hf=hf: e.scalar_tensor_tensor(
                        out=xt[:, s, hf * 512:(hf + 1) * 512], in0=ps[:, :], scalar=r2[:, s:s + 1],
                        in1=xt[:, s, hf * 512:(hf + 1) * 512], op0=ALU.mult, op1=ALU.add),
                        reads=[ps, r2, xt], writes=[xt])
            if final:
                for s in range(NS):
                    P.op("act", lambda e, s=s, xt=xt: e.activation(
                        out=junk[:, :], in_=xt[:, s, :], func=AF.Square, accum_out=ss[:, s:s + 1]),
                        reads=[xt], writes=[junk, ss])
                rsqrt(P, rstd, rstd[:, :], ss, ss[:, :], 1.0 / D_MODEL, epsb)
                for s in range(NS):
                    P.op("dve", lambda e, s=s, xt=xt: e.scalar_tensor_tensor(
                        out=xt[:, s, :], in0=xt[:, s, :], scalar=rstd[:, s:s + 1], in1=gfb[:, :],
                        op0=ALU.mult, op1=ALU.mult), reads=[xt, rstd, gfb], writes=[xt])
            P.dma("sp", out.ap[t0:t0 + TT, :].rearrange("(s p) d -> p s d", p=128), xt[:, :, :],
                  reads=[xt], writes=[out])
        P.finish_outputs([out])
        P.emit()
    return nc


def iota_mask(P, name, free, base, cm, step, dt_out=BF16, parts=128):
    it = P.sb(name + "_i", [parts, free], mybir.dt.int32)
    f = P.sb(name + "_f", [parts, free], F32)
    m = P.sb(name, [parts, free], dt_out)
    P.op("pool", lambda e: e.iota(it[:, :], pattern=[[step, free]], base=base, channel_multiplier=cm), writes=[it])
    P.op("dve", lambda e: e.tensor_copy(out=f[:, :], in_=it[:, :]), reads=[it], writes=[f])
    P.op("dve", lambda e: e.tensor_scalar(out=m[:, :], in0=f[:, :], scalar1=0.0, scalar2=NEG,
                                          op0=ALU.is_lt, op1=ALU.mult), reads=[f], writes=[m])
    return m


def build_da(S, layer_idx):
    lam_init = 0.8 - 0.6 * math.exp(-0.3 * layer_idx)
    scale = DA_QK ** -0.5
    nc = bass.Bass("TRN2", target_bir_lowering=False)
    with contextlib.ExitStack() as st:
        P = Prog(nc, st)
        qTd = P.dram("qT", [2, 32, S], F32, "ExternalInput")
        kTd = P.dram("kT", [2, 32, S], F32, "ExternalInput")
        vd = P.dram("v", [S, 64], F32, "ExternalInput")
        lamd = P.dram("lam4", [1, 128], F32, "ExternalInput")
        gaind = P.dram("gain", [64, 1], F32, "ExternalInput")
        yT = P.dram("yT", [64, S], F32, "ExternalOutput")
        NKT = S // 128
        NQT = S // 512
        q16 = [P.sb(f"q16_{m}", [33, S], BF16) for m in range(2)]
        k16 = [P.sb(f"k16_{m}", [33, S], BF16) for m in range(2)]
        va = P.sb("va", [128, NKT, 65], BF16)
        epsb = const_col(P, "epsb", EPS)
        ident = make_ident(P)
        id16 = P.sb("id16", [128, 128], BF16)
        P.op("dve", lambda e: e.tensor_copy(out=id16[:, :], in_=ident[:, :]), reads=[ident], writes=[id16])
        masks = [iota_mask(P, f"mk{j}", 512, -128 * j, -1, 1) for j in range(4)]
        ones33 = P.sb("ones33", [32, 33], BF16)
        P.op("pool", lambda e: e.memset(ones33[:, :], 1.0), writes=[ones33])
        ones65 = P.sb("ones65", [65, 65], F32)
        P.op("pool", lambda e: e.memset(ones65[:, :], 1.0), writes=[ones65])
        gcol = P.sb("gcol", [64, 1], F32)
        P.dma("sp", gcol[:, :], gaind.ap[:, :], writes=[gcol])
        P.op("dve", lambda e: e.tensor_scalar(out=gcol[:, :], in0=gcol[:, :], scalar1=1.0 - lam_init, scalar2=None,
                                              op0=ALU.mult), reads=[gcol], writes=[gcol])
        lt = P.sb("lt", [65, 128], F32)
        lp = P.sb("lp", [65, 64], F32)
        l2 = P.sb("l2", [65, 2], F32)
        lam = P.sb("lam", [65, 1], F32)
        P.dma("sp", lt[64:65, :], lamd.ap[:, :], writes=[lt])
        P.op("dve", lambda e: e.tensor_tensor(
            out=lp[64:65, :].rearrange("p (a d) -> p a d", a=2),
            in0=lt[64:65, :].rearrange("p (a b d) -> p a b d", a=2, b=2)[:, :, 0, :],
            in1=lt[64:65, :].rearrange("p (a b d) -> p a b d", a=2, b=2)[:, :, 1, :], op=ALU.mult),
            reads=[lt], writes=[lp])
        P.op("dve", lambda e: e.reduce_sum(out=l2[64:65, :], in_=lp[64:65, :].rearrange("p (a d) -> p a d", a=2),
                                           axis=AX.X), reads=[lp], writes=[l2])
        P.op("act", lambda e: e.activation(out=l2[64:65, :], in_=l2[64:65, :], func=AF.Exp), reads=[l2], writes=[l2])
        P.op("dve", lambda e: e.tensor_tensor(out=lam[64:65, :], in0=l2[64:65, 0:1], in1=l2[64:65, 1:2],
                                              op=ALU.subtract), reads=[l2], writes=[lam])
        P.op("dve", lambda e: e.tensor_scalar(out=lam[64:65, :], in0=lam[64:65, :], scalar1=lam_init, scalar2=None,
                                              op0=ALU.add), reads=[lam], writes=[lam])
        for m in range(2):
            P.dma("pool", q16[m][0:32, :], qTd.ap[m, :, :], writes=[q16[m]])
            P.dma("pool", k16[m][0:32, :], kTd.ap[m, :, :], writes=[k16[m]])
            P.op("dve", lambda e, m=m: e.memset(k16[m][32:33, :], 1.0), writes=[k16[m]])
        vr = vd.ap.rearrange("(t p) d -> p t d", p=128)
        for c0 in range(0, NKT, 32):
            c1 = min(NKT, c0 + 32)
            P.dma("pool", va[:, c0:c1, 0:64], vr[:, c0:c1, :], writes=[va])
        P.op("dve", lambda e: e.memset(va[:, :, 64:65], 1.0), writes=[va])
        sq = [P.sb(f"sq{i}", [32, 512], BF16) for i in range(2)]
        pss = [P.ps(f"ps{i}", [128, 1024]) for i in range(2)]
        po = [P.ps(f"po{i}", [65, 512]) for i in range(2)]
        pe = P.ps("pe", [65, 1024])
        kmx = P.sb("kmx", [33, S // 512], F32)
        kmax = P.sb("kmax", [33, 2], F32)
        qn = P.sb("qn", [33, 512], F32)
        for m in range(2):
            for c in range(S // 512):
                t = sq[c % 2]
                ps = pss[c % 2]
                P.op("pool", lambda e, t=t, m=m, c=c: e.tensor_tensor(
                    out=t[:, :], in0=k16[m][0:32, c * 512:(c + 1) * 512], in1=k16[m][0:32, c * 512:(c + 1) * 512],
                    op=ALU.mult), reads=[k16[m]], writes=[t])
                P.op("pe", _mm(ps[0:33, 0:512], ones33[:, :], t[:, :]), reads=[ones33, t], writes=[ps])
                P.op("dve", lambda e, ps=ps, c=c: e.reduce_max(out=kmx[32:33, c:c + 1], in_=ps[32:33, 0:512], axis=AX.X),
                     reads=[ps], writes=[kmx])
            P.op("dve", lambda e, m=m: e.reduce_max(out=kmax[32:33, m:m + 1], in_=kmx[32:33, :], axis=AX.X),
                 reads=[kmx], writes=[kmax])
            P.op("act", lambda e, m=m: e.activation(out=kmax[32:33, m:m + 1], in_=kmax[32:33, m:m + 1], func=AF.Sqrt),
                 reads=[kmax], writes=[kmax])
            for c in range(S // 512):
                t = sq[c % 2]
                ps = pss[c % 2]
                P.op("pool", lambda e, t=t, m=m, c=c: e.tensor_tensor(
                    out=t[:, :], in0=q16[m][0:32, c * 512:(c + 1) * 512], in1=q16[m][0:32, c * 512:(c + 1) * 512],
                    op=ALU.mult), reads=[q16[m]], writes=[t])
                P.op("pe", _mm(ps[0:33, 0:512], ones33[:, :], t[:, :]), reads=[ones33, t], writes=[ps])
                P.op("act", lambda e, ps=ps: e.activation(out=qn[32:33, :], in_=ps[32:33, 0:512], func=AF.Sqrt),
                     reads=[ps], writes=[qn])
                P.op("dve", lambda e, m=m, c=c: e.tensor_scalar(
                    out=q16[m][32:33, c * 512:(c + 1) * 512], in0=qn[32:33, :], scalar1=kmax[32:33, m:m + 1],
                    scalar2=-1.0, op0=ALU.mult, op1=ALU.mult), reads=[qn, kmax], writes=[q16[m]])
        pbs = [P.sb(f"pb{i}", [128, 1024], BF16) for i in range(3)]
        osb = P.sb("osb", [65, 1024], F32)
        ab = P.sb("ab", [65, 1024], F32)
        o = P.sb("o", [64, 512], F32)
        o1 = P.sb("o1", [64, 512], F32)
        osq = P.sb("osq", [64, 512], F32)
        rr = P.sb("rr", [65, 512], F32)
        ysb = [P.sb(f"ysb{i}", [64, 512], F32) for i in range(2)]
        tiles = [(qi, kt) for qi in range(NQT) for kt in range(4 * qi + 4)]

        def emit_s(i):
            qi, kt = tiles[i]
            ps = pss[i % 2]
            j = kt - 4 * qi
            for m in range(2):
                P.op("pe", _mm(ps[:, m * 512:(m + 1) * 512], k16[m][0:33, kt * 128:(kt + 1) * 128],
                               q16[m][0:33, qi * 512:qi * 512 + 512], start=True, stop=(j < 0)),
                     reads=[k16[m], q16[m]], writes=[ps])
                if j >= 0:
                    P.op("pe", _mm(ps[:, m * 512:(m + 1) * 512], id16[:, :], masks[j][:, :], start=False, stop=True),
                         reads=[id16, masks[j]], writes=[ps])

        emit_s(0)
        for i, (qi, kt) in enumerate(tiles):
            q0 = qi * 512
            nkt = 4 * qi + 4
            ps = pss[i % 2]
            pb = pbs[i % 3]
            P.op("act", lambda e, ps=ps, pb=pb: e.activation(out=pb[:, :], in_=ps[:, :], func=AF.Exp, scale=scale),
                 reads=[ps], writes=[pb])
            if i + 1 < len(tiles):
                emit_s(i + 1)
            for m in range(2):
                P.op("pe", _mm(po[m][:, :], va[:, kt, :], pb[:, m * 512:(m + 1) * 512],
                               start=(kt == 0), stop=(kt == nkt - 1)), reads=[va, pb], writes=[po[m]])
            if kt != nkt - 1:
                continue
            for m in range(2):
                P.op("act", lambda e, m=m: e.activation(out=osb[:, m * 512:(m + 1) * 512], in_=po[m][:, :], func=AF.Copy),
                     reads=[po[m]], writes=[osb])
            P.op("dve", lambda e: e.reciprocal(out=ab[64:65, :], in_=osb[64:65, :]), reads=[osb], writes=[ab])
            P.op("dve", lambda e: e.tensor_scalar(out=ab[64:65, 512:1024], in0=ab[64:65, 512:1024],
                                                  scalar1=lam[64:65, 0:1], scalar2=None, op0=ALU.mult),
                 reads=[ab, lam], writes=[ab])
            for m in range(2):
                P.op("pe", _mm(pe[0:64, m * 512:(m + 1) * 512], ones65[64:65, 0:64], ab[64:65, m * 512:(m + 1) * 512]),
                     reads=[ones65, ab], writes=[pe])
            P.op("dve", lambda e: e.tensor_tensor(out=o[:, :], in0=osb[0:64, 0:512], in1=pe[0:64, 0:512], op=ALU.mult),
                 reads=[osb, pe], writes=[o])
            P.op("dve", lambda e: e.tensor_tensor(out=o1[:, :], in0=osb[0:64, 512:1024], in1=pe[0:64, 512:1024],
                                                  op=ALU.mult), reads=[osb, pe], writes=[o1])
            P.op("dve", lambda e: e.tensor_tensor(out=o[:, :], in0=o[:, :], in1=o1[:, :], op=ALU.subtract),
                 reads=[o, o1], writes=[o])
            P.op("act", lambda e: e.activation(out=osq[:, :], in_=o[:, :], func=AF.Square), reads=[o], writes=[osq])
            P.op("pe", _mm(pe[0:65, 0:512], ones65[0:64, 0:65], osq[:, :]), reads=[ones65, osq], writes=[pe])
            rsqrt(P, rr, rr[64:65, :], pe, pe[64:65, 0:512], 1.0 / 64, epsb)
            P.op("pe", _mm(pe[0:64, 512:1024], ones65[64:65, 0:64], rr[64:65, :]), reads=[ones65, rr], writes=[pe])
            y = ysb[qi % 2]
            P.op("dve", lambda e, y=y: e.scalar_tensor_tensor(out=y[:, :], in0=o[:, :], scalar=gcol[:, 0:1],
                                                               in1=pe[0:64, 512:1024], op0=ALU.mult, op1=ALU.mult),
                 reads=[o, gcol, pe], writes=[y])
            P.dma("sp", yT.ap[:, q0:q0 + 512], y[:, :], reads=[y], writes=[yT])
        P.finish_outputs([yT])
        P.emit()
    return nc


def iota_cmp(P, name, pattern, base, cm, op, mul, dt_out, parts=128):
    free = 1
    for _, n in pattern:
        free *= n
    it = P.sb(name + "_i", [parts, free], mybir.dt.int32)
    f = P.sb(name + "_f", [parts, free], F32)
    m = P.sb(name, [parts, free], dt_out)
    P.op("pool", lambda e: e.iota(it[:, :], pattern=[list(x) for x in pattern], base=base, channel_multiplier=cm),
         writes=[it])
    P.op("dve", lambda e: e.tensor_copy(out=f[:, :], in_=it[:, :]), reads=[it], writes=[f])
    P.op("dve", lambda e: e.tensor_scalar(out=m[:, :], in0=f[:, :], scalar1=0.0, scalar2=mul, op0=op, op1=ALU.mult),
         reads=[f], writes=[m])
    return m


def build_ml(S):
    CH = 128
    NCH = S // CH
    NG = S // 512
    nc = bass.Bass("TRN2", target_bir_lowering=False)
    with contextlib.ExitStack() as st:
        P = Prog(nc, st)
        qTp = P.dram("qTp", [64, S + 3], F32, "ExternalInput")
        kTp = P.dram("kTp", [64, S + 3], F32, "ExternalInput")
        cwq = P.dram("cwq", [64, 4], F32, "ExternalInput")
        cwk = P.dram("cwk", [64, 4], F32, "ExternalInput")
        vd = P.dram("v", [S, 64], F32, "ExternalInput")
        oTd = P.dram("oT", [64, S], F32, "ExternalInput")
        ifgd = P.dram("ifg", [128, NCH, 2], F32, "ExternalInput")
        gbd = P.dram("gb", [1, 2], F32, "ExternalInput")
        gaind = P.dram("gain", [64, 1], F32, "ExternalInput")
        yT = P.dram("yT", [64, S], F32, "ExternalOutput")

        epsb = const_col(P, "epsb", EPS)
        oneb = const_col(P, "oneb", 1.0)
        ident = make_ident(P)
        id16 = P.sb("id16", [128, 128], BF16)
        P.op("dve", lambda e: e.tensor_copy(out=id16[:, :], in_=ident[:, :]), reads=[ident], writes=[id16])
        tri = iota_cmp(P, "tri", [[1, 128]], 0, -1, ALU.is_ge, 1.0, F32)
        negm4 = iota_cmp(P, "negm4", [[0, 4], [1, 128]], 0, -1, ALU.is_lt, NEG, F32)
        ones = P.sb("ones", [128, 128], F32)
        P.op("pool", lambda e: e.memset(ones[:, :], 1.0), writes=[ones])
        wq = P.sb("wq", [64, 4], F32)
        wk = P.sb("wk", [64, 4], F32)
        gcol = P.sb("gcol", [64, 1], F32)
        gb = P.sb("gbs", [128, 2], F32)
        ifg = P.sb("ifgs", [128, NCH, 2], F32)
        P.dma("sp", wq[:, :], cwq.ap[:, :], writes=[wq])
        P.dma("sp", wk[:, :], cwk.ap[:, :], writes=[wk])
        P.dma("sp", gcol[:, :], gaind.ap[:, :], writes=[gcol])
        P.dma("sp", gb[:, :], gbd.ap.partition_broadcast(128), writes=[gb])
        P.dma("sp", ifg[:, :, :], ifgd.ap[:, :, :], writes=[ifg])
        va = P.sb("va", [128, NCH, 65], BF16)
        vr = vd.ap.rearrange("(t p) d -> p t d", p=128)
        for c0 in range(0, NCH, 32):
            c1 = min(NCH, c0 + 32)
            P.dma("pool", va[:, c0:c1, 0:64], vr[:, c0:c1, :], writes=[va])
        P.op("dve", lambda e: e.memset(va[:, :, 64:65], 1.0), writes=[va])

        pA = P.ps("pA", [128, 512])
        pB = P.ps("pB", [128, 512])
        pC = P.ps("pC", [128, 512])
        pD = P.ps("pD", [128, 512])
        pE = P.ps("pE", [128, 512])
        pT = P.ps("pT", [128, 512], BF16)

        def G(name):
            return P.sb(name, [128, NCH], F32)
        ipre, fpre, ax, l1, logf, bcol, gall, wb, ua, eg = (G(n) for n in
                                                             ("ipre", "fpre", "ax", "l1", "logf", "bcol", "gall", "wb", "ua", "eg"))
        P.op("dve", lambda e: e.tensor_scalar(out=ipre[:, :], in0=ifg[:, :, 0], scalar1=gb[:, 0:1], scalar2=None,
                                              op0=ALU.add), reads=[ifg, gb], writes=[ipre])
        P.op("dve", lambda e: e.tensor_scalar(out=fpre[:, :], in0=ifg[:, :, 1], scalar1=gb[:, 1:2], scalar2=None,
                                              op0=ALU.add), reads=[ifg, gb], writes=[fpre])
        P.op("dve", lambda e: e.scalar_tensor_tensor(out=ax[:, :], in0=fpre[:, :], scalar=-1.0, in1=fpre[:, :],
                                                     op0=ALU.mult, op1=ALU.max),
             reads=[fpre], writes=[ax])
        P.op("act", lambda e: e.activation(out=ax[:, :], in_=ax[:, :], func=AF.Exp, scale=-1.0), reads=[ax], writes=[ax])
        P.op("act", lambda e: e.activation(out=l1[:, :], in_=ax[:, :], func=AF.Ln, bias=oneb[:, 0:1]),
             reads=[ax, oneb], writes=[l1])
        P.op("dve", lambda e: e.tensor_single_scalar(out=logf[:, :], in_=fpre[:, :], scalar=0.0, op=ALU.min),
             reads=[fpre], writes=[logf])
        P.op("dve", lambda e: e.tensor_tensor(out=logf[:, :], in0=logf[:, :], in1=l1[:, :], op=ALU.subtract),
             reads=[logf, l1], writes=[logf])
        rmax = P.sb("rmax", [128, 1], F32)
        mx = P.sb("mx", [1, 128], F32)
        mcol = P.sb("mcol", [128, 1], F32)
        negmc = P.sb("negmc", [128, 1], F32)
        emb = P.sb("emb", [128, 1], F32)
        P.op("dve", lambda e: e.reduce_max(out=rmax[:, :], in_=ipre[:, :], axis=AX.X), reads=[ipre], writes=[rmax])
        P.op("pe", lambda e: e.transpose(out=pA[0:1, 0:128], in_=rmax[:, 0:1], identity=ident[:, :]),
             reads=[rmax, ident], writes=[pA])
        P.op("dve", lambda e: e.reduce_max(out=mx[0:1, 0:1], in_=pA[0:1, 0:128], axis=AX.X), reads=[pA], writes=[mx])
        P.op("pe", _mm(pA[:, 0:1], ones[0:1, :], mx[0:1, 0:1]), reads=[ones, mx], writes=[pA])
        P.op("dve", lambda e: e.tensor_copy(out=mcol[:, :], in_=pA[:, 0:1]), reads=[pA], writes=[mcol])
        P.op("dve", lambda e: e.tensor_scalar(out=negmc[:, :], in0=mcol[:, :], scalar1=-1.0, scalar2=None, op0=ALU.mult),
             reads=[mcol], writes=[negmc])
        P.op("act", lambda e: e.activation(out=emb[:, :], in_=mcol[:, :], func=AF.Exp, scale=-1.0),
             reads=[mcol], writes=[emb])
        P.op("pe", _mm(pB[:, 0:NCH], tri[:, :], logf[:, :]), reads=[tri, logf], writes=[pB])
        P.op("pe", _mm(pC[:, 0:NCH], ones[:, :], logf[:, :]), reads=[ones, logf], writes=[pC])
        P.op("dve", lambda e: e.tensor_copy(out=bcol[:, :], in_=pB[:, 0:NCH]), reads=[pB], writes=[bcol])
        P.op("dve", lambda e: e.tensor_copy(out=gall[:, :], in_=pC[:, 0:NCH]), reads=[pC], writes=[gall])
        P.op("dve", lambda e: e.tensor_tensor(out=wb[:, :], in0=ipre[:, :], in1=bcol[:, :], op=ALU.subtract),
             reads=[ipre, bcol], writes=[wb])
        P.op("dve", lambda e: e.tensor_scalar(out=wb[:, :], in0=wb[:, :], scalar1=negmc[:, 0:1], scalar2=None,
                                              op0=ALU.add), reads=[wb, negmc], writes=[wb])
        P.op("dve", lambda e: e.tensor_tensor(out=ua[:, :], in0=gall[:, :], in1=wb[:, :], op=ALU.add),
             reads=[gall, wb], writes=[ua])
        P.op("act", lambda e: e.activation(out=ua[:, :], in_=ua[:, :], func=AF.Exp), reads=[ua], writes=[ua])
        P.op("act", lambda e: e.activation(out=eg[:, :], in_=gall[:, :], func=AF.Exp), reads=[gall], writes=[eg])

        def conv_silu(xs, w, acc, out_ap, out_buf, n, post_scale=None, tmp=None):
            P.op("dve", lambda e: e.tensor_scalar(out=acc[:, 0:n], in0=xs[:, 0:n], scalar1=w[:, 0:1], scalar2=None,
                                                  op0=ALU.mult), reads=[xs, w], writes=[acc])
            for j in range(1, 4):
                P.op("dve", lambda e, j=j: e.scalar_tensor_tensor(
                    out=acc[:, 0:n], in0=xs[:, j:j + n], scalar=w[:, j:j + 1], in1=acc[:, 0:n], op0=ALU.mult,
                    op1=ALU.add), reads=[xs, w, acc], writes=[acc])
            if post_scale is None:
                P.op("act", lambda e: e.activation(out=out_ap, in_=acc[:, 0:n], func=AF.Silu), reads=[acc],
                     writes=[out_buf])
            else:
                P.op("act", lambda e: e.activation(out=tmp[:, 0:n], in_=acc[:, 0:n], func=AF.Silu), reads=[acc],
                     writes=[tmp])
                P.op("pool", lambda e: e.tensor_scalar(out=out_ap, in0=tmp[:, 0:n], scalar1=post_scale, scalar2=None,
                                                       op0=ALU.mult), reads=[tmp], writes=[out_buf])

        k16 = P.sb("k16", [64, S], BF16)
        KB = min(1024, S)
        xst = [P.sb(f"xst{i}", [64, KB + 3], F32) for i in range(2)]
        acc = P.sb("acc", [64, KB], F32)
        tmpk = P.sb("tmpk", [64, KB], F32)
        for bi in range(S // KB):
            xs = xst[bi % 2]
            P.dma("sp", xs[:, :], kTp.ap[:, bi * KB:bi * KB + KB + 3], writes=[xs])
            conv_silu(xs, wk, acc, k16[:, bi * KB:(bi + 1) * KB], k16, KB, post_scale=ML_DIM ** -0.5, tmp=tmpk)
        ktok = P.sb("ktok", [128, NCH, 64], BF16)
        for c0 in range(0, NCH, 8):
            n = min(8, NCH - c0)
            for i in range(n):
                c = c0 + i
                P.op("pe", lambda e, c=c, i=i: e.transpose(out=pT[:, i * 64:(i + 1) * 64],
                                                           in_=k16[0:64, c * 128:(c + 1) * 128],
                                                           identity=id16[0:64, 0:64]),
                     reads=[k16, id16], writes=[pT])
            P.op("act", lambda e, c0=c0, n=n: e.activation(
                out=ktok[:, c0:c0 + n, :], in_=pT[:, 0:n * 64].rearrange("p (c d) -> p c d", d=64), func=AF.Copy),
                reads=[pT], writes=[ktok])
        vu = P.sb("vu", [128, NCH, 65], BF16)
        for c in range(NCH):
            P.op("pool" if c % 2 else "dve", lambda e, c=c: e.tensor_scalar(
                out=vu[:, c, :], in0=va[:, c, :], scalar1=ua[:, c:c + 1], scalar2=None, op0=ALU.mult),
                reads=[va, ua], writes=[vu])
        dl = P.sb("dl", [64, NCH, 65], F32)
        pds = [pD, pE]
        for gi, c0 in enumerate(range(0, NCH, 7)):
            n = min(7, NCH - c0)
            pd = pds[gi % 2]
            for i in range(n):
                c = c0 + i
                P.op("pe", _mm(pd[0:64, i * 65:(i + 1) * 65], ktok[:, c, :], vu[:, c, :]), reads=[ktok, vu], writes=[pd])
            P.op("act", lambda e, pd=pd, c0=c0, n=n: e.activation(
                out=dl[:, c0:c0 + n, :], in_=pd[0:64, 0:n * 65].rearrange("p (c d) -> p c d", d=65), func=AF.Copy),
                reads=[pd], writes=[dl])
        stt = P.sb("stt", [64, 65], F32)
        sts = P.sb("sts", [64, NCH, 65], BF16)
        P.op("dve", lambda e: e.memset(stt[:, :], 0.0), writes=[stt])
        for c in range(NCH):
            P.op("pool", lambda e, c=c: e.tensor_copy(out=sts[:, c, :], in_=stt[:, :]), reads=[stt], writes=[sts])
            if c + 1 < NCH:
                P.op("dve", lambda e, c=c: e.scalar_tensor_tensor(
                    out=stt[:, :], in0=stt[:, :], scalar=eg[0:64, c:c + 1], in1=dl[:, c, :], op0=ALU.mult, op1=ALU.add),
                    reads=[stt, eg, dl], writes=[stt])

        xq = [P.sb(f"xq{i}", [64, 515], F32) for i in range(2)]
        ot = [P.sb(f"ot{i}", [64, 512], F32) for i in range(2)]
        accq = P.sb("accq", [64, 512], F32)
        q16 = P.sb("q16", [64, 512], BF16)
        qs16 = P.sb("qs16", [64, 512], BF16)
        tl = P.sb("tl", [128, 512], F32)
        ebt = P.sb("ebt", [64, 512], F32)
        wt = P.sb("wt", [128, 512], F32)
        at16 = P.sb("at16", [128, 512], BF16)
        nsb = P.sb("nsb", [65, 512], F32)
        dd = P.sb("dd", [65, 512], F32)
        sg = P.sb("sg", [64, 512], F32)
        wv = P.sb("wv", [64, 512], F32)
        wsq = P.sb("wsq", [64, 512], F32)
        rr = P.sb("rr", [65, 512], F32)
        ysb = [P.sb(f"ysb{i}", [64, 512], F32) for i in range(2)]

        def loadg(gi):
            P.dma("sp", xq[gi % 2][:, :], qTp.ap[:, gi * 512:gi * 512 + 515], writes=[xq[gi % 2]])
            P.dma("sp", ot[gi % 2][:, :], oTd.ap[:, gi * 512:(gi + 1) * 512], writes=[ot[gi % 2]])

        loadg(0)
        for gi in range(NG):
            if gi + 1 < NG:
                loadg(gi + 1)
            t0 = gi * 512
            c0 = gi * 4
            conv_silu(xq[gi % 2], wq, accq, q16[:, :], q16, 512)
            for cl in range(4):
                P.op("pool", lambda e, cl=cl, c0=c0: e.tensor_scalar(
                    out=tl[:, cl * 128:(cl + 1) * 128], in0=tri[:, :], scalar1=logf[:, c0 + cl:c0 + cl + 1],
                    scalar2=None, op0=ALU.mult), reads=[tri, logf], writes=[tl])
            P.op("pe", _mm(pA[:, :], ones[:, :], tl[:, :], start=True, stop=False), reads=[ones, tl], writes=[pA])
            P.op("pe", _mm(pA[:, :], ident[:, :], negm4[:, :], start=False, stop=True), reads=[ident, negm4], writes=[pA])
            P.op("pe", _mm(pB[0:64, :], ones[:, 0:64], tl[:, :]), reads=[ones, tl], writes=[pB])
            P.op("act", lambda e: e.activation(out=ebt[:, :], in_=pB[0:64, :], func=AF.Exp), reads=[pB], writes=[ebt])
            P.op("dve", lambda e: e.tensor_tensor(out=qs16[:, :], in0=q16[:, :], in1=ebt[:, :], op=ALU.mult),
                 reads=[q16, ebt], writes=[qs16])
            for cl in range(4):
                P.op("act", lambda e, cl=cl, c0=c0: e.activation(
                    out=wt[:, cl * 128:(cl + 1) * 128], in_=pA[:, cl * 128:(cl + 1) * 128], func=AF.Exp,
                    bias=wb[:, c0 + cl:c0 + cl + 1]), reads=[pA, wb], writes=[wt])
            for cl in range(4):
                c = c0 + cl
                P.op("pe", _mm(pC[:, cl * 128:(cl + 1) * 128], k16[0:64, c * 128:(c + 1) * 128],
                               q16[0:64, cl * 128:(cl + 1) * 128]), reads=[k16, q16], writes=[pC])
            P.op("dve", lambda e: e.tensor_tensor(out=at16[:, :], in0=pC[:, :], in1=wt[:, :], op=ALU.mult),
                 reads=[pC, wt], writes=[at16])
            for cl in range(4):
                c = c0 + cl
                P.op("pe", _mm(pD[0:65, cl * 128:(cl + 1) * 128], va[:, c, :], at16[:, cl * 128:(cl + 1) * 128],
                               start=True, stop=False), reads=[va, at16], writes=[pD])
                P.op("pe", _mm(pD[0:65, cl * 128:(cl + 1) * 128], sts[:, c, :], qs16[0:64, cl * 128:(cl + 1) * 128],
                               start=False, stop=True), reads=[sts, qs16], writes=[pD])
            P.op("act", lambda e: e.activation(out=nsb[:, :], in_=pD[0:65, :], func=AF.Copy), reads=[pD], writes=[nsb])
            P.op("dve", lambda e: e.scalar_tensor_tensor(out=dd[64:65, :], in0=nsb[64:65, :], scalar=-1.0,
                                                         in1=nsb[64:65, :], op0=ALU.mult, op1=ALU.max),
                 reads=[nsb], writes=[dd])
            P.op("dve", lambda e: e.tensor_scalar(out=dd[64:65, :], in0=dd[64:65, :], scalar1=emb[64:65, 0:1],
                                                  scalar2=None, op0=ALU.max), reads=[dd, emb], writes=[dd])
            P.op("act", lambda e, gi=gi: e.activation(out=sg[:, :], in_=ot[gi % 2][:, :], func=AF.Sigmoid),
                 reads=[ot[gi % 2]], writes=[sg])
            P.op("dve", lambda e: e.tensor_tensor(out=wv[:, :], in0=sg[:, :], in1=nsb[0:64, :], op=ALU.mult),
                 reads=[sg, nsb], writes=[wv])
            P.op("act", lambda e: e.activation(out=wsq[:, :], in_=wv[:, :], func=AF.Square), reads=[wv], writes=[wsq])
            P.op("pe", _mm(pE[0:65, :], ones[0:64, 0:65], wsq[:, :]), reads=[ones, wsq], writes=[pE])
            P.op("dve", lambda e: e.scalar_tensor_tensor(out=rr[64:65, :], in0=dd[64:65, :], scalar=EPS, in1=dd[64:65, :],
                                                         op0=ALU.mult, op1=ALU.mult), reads=[dd], writes=[rr])
            P.op("dve", lambda e: e.scalar_tensor_tensor(out=rr[64:65, :], in0=pE[64:65, :], scalar=1.0 / 64,
                                                         in1=rr[64:65, :], op0=ALU.mult, op1=ALU.add),
                 reads=[pE, rr], writes=[rr])
            P.op("act", lambda e: e.activation(out=rr[64:65, :], in_=rr[64:65, :], func=AF.Sqrt), reads=[rr], writes=[rr])
            P.op("dve", lambda e: e.reciprocal(out=rr[64:65, :], in_=rr[64:65, :]), reads=[rr], writes=[rr])
            P.op("pe", _mm(pE[0:64, :], ones[64:65, 0:64], rr[64:65, :]), reads=[ones, rr], writes=[pE])
            y = ysb[gi % 2]
            P.op("dve", lambda e, y=y: e.scalar_tensor_tensor(out=y[:, :], in0=wv[:, :], scalar=gcol[:, 0:1],
                                                               in1=pE[0:64, :], op0=ALU.mult, op1=ALU.mult),
                 reads=[wv, gcol, pE], writes=[y])
            P.dma("sp", yT.ap[:, t0:t0 + 512], y[:, :], reads=[y], writes=[yT])
        P.finish_outputs([yT])
        P.emit()
    return nc


def build_nsa(S):
    NQ = S // 256
    NKT = S // 128
    NCB = (S - CMP_BLOCK) // CMP_STRIDE + 1
    NCP = ((NCB + 127) // 128) * 128
    NSB = S // SEL_BLOCK
    NJ = (NSB + 127) // 128
    SCW = NJ * 128
    scale = NSA_DIM ** -0.5
    nc = bass.Bass("TRN2", target_bir_lowering=False)
    with contextlib.ExitStack() as st:
        P = Prog(nc, st)
        qB = P.dram("qB", [NQ, 64, 512], F32, "ExternalInput")
        gB = P.dram("gB", [NQ, 1, 1536], F32, "ExternalInput")
        kcT = P.dram("kcT", [64, S], F32, "ExternalInput")
        vcT = P.dram("vcT", [64, S], F32, "ExternalInput")
        ksT = P.dram("ksT", [64, S], F32, "ExternalInput")
        kwT = P.dram("kwT", [64, S], F32, "ExternalInput")
        vsd = P.dram("vs", [S, 64], F32, "ExternalInput")
        vwd = P.dram("vw", [S, 64], F32, "ExternalInput")
        peT = P.dram("peT", [2, 64, 32], F32, "ExternalInput")
        w1d = P.dram("w1", [2, 2048, 256], F32, "ExternalInput")
        w2d = P.dram("w2", [2, 256, 64], F32, "ExternalInput")
        pard = P.dram("par", [1, 1], F32, "ExternalInput")
        yB = P.dram("yB", [NQ, 64, 512], F32, "ExternalOutput")

        ident = make_ident(P)
        id16 = P.sb("id16", [128, 128], BF16)
        P.op("dve", lambda e: e.tensor_copy(out=id16[:, :], in_=ident[:, :]), reads=[ident], writes=[id16])
        ones16 = P.sb("ones16", [64, 128], BF16)
        P.op("pool", lambda e: e.memset(ones16[:, :], 1.0), writes=[ones16])
        ones32 = P.sb("ones32", [65, 64], F32)
        P.op("pool", lambda e: e.memset(ones32[:, :], 1.0), writes=[ones32])
        pcol = P.sb("pcol", [128, 1], F32)
        p128 = P.sb("p128", [128, 1], F32)
        P.dma("sp", pcol[:, :], pard.ap.partition_broadcast(128), writes=[pcol])
        P.op("dve", lambda e: e.tensor_scalar(out=p128[:, :], in0=pcol[:, :], scalar1=128.0, scalar2=None, op0=ALU.mult),
             reads=[pcol], writes=[p128])
        iti = P.sb("iti", [128, 512], mybir.dt.int32)
        tA = P.sb("tA", [128, 512], F32)
        tB = P.sb("tB", [128, 512], F32)

        def iota_to(dst, pattern, base, cm, free):
            P.op("pool", lambda e: e.iota(iti[:, 0:free], pattern=[list(x) for x in pattern], base=base,
                                          channel_multiplier=cm), writes=[iti])
            P.op("dve", lambda e: e.tensor_copy(out=dst[:, 0:free], in_=iti[:, 0:free]), reads=[iti], writes=[dst])

        def mask_to(dst_ap, dst_buf, src, off, psign, free=512, accumulate=False):
            P.op("dve", lambda e: e.tensor_scalar(
                out=tB[:, 0:free], in0=src[:, 0:free], scalar1=p128[:, 0:1], scalar2=float(off),
                op0=(ALU.add if psign > 0 else ALU.subtract), op1=ALU.add), reads=[src, p128], writes=[tB])
            if not accumulate:
                P.op("dve", lambda e: e.tensor_scalar(out=dst_ap, in0=tB[:, 0:free], scalar1=0.0, scalar2=NEG,
                                                      op0=ALU.is_gt, op1=ALU.mult), reads=[tB], writes=[dst_buf])
            else:
                P.op("dve", lambda e: e.tensor_scalar(out=tB[:, 0:free], in0=tB[:, 0:free], scalar1=0.0, scalar2=NEG,
                                                      op0=ALU.is_gt, op1=ALU.mult), reads=[tB], writes=[tB])
                P.op("dve", lambda e: e.tensor_tensor(out=dst_ap, in0=dst_ap, in1=tB[:, 0:free], op=ALU.add),
                     reads=[tB, dst_buf], writes=[dst_buf])

        kq = P.sb("kq", [128, 512], F32)
        nkq = P.sb("nkq", [128, 512], F32)
        iota_to(kq, [[0, 4], [-1, 128]], 0, 1, 512)
        P.op("dve", lambda e: e.tensor_scalar(out=nkq[:, :], in0=kq[:, :], scalar1=-1.0, scalar2=None, op0=ALU.mult),
             reads=[kq], writes=[nkq])
        maskA = P.sb("maskA", [128, 512], BF16)
        maskB = P.sb("maskB", [128, 512], BF16)
        mask_to(maskA[:, :], maskA, kq, 0, -1)
        mask_to(maskB[:, :], maskB, kq, 128, -1)
        wmasks = {}
        for r in range(-4, 2):
            wm = P.sb(f"wm{r + 4}", [128, 512], BF16)
            mask_to(wm[:, :], wm, kq, 128 * r, -1)
            mask_to(wm[:, :], wm, nkq, -128 * r - 511, 1, accumulate=True)
            wmasks[r] = wm
        cq = P.sb("cq", [128, 32], F32)
        iota_to(cq, [[16, 17]], -1, -1, 17)
        cmask = P.sb("cmask", [128, 32], F32)
        mask_to(cmask[:, 0:17], cmask, cq, 0, -1, free=17)
        exall = P.sb("exall", [128, 64 * 128], BF16)
        for part in range(16):
            iota_to(tA, [[-2, 4], [-1, 2], [0, 64]], -8 * part, 1, 512)
            P.op("dve", lambda e, part=part: e.tensor_scalar(
                out=exall[:, part * 512:(part + 1) * 512], in0=tA[:, :], scalar1=0.0, scalar2=None, op0=ALU.is_equal),
                reads=[tA], writes=[exall])
        ge = P.sb("ge", [128, 8], F32)
        iota_to(ge, [[0, 8]], -64, 1, 8)
        P.op("dve", lambda e: e.tensor_scalar(out=ge[:, :], in0=ge[:, :], scalar1=0.0, scalar2=None, op0=ALU.is_ge),
             reads=[ge], writes=[ge])
        mk = P.sb("mk", [128, 4, 5], F32)
        m1w = P.sb("m1w", [128, 5], F32)
        m2w = P.sb("m2w", [128, 5], F32)
        P.op("dve", lambda e: e.memset(mk[:, :, :], 0.0), writes=[mk])

        def lin(dst, a, b):
            P.op("dve", lambda e: e.tensor_scalar(out=dst, in0=ge[:, 0:1], scalar1=float(a), scalar2=float(b),
                                                  op0=ALU.mult, op1=ALU.add), reads=[ge], writes=[mk])
        lin(mk[:, 0, 0:1], 1.0, 0.0)
        lin(mk[:, 1, 0:1], -10001.0, 10001.0)
        lin(mk[:, 1, 1:2], 0.0, 10002.0)
        lin(mk[:, 1, 2:3], 10004.0, -1.0)
        lin(mk[:, 1, 3:4], 0.0, -1.0)
        lin(mk[:, 1, 4:5], 0.0, -1.0)
        lin(mk[:, 2, 0:1], 0.0, 1.0)
        lin(mk[:, 2, 1:2], 0.0, 1.0)
        lin(mk[:, 2, 2:3], 1.0, 0.0)
        lin(mk[:, 3, 2:3], -10003.0, 10003.0)
        lin(mk[:, 3, 3:4], 0.0, 10004.0)
        lin(mk[:, 3, 4:5], 10006.0, -1.0)
        for dst, i0 in ((m1w, 0), (m2w, 1)):
            P.op("dve", lambda e, dst=dst, i0=i0: e.tensor_tensor(out=dst[:, :], in0=mk[:, i0 + 2, :], in1=mk[:, i0, :],
                                                                   op=ALU.subtract), reads=[mk], writes=[dst])
            P.op("dve", lambda e, dst=dst, i0=i0: e.scalar_tensor_tensor(
                out=dst[:, :], in0=dst[:, :], scalar=pcol[:, 0:1], in1=mk[:, i0, :], op0=ALU.mult, op1=ALU.add),
                reads=[dst, pcol, mk], writes=[dst])

        pS = [P.ps(f"pS{i}", [128, 512]) for i in range(2)]
        pO = [P.ps(f"pO{i}", [65, 512]) for i in range(3)]
        pC = P.ps("pC", [128, 1024])
        pT = P.ps("pT", [128, 512], BF16)

        ks16 = P.sb("ks16", [65, S], BF16)
        P.dma("pool", ks16[0:64, :], ksT.ap[:, :], writes=[ks16])
        P.op("dve", lambda e: e.memset(ks16[64:65, :], 1.0), writes=[ks16])
        vs16 = P.sb("vs16", [128, NKT, 65], BF16)
        vr = vsd.ap.rearrange("(t p) d -> p t d", p=128)
        for c0 in range(0, NKT, 32):
            c1 = min(NKT, c0 + 32)
            P.dma("pool", vs16[:, c0:c1, 0:64], vr[:, c0:c1, :], writes=[vs16])
        P.op("dve", lambda e: e.memset(vs16[:, :, 64:65], 1.0), writes=[vs16])

        RW = 16 * 512 + 32
        raw = P.sb("raw", [64, RW], BF16)
        sqt = [P.sb(f"sqt{i}", [64, 512], BF16) for i in range(2)]
        kmx = P.sb("kmx", [128, 64], F32)
        kst = P.sb("kst", [128, 4], F32)
        kn = [0]

        def kmax_chunks(src_buf, src_ap_fn, ncols):
            for c in range((ncols + 511) // 512):
                w = min(512, ncols - c * 512)
                t = sqt[kn[0] % 2]
                ps = pS[kn[0] % 2]
                P.op("pool", lambda e, t=t, c=c, w=w: e.tensor_tensor(
                    out=t[:, 0:w], in0=src_ap_fn(c * 512, w), in1=src_ap_fn(c * 512, w), op=ALU.mult),
                    reads=[src_buf], writes=[t])
                P.op("pe", _mm(ps[:, 0:w], ones16[:, 0:128], t[:, 0:w]), reads=[ones16, t], writes=[ps])
                P.op("dve", lambda e, ps=ps, w=w, k=kn[0]: e.reduce_max(out=kmx[:, k:k + 1], in_=ps[:, 0:w], axis=AX.X),
                     reads=[ps], writes=[kmx])
                kn[0] += 1

        def kmax_finish(col):
            n = kn[0]
            P.op("dve", lambda e: e.reduce_max(out=kst[:, col:col + 1], in_=kmx[:, 0:n], axis=AX.X), reads=[kmx],
                 writes=[kst])
            kn[0] = 0

        kmax_chunks(ks16, lambda c0, w: ks16[0:64, c0:c0 + w], S)
        kmax_finish(0)
        for t0 in range(0, S, 8192):
            tw = min(8192, S - t0)
            P.dma("pool", raw[:, 0:tw], kwT.ap[:, t0:t0 + tw], writes=[raw])
            kmax_chunks(raw, lambda c0, w: raw[0:64, c0:c0 + w], tw)
        kmax_finish(1)

        w1s = P.sb("w1s", [64, 32, 256], BF16)
        w2s = P.sb("w2s", [128, 2, 64], BF16)
        pes = P.sb("pes", [64, 32], BF16)
        hb = P.sb("hb", [128, 2], F32)
        xb = P.sb("xb", [128, 512], F32)
        x2 = P.sb("x2", [128, 512], F32)
        hid = P.sb("hid", [128, 2, NCP], BF16)
        P.op("dve", lambda e: e.memset(hid[:, :, :], 0.0), writes=[hid])
        kcmp = P.sb("kcmp", [64, NCP], BF16)
        P.op("dve", lambda e: e.memset(kcmp[:, :], 0.0), writes=[kcmp])
        vcmp = P.sb("vcmp", [128, NCP // 128, 64], BF16)
        c0g = math.sqrt(2.0 / math.pi)
        for kv, src in ((0, kcT), (1, vcT)):
            for t0 in range(0, 32, 8):
                P.dma("pool", w1s[:, t0:t0 + 8, :],
                      w1d.ap[kv, t0 * 64:(t0 + 8) * 64, :].rearrange("(t d) h -> d t h", d=64), writes=[w1s])
            P.dma("pool", w2s[:, :, :], w2d.ap[kv, :, :].rearrange("(c p) d -> p c d", p=128), writes=[w2s])
            P.dma("pool", pes[:, :], peT.ap[kv, :, :], writes=[pes])
            for hc in range(2):
                for t in range(32):
                    P.op("pe", _mm(pS[0][:, 0:1], w1s[:, t, hc * 128:(hc + 1) * 128], pes[:, t:t + 1],
                                   start=(t == 0), stop=(t == 31)), reads=[w1s, pes], writes=[pS[0]])
                P.op("dve", lambda e, hc=hc: e.tensor_copy(out=hb[:, hc:hc + 1], in_=pS[0][:, 0:1]), reads=[pS[0]],
                     writes=[hb])
            for n0 in range(0, NCB, 512):
                nw = min(512, NCB - n0)
                P.dma("pool", raw[:, 0:16 * (nw + 1)], src.ap[:, 16 * n0:16 * (n0 + nw + 1)], writes=[raw])
                rawv = raw[:, 0:16 * (nw + 1)].rearrange("p (c s) -> p s c", s=16)
                for hc in range(2):
                    ps = pS[1]
                    for t in range(32):
                        P.op("pe", _mm(ps[:, 0:nw], w1s[:, t, hc * 128:(hc + 1) * 128],
                                       rawv[:, t % 16, t // 16:t // 16 + nw], start=(t == 0), stop=(t == 31)),
                             reads=[w1s, raw], writes=[ps])
                    P.op("act", lambda e, ps=ps, nw=nw, hc=hc: e.activation(
                        out=xb[:, 0:nw], in_=ps[:, 0:nw], func=AF.Identity, bias=hb[:, hc:hc + 1]),
                        reads=[ps, hb], writes=[xb])
                    P.op("act", lambda e, nw=nw: e.activation(out=x2[:, 0:nw], in_=xb[:, 0:nw], func=AF.Square),
                         reads=[xb], writes=[x2])
                    P.op("dve", lambda e, nw=nw: e.tensor_scalar(out=x2[:, 0:nw], in0=x2[:, 0:nw], scalar1=0.044715,
                                                                 scalar2=1.0, op0=ALU.mult, op1=ALU.add),
                         reads=[x2], writes=[x2])
                    P.op("dve", lambda e, nw=nw: e.tensor_tensor(out=x2[:, 0:nw], in0=x2[:, 0:nw], in1=xb[:, 0:nw],
                                                                 op=ALU.mult), reads=[x2, xb], writes=[x2])
                    P.op("act", lambda e, nw=nw: e.activation(out=x2[:, 0:nw], in_=x2[:, 0:nw], func=AF.Sigmoid,
                                                              scale=2.0 * c0g), reads=[x2], writes=[x2])
                    P.op("dve", lambda e, hc=hc, n0=n0, nw=nw: e.tensor_tensor(
                        out=hid[:, hc, n0:n0 + nw], in0=x2[:, 0:nw], in1=xb[:, 0:nw], op=ALU.mult),
                        reads=[x2, xb], writes=[hid])
            if kv == 0:
                for n0 in range(0, NCB, 512):
                    nw = min(512, NCB - n0)
                    ps = pS[0]
                    for hc in range(2):
                        P.op("pe", _mm(ps[0:64, 0:nw], w2s[:, hc, :], hid[:, hc, n0:n0 + nw], start=(hc == 0),
                                       stop=(hc == 1)), reads=[w2s, hid], writes=[ps])
                    P.op("act", lambda e, ps=ps, n0=n0, nw=nw: e.activation(out=kcmp[:, n0:n0 + nw], in_=ps[0:64, 0:nw],
                                                                            func=AF.Copy), reads=[ps], writes=[kcmp])
            else:
                for ct in range(NCP // 128):
                    ps = pS[ct % 2]
                    for hc in range(2):
                        P.op("pe", _mm(ps[:, 0:64], hid[:, hc, ct * 128:(ct + 1) * 128], w2s[:, hc, :], start=(hc == 0),
                                       stop=(hc == 1)), reads=[hid, w2s], writes=[ps])
                    P.op("act", lambda e, ps=ps, ct=ct: e.activation(out=vcmp[:, ct, :], in_=ps[:, 0:64], func=AF.Copy),
                         reads=[ps], writes=[vcmp])
        kmax_chunks(kcmp, lambda c0, w: kcmp[0:64, c0:c0 + w], NCB)
        kmax_finish(2)
        negsw = P.sb("negsw", [128, 1], F32)
        negc = P.sb("negc", [128, 1], F32)
        P.op("dve", lambda e: e.tensor_tensor(out=kst[:, 3:4], in0=kst[:, 0:1], in1=kst[:, 1:2], op=ALU.max),
             reads=[kst], writes=[kst])
        P.op("act", lambda e: e.activation(out=kst[:, 2:4], in_=kst[:, 2:4], func=AF.Sqrt), reads=[kst], writes=[kst])
        P.op("dve", lambda e: e.tensor_scalar(out=negsw[:, :], in0=kst[:, 3:4], scalar1=-1.0, scalar2=None, op0=ALU.mult),
             reads=[kst], writes=[negsw])
        P.op("dve", lambda e: e.tensor_scalar(out=negc[:, :], in0=kst[:, 2:3], scalar1=-scale, scalar2=None, op0=ALU.mult),
             reads=[kst], writes=[negc])

        q16 = [P.sb(f"q16_{i}", [65, 512], BF16) for i in range(2)]
        kwb = [P.sb(f"kwb{i}", [65, 768], BF16) for i in range(2)]
        vwb = [P.sb(f"vwb{i}", [128, 6, 65], BF16) for i in range(2)]
        for i in range(2):
            P.op("dve", lambda e, i=i: e.memset(kwb[i][64:65, :], 1.0), writes=[kwb[i]])
            P.op("dve", lambda e, i=i: e.memset(vwb[i][:, :, 64:65], 1.0), writes=[vwb[i]])
        gsb = P.sb("gsb", [65, 1536], F32)
        qsq = P.sb("qsq", [64, 512], BF16)
        qn = P.sb("qn", [65, 512], F32)
        biasA = P.sb("biasA", [128, 4], F32)
        den = P.sb("den", [128, 4], F32)
        rden = P.sb("rden", [128, 4], F32)
        p4 = P.sb("p4", [128, 4, NCP], F32)
        pn16 = P.sb("pn16", [128, 4, NCP], BF16)
        P.op("pool", lambda e: e.memset(pn16[:, :, :], 0.0), writes=[pn16])
        impP = P.sb("impP", [128, 4 + NCP + 8], F32)
        P.op("pool", lambda e: e.memset(impP[:, :], 0.0), writes=[impP])
        pslc = P.sb("pslc", [128, NSB], F32)
        sc = P.sb("sc", [128, SCW], F32)
        sc2 = P.sb("sc2", [128, SCW], F32)
        m8a = P.sb("m8a", [128, 8], F32)
        m8b = P.sb("m8b", [128, 8], F32)
        selm = P.sb("selm", [128, SCW], BF16)
        selT = P.sb("selT", [128, NJ, 512], BF16)
        pnT = [P.sb(f"pnT{i}", [128, 512], BF16) for i in range(2)]
        pbs = [P.sb(f"pb{i}", [128, 512], BF16) for i in range(3)]
        osb = P.sb("osb", [65, 3, 512], F32)
        frow = P.sb("frow", [65, 3, 512], F32)
        acc = P.sb("acc", [64, 512], F32)
        acc2 = P.sb("acc2", [64, 512], F32)
        ysb = [P.sb(f"ysb{i}", [64, 512], F32) for i in range(2)]

        def loadq(i):
            P.dma("pool", q16[i % 2][0:64, :], qB.ap[i, :, :], writes=[q16[i % 2]])
            k0 = max(0, 2 * i - 4)
            k1 = 2 * i + 2
            o0 = k0 - (2 * i - 4)
            P.dma("pool", kwb[i % 2][0:64, o0 * 128:6 * 128], kwT.ap[:, k0 * 128:k1 * 128], writes=[kwb[i % 2]])
            P.dma("pool", vwb[i % 2][:, o0:6, 0:64],
                  vwd.ap[k0 * 128:k1 * 128, :].rearrange("(t p) d -> p t d", p=128), writes=[vwb[i % 2]])

        cnt = [0]

        def attend(q, tiles, acc_ps):
            def emit_s(n):
                k_ap, k_buf, _, _, extra = tiles[n]
                ps = pS[(cnt[0] + n) % 2]
                P.op("pe", _mm(ps[:, :], k_ap, q[0:65, :], start=True, stop=False), reads=[k_buf, q], writes=[ps])
                for xi, (l_ap, r_ap, bufs) in enumerate(extra):
                    P.op("pe", _mm(ps[:, :], l_ap, r_ap, start=False, stop=(xi == len(extra) - 1)), reads=bufs,
                         writes=[ps])
            emit_s(0)
            for n, (_, _, v_ap, v_buf, _) in enumerate(tiles):
                ps = pS[(cnt[0] + n) % 2]
                pb = pbs[(cnt[0] + n) % 3]
                P.op("act", lambda e, ps=ps, pb=pb: e.activation(out=pb[:, :], in_=ps[:, :], func=AF.Exp, scale=scale),
                     reads=[ps], writes=[pb])
                if n + 1 < len(tiles):
                    emit_s(n + 1)
                P.op("pe", _mm(acc_ps[0:65, :], v_ap, pb[:, :], start=(n == 0), stop=(n == len(tiles) - 1)),
                     reads=[v_buf, pb], writes=[acc_ps])
            cnt[0] += len(tiles)

        loadq(0)
        for i in range(NQ):
            if i + 1 < NQ:
                loadq(i + 1)
            q = q16[i % 2]
            P.dma("sp", gsb[64:65, :], gB.ap[i, :, :], writes=[gsb])
            P.op("pool", lambda e, q=q: e.tensor_tensor(out=qsq[:, :], in0=q[0:64, :], in1=q[0:64, :], op=ALU.mult),
                 reads=[q], writes=[qsq])
            P.op("pe", _mm(pC[0:65, 0:512], ones16[:, 0:65], qsq[:, :]), reads=[ones16, qsq], writes=[pC])
            P.op("act", lambda e: e.activation(out=qn[64:65, :], in_=pC[64:65, 0:512], func=AF.Sqrt), reads=[pC],
                 writes=[qn])
            P.op("dve", lambda e, q=q: e.tensor_scalar(out=q[64:65, :], in0=qn[64:65, :], scalar1=negsw[64:65, 0:1],
                                                       scalar2=None, op0=ALU.mult), reads=[qn, negsw], writes=[q])
            for h in range(4):
                P.op("pe", _mm(pC[:, 512 + h:513 + h], qsq[:, h * 128:(h + 1) * 128], ones16[:, 0:1]),
                     reads=[qsq, ones16], writes=[pC])
            P.op("act", lambda e: e.activation(out=biasA[:, :], in_=pC[:, 512:516], func=AF.Sqrt), reads=[pC],
                 writes=[biasA])
            P.op("dve", lambda e: e.tensor_scalar(out=biasA[:, :], in0=biasA[:, :], scalar1=negc[:, 0:1], scalar2=None,
                                                  op0=ALU.mult), reads=[biasA, negc], writes=[biasA])
            ncv = min(NCB, 16 * i + 15)
            mlo = max(0, 16 * i - 2)
            mhi = min(ncv, 16 * i + 15)
            for h in range(4):
                for n0 in range(0, ncv, 512):
                    nw = min(512, ncv - n0)
                    P.op("pe", _mm(pC[:, n0:n0 + nw], q[0:64, h * 128:(h + 1) * 128], kcmp[:, n0:n0 + nw]),
                         reads=[q, kcmp], writes=[pC])
                P.op("dve", lambda e, mlo=mlo, mhi=mhi, i=i: e.tensor_tensor(
                    out=pC[:, mlo:mhi], in0=pC[:, mlo:mhi], in1=cmask[:, mlo - (16 * i - 2):mhi - (16 * i - 2)],
                    op=ALU.add), reads=[pC, cmask], writes=[pC])
                P.op("act", lambda e, h=h, ncv=ncv: e.activation(
                    out=p4[:, h, 0:ncv], in_=pC[:, 0:ncv], func=AF.Exp, scale=scale, bias=biasA[:, h:h + 1],
                    accum_out=den[:, h:h + 1]), reads=[pC, biasA], writes=[p4, den])
            P.op("dve", lambda e: e.tensor_scalar(out=rden[:, :], in0=den[:, :], scalar1=1e-30, scalar2=None, op0=ALU.max),
                 reads=[den], writes=[rden])
            P.op("dve", lambda e: e.reciprocal(out=rden[:, :], in_=rden[:, :]), reads=[rden], writes=[rden])
            for h in range(4):
                P.op("pool", lambda e, h=h, ncv=ncv: e.tensor_scalar(
                    out=pn16[:, h, 0:ncv], in0=p4[:, h, 0:ncv], scalar1=rden[:, h:h + 1], scalar2=None, op0=ALU.mult),
                    reads=[p4, rden], writes=[pn16])
                if h == 0:
                    P.op("dve", lambda e, ncv=ncv: e.tensor_scalar(
                        out=impP[:, 4:4 + ncv], in0=p4[:, 0, 0:ncv], scalar1=rden[:, 0:1], scalar2=None, op0=ALU.mult),
                        reads=[p4, rden], writes=[impP])
                else:
                    P.op("dve", lambda e, h=h, ncv=ncv: e.scalar_tensor_tensor(
                        out=impP[:, 4:4 + ncv], in0=p4[:, h, 0:ncv], scalar=rden[:, h:h + 1], in1=impP[:, 4:4 + ncv],
                        op0=ALU.mult, op1=ALU.add), reads=[p4, rden, impP], writes=[impP])
            ncol = min(NSB, 4 * i + 4)
            v0 = impP[:, 3:3 + 4 * (NSB + 1)].rearrange("p (j k) -> p j k", k=4)
            P.op("dve", lambda e, v0=v0: e.reduce_sum(out=pslc[:, :], in_=v0[:, 0:NSB, :], axis=AX.X), reads=[impP],
                 writes=[pslc])
            P.op("dve", lambda e, v0=v0: e.tensor_tensor(out=pslc[:, :], in0=pslc[:, :], in1=v0[:, 1:NSB + 1, 0],
                                                         op=ALU.add), reads=[pslc, impP], writes=[pslc])
            P.op("pool", lambda e: e.memset(sc[:, :], -1.0), writes=[sc])
            P.op("dve", lambda e, ncol=ncol: e.tensor_copy(out=sc[:, 0:ncol], in_=pslc[:, 0:ncol]), reads=[pslc],
                 writes=[sc])
            jlo = max(0, 4 * i - 1)
            jhi = min(NSB, 4 * i + 4)
            a0 = jlo - (4 * i - 1)
            a1 = a0 + (jhi - jlo)
            P.op("dve", lambda e, jlo=jlo, jhi=jhi, a0=a0, a1=a1: e.tensor_tensor(
                out=sc[:, jlo:jhi], in0=sc[:, jlo:jhi], in1=m1w[:, a0:a1], op=ALU.mult), reads=[sc, m1w], writes=[sc])
            P.op("dve", lambda e, jlo=jlo, jhi=jhi, a0=a0, a1=a1: e.tensor_tensor(
                out=sc[:, jlo:jhi], in0=sc[:, jlo:jhi], in1=m2w[:, a0:a1], op=ALU.add), reads=[sc, m2w], writes=[sc])
            P.op("dve", lambda e: e.memset(sc[:, 0:1], 10000.0), writes=[sc])
            P.op("dve", lambda e: e.max(out=m8a[:, :], in_=sc[:, :]), reads=[sc], writes=[m8a])
            P.op("dve", lambda e: e.match_replace(out=sc2[:, :], in_to_replace=m8a[:, :], in_values=sc[:, :],
                                                  imm_value=-2.0), reads=[m8a, sc], writes=[sc2])
            P.op("dve", lambda e: e.max(out=m8b[:, :], in_=sc2[:, :]), reads=[sc2], writes=[m8b])
            P.op("dve", lambda e: e.tensor_scalar(out=selm[:, :], in0=sc[:, :], scalar1=m8b[:, 7:8], scalar2=NEG,
                                                  op0=ALU.is_lt, op1=ALU.mult), reads=[sc, m8b], writes=[selm])
            njc = (ncol + 127) // 128
            for jc in range(njc):
                for h in range(4):
                    P.op("pe", lambda e, jc=jc, h=h: e.transpose(out=pT[:, h * 128:(h + 1) * 128],
                                                                  in_=selm[:, jc * 128:(jc + 1) * 128],
                                                                  identity=id16[:, :]), reads=[selm, id16], writes=[pT])
                P.op("act", lambda e, jc=jc: e.activation(out=selT[:, jc, :], in_=pT[:, :], func=AF.Copy), reads=[pT],
                     writes=[selT])
            ncc = (ncv + 127) // 128
            for cc in range(ncc):
                pt = pnT[cc % 2]
                for h in range(4):
                    P.op("pe", lambda e, cc=cc, h=h: e.transpose(out=pT[:, h * 128:(h + 1) * 128],
                                                                  in_=pn16[:, h, cc * 128:(cc + 1) * 128],
                                                                  identity=id16[:, :]), reads=[pn16, id16], writes=[pT])
                P.op("dve", lambda e, pt=pt: e.tensor_copy(out=pt[:, :], in_=pT[:, :]), reads=[pT], writes=[pt])
                P.op("pe", _mm(pO[2][0:64, :], vcmp[:, cc, :], pt[:, :], start=(cc == 0), stop=(cc == ncc - 1)),
                     reads=[vcmp, pt], writes=[pO[2]])
            tiles = []
            for kt in range(2 * i + 2):
                ex = [(exall[:, (kt % 64) * 128:(kt % 64 + 1) * 128], selT[:, kt // 64, :], [exall, selT])]
                if kt == 2 * i:
                    ex.append((id16[:, :], maskA[:, :], [id16, maskA]))
                if kt == 2 * i + 1:
                    ex.append((id16[:, :], maskB[:, :], [id16, maskB]))
                tiles.append((ks16[0:65, kt * 128:(kt + 1) * 128], ks16, vs16[:, kt, :], vs16, ex))
            attend(q, tiles, pO[0])
            tiles = []
            kw_, vw_ = kwb[i % 2], vwb[i % 2]
            for r in range(-4, 2):
                if 2 * i + r >= 0:
                    tiles.append((kw_[0:65, (r + 4) * 128:(r + 5) * 128], kw_, vw_[:, r + 4, :], vw_,
                                  [(id16[:, :], wmasks[r][:, :], [id16, wmasks[r]])]))
            attend(q, tiles, pO[1])
            P.op("act", lambda e: e.activation(out=frow[64:65, :, :], in_=gsb[64:65, :].rearrange(
                "p (b n) -> p b n", b=3), func=AF.Sigmoid), reads=[gsb], writes=[frow])
            P.op("act", lambda e: e.activation(out=osb[:, 0, :], in_=pO[0][0:65, :], func=AF.Copy), reads=[pO[0]],
                 writes=[osb])
            P.op("act", lambda e: e.activation(out=osb[:, 1, :], in_=pO[1][0:65, :], func=AF.Copy), reads=[pO[1]],
                 writes=[osb])
            P.op("act", lambda e: e.activation(out=osb[0:64, 2, :], in_=pO[2][0:64, :], func=AF.Copy), reads=[pO[2]],
                 writes=[osb])
            P.op("dve", lambda e: e.tensor_scalar(out=osb[64:65, 0:2, :], in0=osb[64:65, 0:2, :], scalar1=1e-30,
                                                  scalar2=None, op0=ALU.max), reads=[osb], writes=[osb])
            P.op("dve", lambda e: e.reciprocal(out=osb[64:65, 0:2, :], in_=osb[64:65, 0:2, :]), reads=[osb], writes=[osb])
            P.op("dve", lambda e: e.tensor_tensor(out=frow[64:65, 1:3, :], in0=frow[64:65, 1:3, :], in1=osb[64:65, 0:2, :],
                                                  op=ALU.mult), reads=[frow, osb], writes=[frow])
            P.op("pe", _mm(pC[0:64, 0:512], ones32[64:65, 0:64], frow[64:65, 1, :]), reads=[ones32, frow], writes=[pC])
            P.op("pe", _mm(pC[0:64, 512:1024], ones32[64:65, 0:64], frow[64:65, 2, :]), reads=[ones32, frow], writes=[pC])
            P.op("dve", lambda e: e.tensor_tensor(out=acc[:, :], in0=osb[0:64, 0, :], in1=pC[0:64, 0:512], op=ALU.mult),
                 reads=[osb, pC], writes=[acc])
            P.op("dve", lambda e: e.tensor_tensor(out=acc2[:, :], in0=osb[0:64, 1, :], in1=pC[0:64, 512:1024],
                                                  op=ALU.mult), reads=[osb, pC], writes=[acc2])
            P.op("dve", lambda e: e.tensor_tensor(out=acc[:, :], in0=acc[:, :], in1=acc2[:, :], op=ALU.add),
                 reads=[acc, acc2], writes=[acc])
            P.op("pe", _mm(pC[0:64, 0:512], ones32[64:65, 0:64], frow[64:65, 0, :]), reads=[ones32, frow], writes=[pC])
            P.op("dve", lambda e: e.tensor_tensor(out=acc2[:, :], in0=osb[0:64, 2, :], in1=pC[0:64, 0:512], op=ALU.mult),
                 reads=[osb, pC], writes=[acc2])
            y = ysb[i % 2]
            P.op("dve", lambda e, y=y: e.tensor_tensor(out=y[:, :], in0=acc[:, :], in1=acc2[:, :], op=ALU.add),
                 reads=[acc, acc2], writes=[y])
            P.dma("sp", yB.ap[i, :, :], y[:, :], reads=[y], writes=[yB])
        P.finish_outputs([yB])
        P.emit()
    return nc


_PROGS = {}


def _prog(key, builder):
    if key not in _PROGS:
        _PROGS[key] = builder()
    return _PROGS[key]


def _run(nc, maps):
    res = run_bass_kernel_spmd(nc, maps, core_ids=list(range(len(maps))))
    return res.results


def _c(a):
    return np.ascontiguousarray(a, dtype=np.float32)


def kernel(x, norm1, w_in, ml_conv, ml_gate_bias, ml_norm, da_lambda, da_norm, nsa_pe, nsa_w1, nsa_w2, w_out, norm2,
           w_ff1, w_ff2, final_norm):
    x = np.asarray(x, np.float32)
    B, S, D = x.shape
    NCORE = 8
    QT = S * B // NCORE
    RPB = S // QT
    for l in range(DEPTH):
        nc = _prog(("l1", QT), lambda: build_l1(QT))
        maps = []
        for c in range(NCORE):
            b, r = divmod(c, RPB)
            xs = x[b, r * QT:(r + 1) * QT]
            maps.append({"x": _c(xs), "xT": _c(xs.T), "g": _c(norm1[l][:, None]), "w": _c(w_in[l])})
        res = _run(nc, maps)
        z = np.stack([np.concatenate([res[b * RPB + r]["z"] for r in range(RPB)], 0) for b in range(B)], 0)
        mixT = np.zeros((B, D_MODEL, S), np.float32)
        nc = _prog(("ml", S), lambda: build_ml(S))
        maps = []
        for c in range(NCORE):
            b, j = divmod(c, ML_HEADS)
            zb = z[b]
            pad = np.zeros((64, 3), np.float32)
            ifg = np.stack([zb[:, 1024 + j], zb[:, 1028 + j]], -1).reshape(S // 128, 128, 2).transpose(1, 0, 2)
            maps.append({
                "qTp": _c(np.concatenate([pad, zb[:, j * 64:(j + 1) * 64].T], 1)),
                "kTp": _c(np.concatenate([pad, zb[:, 256 + j * 64:256 + (j + 1) * 64].T], 1)),
                "cwq": _c(ml_conv[l][:, j * 64:(j + 1) * 64].T),
                "cwk": _c(ml_conv[l][:, 256 + j * 64:256 + (j + 1) * 64].T),
                "v": _c(zb[:, 512 + j * 64:512 + (j + 1) * 64]),
                "oT": _c(zb[:, 768 + j * 64:768 + (j + 1) * 64].T),
                "ifg": _c(ifg),
                "gb": _c(np.asarray([[ml_gate_bias[l][j], ml_gate_bias[l][ML_HEADS + j]]])),
                "gain": _c(ml_norm[l][j * 64:(j + 1) * 64][:, None]),
            })
        res = _run(nc, maps)
        for c in range(NCORE):
            b, j = divmod(c, ML_HEADS)
            mixT[b, j * 64:(j + 1) * 64] = res[c]["yT"]
        nc = _prog(("da", S, l), lambda: build_da(S, l))
        maps = []
        for c in range(NCORE):
            b, j = divmod(c, DA_HEADS)
            zb = z[b]
            maps.append({
                "qT": _c(zb[:, 1032:1288].reshape(S, 4, 2, 32)[:, j].transpose(1, 2, 0)),
                "kT": _c(zb[:, 1288:1544].reshape(S, 4, 2, 32)[:, j].transpose(1, 2, 0)),
                "v": _c(zb[:, 1544 + j * 64:1544 + (j + 1) * 64]),
                "lam4": _c(np.asarray(da_lambda[l]).reshape(1, 128)),
                "gain": _c(da_norm[l][j * 64:(j + 1) * 64][:, None]),
            })
        res = _run(nc, maps)
        for c in range(NCORE):
            b, j = divmod(c, DA_HEADS)
            mixT[b, 256 + j * 64:256 + (j + 1) * 64] = res[c]["yT"]
        nc = _prog(("nsa", S), lambda: build_nsa(S))
        NQ = S // 256
        maps = []
        for c in range(NCORE):
            b, gp = divmod(c, 4)
            g, p = divmod(gp, 2)
            zb = z[b]
            qg = zb[:, 1800:2312].reshape(S, 2, 4, 64)[:, g].reshape(S // 128, 128, 4, 64)[p::2]
            gg = zb[:, 3080:3104].reshape(S, 2, 4, 3)[:, g].reshape(S // 128, 128, 4, 3)[p::2]
            sl = slice(g * 64, (g + 1) * 64)
            maps.append({
                "qB": _c(qg.transpose(0, 3, 2, 1).reshape(NQ, 64, 512)),
                "gB": _c(gg.transpose(0, 3, 2, 1).reshape(NQ, 1, 1536)),
                "kcT": _c(zb[:, 2312:2440][:, sl].T), "vcT": _c(zb[:, 2440:2568][:, sl].T),
                "ksT": _c(zb[:, 2568:2696][:, sl].T), "kwT": _c(zb[:, 2824:2952][:, sl].T),
                "vs": _c(zb[:, 2696:2824][:, sl]), "vw": _c(zb[:, 2952:3080][:, sl]),
                "peT": _c(np.asarray(nsa_pe[l]).transpose(0, 2, 1)), "w1": _c(nsa_w1[l]), "w2": _c(nsa_w2[l]),
                "par": np.array([[float(p)]], np.float32),
            })
        res = _run(nc, maps)
        for c in range(NCORE):
            b, gp = divmod(c, 4)
            g, p = divmod(gp, 2)
            y = res[c]["yB"].reshape(NQ, 64, 4, 128)
            dst = mixT[b, 512 + g * 256:512 + (g + 1) * 256].reshape(4, 64, S // 128, 128)
            dst[:, :, p::2, :] = y.transpose(2, 1, 0, 3)
        final = (l == DEPTH - 1)
        nc = _prog(("l3", QT, final), lambda: build_l3(QT, final))
        maps = []
        for c in range(NCORE):
            b, r = divmod(c, RPB)
            maps.append({"mixT": _c(mixT[b][:, r * QT:(r + 1) * QT]), "x": _c(x[b, r * QT:(r + 1) * QT]),
                         "wo": _c(w_out[l]), "g2": _c(norm2[l][:, None]), "w1": _c(w_ff1[l]), "w2": _c(w_ff2[l]),
                         "gf": _c(np.asarray(final_norm)[None, :])})
        res = _run(nc, maps)
        x = np.stack([np.concatenate([res[b * RPB + r]["out"] for r in range(RPB)], 0) for b in range(B)], 0)
    return x.astype(np.float32)
```

```python
import contextlib
import math
import numpy as np
import concourse.bass as bass
import concourse.mybir as mybir
from concourse.bass_utils import run_bass_kernel_spmd

F32 = mybir.dt.float32
BF16 = mybir.dt.bfloat16
AF = mybir.ActivationFunctionType
ALU = mybir.AluOpType
AX = mybir.AxisListType

D_MODEL = 1024
DEPTH = 2
ML_HEADS, ML_DIM = 4, 64
DA_HEADS, DA_QK, DA_V = 4, 32, 64
NSA_GROUPS, NSA_REP, NSA_DIM = 2, 4, 64
CMP_BLOCK, CMP_STRIDE, CMP_HIDDEN = 32, 16, 256
SEL_BLOCK, SEL_TOPK, WINDOW = 64, 16, 512
D_FF = 4096
IN_COLS = 3104
EPS = 1e-6
NEG = -30000.0

EPOCH_ENG = 30000
EPOCH_DMA = 2000


class Buf:
    def __init__(self, prog, ap, name):
        self.p = prog
        self.ap = ap
        self.name = name
        self.lw = None
        self.rd = {}
        self.dma = None

    def __getitem__(self, k):
        return self.ap[k]


class Counter:
    def __init__(self, prog, name, step):
        self.p = prog
        self.name = name
        self.step = step
        self.epoch = EPOCH_ENG if step == 1 else EPOCH_DMA
        self.n = 0
        self.sems = []

    def _sem(self, i):
        while len(self.sems) <= i:
            self.sems.append(self.p.new_sem(f"{self.name}_{len(self.sems)}"))
        return self.sems[i]

    def next(self):
        self.n += 1
        return self._sem((self.n - 1) // self.epoch), self.step

    def wait_specs(self, v):
        out = []
        last = (v - 1) // self.epoch
        for ep in range(last + 1):
            hv = (min(v, (ep + 1) * self.epoch) - ep * self.epoch) * self.step
            out.append((self._sem(ep), hv, ep))
        return out


class Prog:
    ENG = ("pe", "act", "dve", "pool", "sp")

    def __init__(self, nc, stack, same_engine_sync=True):
        self.nc = nc
        self.stack = stack
        self.semstack = stack
        self.ops = {e: [] for e in self.ENG}
        self.cnt = {e: Counter(self, "c" + e, 1) for e in self.ENG}
        self.waited = {e: {} for e in self.ENG}
        self.same = same_engine_sync
        self.nsem = 0
        self.out_waits = None
        self.ninst = 0
        self.temps = None
        self.inherit = {}

    def new_sem(self, name):
        self.nsem += 1
        return self.semstack.enter_context(self.nc.semaphore(name))

    def sb(self, name, shape, dt):
        t = self.stack.enter_context(self.nc.sbuf_tensor(name, list(shape), dt))
        b = Buf(self, t, name)
        if self.temps is not None:
            self.temps.append(b)
        else:
            b.rd.update(self.inherit)
        return b

    @contextlib.contextmanager
    def temp_scope(self):
        outer = self.stack
        with contextlib.ExitStack() as sub:
            self.stack = sub
            self.temps = []
            try:
                yield
            finally:
                for b in self.temps:
                    for c, v in ([b.lw] if b.lw else []) + list(b.rd.items()):
                        self.inherit[c] = max(self.inherit.get(c, 0), v)
                self.temps = None
                self.stack = outer

    def ps(self, name, shape, dt=F32):
        t = self.stack.enter_context(self.nc.psum_tensor(name, list(shape), dt))
        return Buf(self, t, name)

    def dram(self, name, shape, dt, kind):
        t = self.nc.dram_tensor(name, list(shape), dt, kind=kind).ap()
        return Buf(self, t, name)

    def _deps(self, eng, reads, writes):
        deps = []
        for b in reads:
            if b.lw is not None:
                deps.append(b.lw)
        for b in writes:
            if b.lw is not None:
                deps.append(b.lw)
            deps.extend(b.rd.items())
        out = []
        w = self.waited[eng]
        for c, v in deps:
            if c is self.cnt[eng] and (eng == "pe" or not self.same):
                continue
            if c.step == 16:
                v = c.n
            for sem, hv, ep in c.wait_specs(v):
                key = (id(c), ep)
                if w.get(key, 0) >= hv:
                    continue
                w[key] = hv
                out.append((sem, hv))
        return out

    def op(self, eng, fn, reads=(), writes=()):
        waits = self._deps(eng, reads, writes)
        c = self.cnt[eng]
        sem, amt = c.next()
        v = c.n
        for b in reads:
            b.rd[c] = v
        for b in writes:
            b.lw = (c, v)
            b.rd = {}
        self.ops[eng].append((waits, fn, sem, amt))
        self.ninst += 1 + len(waits)

    def dma(self, q, out_ap, in_ap, reads=(), writes=(), **kw):
        waits = self._deps(q, reads, writes)
        owner = writes[0]
        if owner.dma is None:
            owner.dma = Counter(self, "d" + owner.name, 16)
        c = owner.dma
        sem, amt = c.next()
        v = c.n
        for b in reads:
            b.rd[c] = v
        for b in writes:
            b.lw = (c, v)
            b.rd = {}
        self.ops[q].append((waits, lambda e: e.dma_start(out=out_ap, in_=in_ap, **kw), sem, amt))
        self.ninst += 1 + len(waits)

    def finish_outputs(self, bufs, eng="sp"):
        waits = []
        for b in bufs:
            c = b.dma
            for sem, hv, ep in c.wait_specs(c.n):
                waits.append((sem, hv))
        self.out_waits = (eng, waits)

    def emit(self):
        nc = self.nc
        amap = {"pe": "tensor", "act": "scalar", "dve": "vector", "pool": "gpsimd", "sp": "sync"}
        with nc.Block() as block:
            for ename in self.ENG:
                ops = self.ops[ename]
                final = self.out_waits[1] if (self.out_waits and self.out_waits[0] == ename) else []

                def body(e, ops=ops, final=final):
                    for waits, fn, sem, amt in ops:
                        for s, v in waits:
                            e.wait_ge(s, v)
                        fn(e).then_inc(sem, amt)
                    for s, v in final:
                        e.wait_ge(s, v)

                if not ops and not final:
                    continue
                getattr(block, amap[ename])(body)


def rsqrt(P, ob, o_ap, ib, i_ap, scale, bias_buf):
    bp = o_ap.base_partition()
    np_ = o_ap.shape[0]
    P.op("act", lambda e: e.activation(out=o_ap, in_=i_ap, func=AF.Sqrt, bias=bias_buf[bp:bp + np_, 0:1], scale=scale),
         reads=[ib, bias_buf], writes=[ob])
    P.op("dve", lambda e: e.reciprocal(out=o_ap, in_=o_ap), reads=[ob], writes=[ob])


def const_col(P, name, val, parts=128):
    b = P.sb(name, [parts, 1], F32)
    P.op("pool", lambda e: e.memset(b[:, :], val), writes=[b])
    return b


def _mm(out, lhsT, rhs, start=True, stop=True):
    return lambda e: e.matmul(out, lhsT=lhsT, rhs=rhs, start=start, stop=stop)


def build_l1(T):
    nc = bass.Bass("TRN2", target_bir_lowering=False)
    with contextlib.ExitStack() as st:
        P = Prog(nc, st)
        x = P.dram("x", [T, D_MODEL], F32, "ExternalInput")
        xT = P.dram("xT", [D_MODEL, T], F32, "ExternalInput")
        g = P.dram("g", [D_MODEL, 1], F32, "ExternalInput")
        w = P.dram("w", [D_MODEL, IN_COLS], F32, "ExternalInput")
        z = P.dram("z", [T, IN_COLS], F32, "ExternalOutput")
        KC = D_MODEL // 128
        wsb = P.sb("wsb", [128, KC, IN_COLS], BF16)
        gsb = P.sb("gsb", [128, KC], F32)
        for k in range(KC):
            P.dma("pool", wsb[:, k, :], w[k * 128:(k + 1) * 128, :], writes=[wsb])
        P.dma("sp", gsb[:, :], g.ap.rearrange("(k p) o -> p (k o)", p=128), writes=[gsb],
              allow_slow_non_contiguous=True)
        epsb = const_col(P, "epsb", EPS)
        TT = 512
        nbuf = 2
        xts = [P.sb(f"xt{i}", [128, TT // 128, D_MODEL], F32) for i in range(nbuf)]
        xTs = [P.sb(f"xTs{i}", [128, KC, TT], F32) for i in range(nbuf)]
        hTs = [P.sb(f"hT{i}", [128, KC, TT], BF16) for i in range(nbuf)]
        junk = P.sb("junk", [128, D_MODEL], F32)
        sss = [P.sb(f"ss{i}", [128, TT // 128], F32) for i in range(nbuf)]
        rstds = [P.sb(f"rstd{i}", [128, TT // 128], F32) for i in range(nbuf)]
        zsb = [P.sb(f"zsb{i}", [128, IN_COLS], F32) for i in range(2)]
        pss = [P.ps(f"ps{i}", [128, 512]) for i in range(4)]
        chunks = [(c, min(512, IN_COLS - c)) for c in range(0, IN_COLS, 512)]
        pi = 0
        zi = 0
        def load(it):
            b = it % nbuf
            t0 = it * TT
            P.dma("sp", xts[b][:, :, :], x.ap[t0:t0 + TT, :].rearrange("(s p) d -> p s d", p=128), writes=[xts[b]])
            P.dma("sp", xTs[b][:, :, :], xT.ap[:, t0:t0 + TT].rearrange("(k p) t -> p k t", p=128), writes=[xTs[b]])

        load(0)
        for it in range(T // TT):
            b = it % nbuf
            t0 = it * TT
            xt, xTt, hT, ss, rstd = xts[b], xTs[b], hTs[b], sss[b], rstds[b]
            if it + 1 < T // TT:
                load(it + 1)
            for s in range(TT // 128):
                P.op("act", lambda e, s=s, xt=xt, ss=ss: e.activation(
                    out=junk[:, :], in_=xt[:, s, :], func=AF.Square, accum_out=ss[:, s:s + 1]),
                    reads=[xt], writes=[junk, ss])
            rsqrt(P, rstd, rstd[:, :], ss, ss[:, :], 1.0 / D_MODEL, epsb)
            for k in range(KC):
                P.op("pool", lambda e, k=k, hT=hT, xTt=xTt: e.tensor_scalar(
                    out=hT[:, k, :], in0=xTt[:, k, :], scalar1=gsb[:, k:k + 1], scalar2=None, op0=ALU.mult),
                    reads=[xTt, gsb], writes=[hT])
            for s in range(TT // 128):
                zt = zsb[zi % 2]
                zi += 1
                for (c0, cw) in chunks:
                    ps = pss[pi % 4]
                    pi += 1
                    for k in range(KC):
                        P.op("pe", _mm(ps[:, :cw], hT[:, k, s * 128:(s + 1) * 128], wsb[:, k, c0:c0 + cw],
                                       start=(k == 0), stop=(k == KC - 1)),
                             reads=[hT, wsb], writes=[ps])
                    P.op("act", lambda e, ps=ps, zt=zt, c0=c0, cw=cw, s=s, rstd=rstd: e.activation(
                        out=zt[:, c0:c0 + cw], in_=ps[:, :cw], func=AF.Copy, scale=rstd[:, s:s + 1]),
                        reads=[ps, rstd], writes=[zt])
                P.dma("sp", z.ap[t0 + s * 128:t0 + (s + 1) * 128, :], zt[:, :], reads=[zt], writes=[z])
        P.finish_outputs([z])
        P.emit()
    return nc


def make_ident(P, name="ident"):
    it = P.sb(name + "_i", [128, 128], mybir.dt.int32)
    idf = P.sb(name, [128, 128], F32)
    P.op("pool", lambda e: e.iota(it[:, :], pattern=[[1, 128]], base=0, channel_multiplier=-1), writes=[it])
    P.op("dve", lambda e: e.tensor_copy(out=idf[:, :], in_=it[:, :]), reads=[it], writes=[idf])
    P.op("dve", lambda e: e.tensor_single_scalar(out=idf[:, :], in_=idf[:, :], scalar=0.0, op=ALU.is_equal),
         reads=[idf], writes=[idf])
    return idf


def build_l3(T, final):
    nc = bass.Bass("TRN2", target_bir_lowering=False)
    with contextlib.ExitStack() as st:
        P = Prog(nc, st)
        mixT = P.dram("mixT", [D_MODEL, T], F32, "ExternalInput")
        x = P.dram("x", [T, D_MODEL], F32, "ExternalInput")
        wo = P.dram("wo", [D_MODEL, D_MODEL], F32, "ExternalInput")
        g2 = P.dram("g2", [D_MODEL, 1], F32, "ExternalInput")
        w1 = P.dram("w1", [D_MODEL, D_FF], F32, "ExternalInput")
        w2 = P.dram("w2", [D_FF, D_MODEL], F32, "ExternalInput")
        gf = P.dram("gf", [1, D_MODEL], F32, "ExternalInput")
        out = P.dram("out", [T, D_MODEL], F32, "ExternalOutput")
        KC = D_MODEL // 128
        FC = D_FF // 128
        TT = 256
        NS = TT // 128
        wos = [P.sb(f"wo{k}", [128, D_MODEL], BF16) for k in range(KC)]
        w1s = [P.sb(f"w1_{k}", [128, D_FF], BF16) for k in range(KC)]
        w2s = [P.sb(f"w2_{k}", [128, 4, D_MODEL], BF16) for k in range(FC // 4)]
        g2sb = P.sb("g2sb", [128, KC], F32)
        epsb = const_col(P, "epsb", EPS)
        ident = make_ident(P)
        mixs = [P.sb(f"mix{i}", [128, KC, TT], BF16) for i in range(2)]
        xts = [P.sb(f"xt{i}", [128, NS, D_MODEL], F32) for i in range(2)]
        x1T = P.sb("x1T", [128, KC, TT], BF16)
        uT = P.sb("uT", [128, FC, TT], BF16)
        rl = [P.sb(f"rl{i}", [128, 512], F32) for i in range(2)]
        junk = P.sb("junk", [128, D_MODEL], F32)
        ss = P.sb("ss", [128, NS], F32)
        rstd = P.sb("rstd", [128, NS], F32)
        r2 = P.sb("r2", [128, NS], F32)
        pss = [P.ps(f"ps{i}", [128, 512]) for i in range(6)]
        pi = [0]

        def nps():
            pi[0] += 1
            return pss[pi[0] % len(pss)]

        def load(it):
            b = it % 2
            t0 = it * TT
            P.dma("pool", mixs[b][:, :, :], mixT.ap[:, t0:t0 + TT].rearrange("(k p) t -> p k t", p=128),
                  writes=[mixs[b]])
            P.dma("sp", xts[b][:, :, :], x.ap[t0:t0 + TT, :].rearrange("(s p) d -> p s d", p=128), writes=[xts[b]])

        for k in range(KC):
            P.dma("pool", wos[k][:, :], wo.ap[k * 128:(k + 1) * 128, :], writes=[wos[k]])
        load(0)
        P.dma("sp", g2sb[:, :], g2.ap.rearrange("(k p) o -> p (k o)", p=128), writes=[g2sb],
              allow_slow_non_contiguous=True)
        if final:
            gfb = P.sb("gfb", [128, D_MODEL], F32)
            P.dma("sp", gfb[:, :], gf.ap.partition_broadcast(128), writes=[gfb])
        for k in range(KC):
            P.dma("pool", w1s[k][:, :], w1.ap[k * 128:(k + 1) * 128, :], writes=[w1s[k]])
        for k in range(FC // 4):
            P.dma("pool", w2s[k][:, :, :], w2.ap[k * 512:(k + 1) * 512, :].rearrange("(c p) d -> p c d", p=128),
                  writes=[w2s[k]])

        for it in range(T // TT):
            b = it % 2
            t0 = it * TT
            mix, xt = mixs[b], xts[b]
            if it + 1 < T // TT:
                load(it + 1)
            for s in range(NS):
                for hf in range(2):
                    ps = nps()
                    for k in range(KC):
                        P.op("pe", _mm(ps[:, :], mix[:, k, s * 128:(s + 1) * 128], wos[k][:, hf * 512:(hf + 1) * 512],
                                       start=(k == 0), stop=(k == KC - 1)), reads=[mix, wos[k]], writes=[ps])
                    P.op("dve", lambda e, ps=ps, xt=xt, s=s, hf=hf: e.tensor_tensor(
                        out=xt[:, s, hf * 512:(hf + 1) * 512], in0=ps[:, :], in1=xt[:, s, hf * 512:(hf + 1) * 512],
                        op=ALU.add), reads=[ps, xt], writes=[xt])
            for s in range(NS):
                P.op("act", lambda e, s=s, xt=xt: e.activation(
                    out=junk[:, :], in_=xt[:, s, :], func=AF.Square, accum_out=ss[:, s:s + 1]),
                    reads=[xt], writes=[junk, ss])
            rsqrt(P, rstd, rstd[:, :], ss, ss[:, :], 1.0 / D_MODEL, epsb)
            P.op("dve", lambda e: e.tensor_tensor(out=r2[:, :], in0=rstd[:, :], in1=rstd[:, :], op=ALU.mult),
                 reads=[rstd], writes=[r2])
            for k in range(KC):
                ps = nps()
                for s in range(NS):
                    P.op("pe", lambda e, ps=ps, xt=xt, s=s, k=k: e.transpose(
                        out=ps[:, s * 128:(s + 1) * 128], in_=xt[:, s, k * 128:(k + 1) * 128], identity=ident[:, :]),
                        reads=[xt, ident], writes=[ps])
                P.op("act", lambda e, ps=ps, k=k: e.activation(
                    out=x1T[:, k, :], in_=ps[:, :TT], func=AF.Copy, scale=g2sb[:, k:k + 1]),
                    reads=[ps, g2sb], writes=[x1T])
            for f2 in range(FC // 2):
                ps = nps()
                for j in range(2):
                    f = f2 * 2 + j
                    for k in range(KC):
                        P.op("pe", _mm(ps[:, j * TT:(j + 1) * TT], w1s[k][:, f * 128:(f + 1) * 128], x1T[:, k, :],
                                       start=(k == 0), stop=(k == KC - 1)), reads=[w1s[k], x1T], writes=[ps])
                r = rl[f2 % 2]
                P.op("act", lambda e, ps=ps, r=r: e.activation(out=r[:, :], in_=ps[:, :], func=AF.Relu),
                     reads=[ps], writes=[r])
                P.op("pool", lambda e, r=r, f2=f2: e.tensor_tensor(
                    out=uT[:, 2 * f2:2 * f2 + 2, :], in0=r[:, :].rearrange("p (j t) -> p j t", j=2),
                    in1=r[:, :].rearrange("p (j t) -> p j t", j=2), op=ALU.mult), reads=[r], writes=[uT])
            for s in range(NS):
                for hf in range(2):
                    ps = nps()
                    for f in range(FC):
                        P.op("pe", _mm(ps[:, :], uT[:, f, s * 128:(s + 1) * 128],
                                       w2s[f // 4][:, f % 4, hf * 512:(hf + 1) * 512],
                                       start=(f == 0), stop=(f == FC - 1)), reads=[uT, w2s[f // 4]], writes=[ps])
                    P.op("dve", lambda e, ps=ps, xt=xt, s=s, hf=hf: e.scalar_tensor_tensor(
                        out=xt[:, s, hf * 512:(hf + 1) * 512], in0=ps[:, :], scalar=r2[:, s:s + 1],
                        in1=xt[:, s, hf * 512:(hf + 1) * 512], op0=ALU.mult, op1=ALU.add),
                        reads=[ps, r2, xt], writes=[xt])
            if final:
                for s in range(NS):
                    P.op("act", lambda e, s=s, xt=xt: e.activation(
                        out=junk[:, :], in_=xt[:, s, :], func=AF.Square, accum_out=ss[:, s:s + 1]),
                        reads=[xt], writes=[junk, ss])
                rsqrt(P, rstd, rstd[:, :], ss, ss[:, :], 1.0 / D_MODEL, epsb)
                for s in range(NS):
                    P.op("dve", lambda e, s=s, xt=xt: e.scalar_tensor_tensor(
                        out=xt[:, s, :], in0=xt[:, s, :], scalar=rstd[:, s:s + 1], in1=gfb[:, :],
                        op0=ALU.mult, op1=ALU.mult), reads=[xt, rstd, gfb], writes=[xt])
            P.dma("sp", out.ap[t0:t0 + TT, :].rearrange("(s p) d -> p s d", p=128), xt[:, :, :],
                  reads=[xt], writes=[out])
        P.finish_outputs([out])
        P.emit()
    return nc


def iota_mask(P, name, free, base, cm, step, dt_out=BF16, parts=128):
    it = P.sb(name + "_i", [parts, free], mybir.dt.int32)
    f = P.sb(name + "_f", [parts, free], F32)
    m = P.sb(name, [parts, free], dt_out)
    P.op("pool", lambda e: e.iota(it[:, :], pattern=[[step, free]], base=base, channel_multiplier=cm), writes=[it])
    P.op("dve", lambda e: e.tensor_copy(out=f[:, :], in_=it[:, :]), reads=[it], writes=[f])
    P.op("dve", lambda e: e.tensor_scalar(out=m[:, :], in0=f[:, :], scalar1=0.0, scalar2=NEG,
                                          op0=ALU.is_lt, op1=ALU.mult), reads=[f], writes=[m])
    return m


def build_da(S, layer_idx):
    lam_init = 0.8 - 0.6 * math.exp(-0.3 * layer_idx)
    scale = DA_QK ** -0.5
    nc = bass.Bass("TRN2", target_bir_lowering=False)
    with contextlib.ExitStack() as st:
        P = Prog(nc, st)
        qTd = P.dram("qT", [2, 32, S], F32, "ExternalInput")
        kTd = P.dram("kT", [2, 32, S], F32, "ExternalInput")
        vd = P.dram("v", [S, 64], F32, "ExternalInput")
        lamd = P.dram("lam4", [1, 128], F32, "ExternalInput")
        gaind = P.dram("gain", [64, 1], F32, "ExternalInput")
        yT = P.dram("yT", [64, S], F32, "ExternalOutput")
        NKT = S // 128
        NQT = S // 512
        q16 = [P.sb(f"q16_{m}", [128, S], BF16) for m in range(2)]
        k16 = [P.sb(f"k16_{m}", [128, S], BF16) for m in range(2)]
        for m in range(2):
            P.op("pool", lambda e, m=m: e.memset(q16[m][:, :], 0.0), writes=[q16[m]])
            P.op("dve", lambda e, m=m: e.memset(k16[m][:, :], 0.0), writes=[k16[m]])
        va = P.sb("va", [128, NKT, 65], BF16)
        epsb = const_col(P, "epsb", EPS)
        ident = make_ident(P)
        id16 = P.sb("id16", [128, 128], BF16)
        P.op("dve", lambda e: e.tensor_copy(out=id16[:, :], in_=ident[:, :]), reads=[ident], writes=[id16])
        masks = [iota_mask(P, f"mk{j}", 512, -128 * j, -1, 1) for j in range(4)]
        ones33 = P.sb("ones33", [32, 33], BF16)
        P.op("pool", lambda e: e.memset(ones33[:, :], 1.0), writes=[ones33])
        ones65 = P.sb("ones65", [65, 65], F32)
        P.op("pool", lambda e: e.memset(ones65[:, :], 1.0), writes=[ones65])
        gcol = P.sb("gcol", [64, 1], F32)
        P.dma("sp", gcol[:, :], gaind.ap[:, :], writes=[gcol])
        P.op("dve", lambda e: e.tensor_scalar(out=gcol[:, :], in0=gcol[:, :], scalar1=1.0 - lam_init, scalar2=None,
                                              op0=ALU.mult), reads=[gcol], writes=[gcol])
        lt = P.sb("lt", [65, 128], F32)
        lp = P.sb("lp", [65, 64], F32)
        l2 = P.sb("l2", [65, 2], F32)
        lam = P.sb("lam", [65, 1], F32)
        P.dma("sp", lt[64:65, :], lamd.ap[:, :], writes=[lt])
        P.op("dve", lambda e: e.tensor_tensor(
            out=lp[64:65, :].rearrange("p (a d) -> p a d", a=2),
            in0=lt[64:65, :].rearrange("p (a b d) -> p a b d", a=2, b=2)[:, :, 0, :],
            in1=lt[64:65, :].rearrange("p (a b d) -> p a b d", a=2, b=2)[:, :, 1, :], op=ALU.mult),
            reads=[lt], writes=[lp])
        P.op("dve", lambda e: e.reduce_sum(out=l2[64:65, :], in_=lp[64:65, :].rearrange("p (a d) -> p a d", a=2),
                                           axis=AX.X), reads=[lp], writes=[l2])
        P.op("act", lambda e: e.activation(out=l2[64:65, :], in_=l2[64:65, :], func=AF.Exp), reads=[l2], writes=[l2])
        P.op("dve", lambda e: e.tensor_tensor(out=lam[64:65, :], in0=l2[64:65, 0:1], in1=l2[64:65, 1:2],
                                              op=ALU.subtract), reads=[l2], writes=[lam])
        P.op("dve", lambda e: e.tensor_scalar(out=lam[64:65, :], in0=lam[64:65, :], scalar1=lam_init, scalar2=None,
                                              op0=ALU.add), reads=[lam], writes=[lam])
        for m in range(2):
            P.dma("pool", q16[m][0:32, :], qTd.ap[m, :, :], writes=[q16[m]])
            P.dma("pool", k16[m][0:32, :], kTd.ap[m, :, :], writes=[k16[m]])
            P.op("dve", lambda e, m=m: e.memset(k16[m][32:33, :], 1.0), writes=[k16[m]])
        vr = vd.ap.rearrange("(t p) d -> p t d", p=128)
        for c0 in range(0, NKT, 32):
            c1 = min(NKT, c0 + 32)
            P.dma("pool", va[:, c0:c1, 0:64], vr[:, c0:c1, :], writes=[va])
        P.op("dve", lambda e: e.memset(va[:, :, 64:65], 1.0), writes=[va])
        sq = [P.sb(f"sq{i}", [32, 512], BF16) for i in range(2)]
        pss = [P.ps(f"ps{i}", [128, 1024]) for i in range(2)]
        po = [P.ps(f"po{i}", [65, 512]) for i in range(2)]
        pe = P.ps("pe", [65, 1024])
        kmx = P.sb("kmx", [33, S // 512], F32)
        kmax = P.sb("kmax", [33, 2], F32)
        qn = P.sb("qn", [33, 512], F32)
        for m in range(2):
            for c in range(S // 512):
                t = sq[c % 2]
                ps = pss[c % 2]
                P.op("pool", lambda e, t=t, m=m, c=c: e.tensor_tensor(
                    out=t[:, :], in0=k16[m][0:32, c * 512:(c + 1) * 512], in1=k16[m][0:32, c * 512:(c + 1) * 512],
                    op=ALU.mult), reads=[k16[m]], writes=[t])
                P.op("pe", _mm(ps[0:33, 0:512], ones33[:, :], t[:, :]), reads=[ones33, t], writes=[ps])
                P.op("dve", lambda e, ps=ps, c=c: e.reduce_max(out=kmx[32:33, c:c + 1], in_=ps[32:33, 0:512], axis=AX.X),
                     reads=[ps], writes=[kmx])
            P.op("dve", lambda e, m=m: e.reduce_max(out=kmax[32:33, m:m + 1], in_=kmx[32:33, :], axis=AX.X),
                 reads=[kmx], writes=[kmax])
            P.op("act", lambda e, m=m: e.activation(out=kmax[32:33, m:m + 1], in_=kmax[32:33, m:m + 1], func=AF.Sqrt),
                 reads=[kmax], writes=[kmax])
            for c in range(S // 512):
                t = sq[c % 2]
                ps = pss[c % 2]
                P.op("pool", lambda e, t=t, m=m, c=c: e.tensor_tensor(
                    out=t[:, :], in0=q16[m][0:32, c * 512:(c + 1) * 512], in1=q16[m][0:32, c * 512:(c + 1) * 512],
                    op=ALU.mult), reads=[q16[m]], writes=[t])
                P.op("pe", _mm(ps[0:33, 0:512], ones33[:, :], t[:, :]), reads=[ones33, t], writes=[ps])
                P.op("act", lambda e, ps=ps: e.activation(out=qn[32:33, :], in_=ps[32:33, 0:512], func=AF.Sqrt),
                     reads=[ps], writes=[qn])
                P.op("dve", lambda e, m=m, c=c: e.tensor_scalar(
                    out=q16[m][32:33, c * 512:(c + 1) * 512], in0=qn[32:33, :], scalar1=kmax[32:33, m:m + 1],
                    scalar2=-1.0, op0=ALU.mult, op1=ALU.mult), reads=[qn, kmax], writes=[q16[m]])
        pbs = [P.sb(f"pb{i}", [128, 1024], BF16) for i in range(3)]
        osb = P.sb("osb", [65, 1024], F32)
        ab = P.sb("ab", [65, 1024], F32)
        o = P.sb("o", [64, 512], F32)
        o1 = P.sb("o1", [64, 512], F32)
        osq = P.sb("osq", [64, 512], F32)
        rr = P.sb("rr", [65, 512], F32)
        ysb = [P.sb(f"ysb{i}", [64, 512], F32) for i in range(2)]
        tiles = [(qi, kt) for qi in range(NQT) for kt in range(4 * qi + 4)]

        def emit_s(i):
            qi, kt = tiles[i]
            ps = pss[i % 2]
            j = kt - 4 * qi
            for m in range(2):
                P.op("pe", _mm(ps[:, m * 512:(m + 1) * 512], k16[m][0:128, kt * 128:(kt + 1) * 128],
                               q16[m][0:128, qi * 512:qi * 512 + 512], start=True, stop=(j < 0)),
                     reads=[k16[m], q16[m]], writes=[ps])
                if j >= 0:
                    P.op("pe", _mm(ps[:, m * 512:(m + 1) * 512], id16[:, :], masks[j][:, :], start=False, stop=True),
                         reads=[id16, masks[j]], writes=[ps])

        emit_s(0)
        for i, (qi, kt) in enumerate(tiles):
            q0 = qi * 512
            nkt = 4 * qi + 4
            ps = pss[i % 2]
            pb = pbs[i % 3]
            P.op("act", lambda e, ps=ps, pb=pb: e.activation(out=pb[:, :], in_=ps[:, :], func=AF.Exp, scale=scale),
                 reads=[ps], writes=[pb])
            if i + 1 < len(tiles):
                emit_s(i + 1)
            for m in range(2):
                P.op("pe", _mm(po[m][:, :], va[:, kt, :], pb[:, m * 512:(m + 1) * 512],
                               start=(kt == 0), stop=(kt == nkt - 1)), reads=[va, pb], writes=[po[m]])
            if kt != nkt - 1:
                continue
            for m in range(2):
                P.op("act", lambda e, m=m: e.activation(out=osb[:, m * 512:(m + 1) * 512], in_=po[m][:, :], func=AF.Copy),
                     reads=[po[m]], writes=[osb])
            P.op("act", lambda e: e.activation(out=ab[64:65, :], in_=osb[64:65, :], func=AF.Ln), reads=[osb], writes=[ab])
            P.op("act", lambda e: e.activation(out=ab[64:65, :], in_=ab[64:65, :], func=AF.Exp, scale=-1.0),
                 reads=[ab], writes=[ab])
            P.op("dve", lambda e: e.tensor_scalar(out=ab[64:65, 512:1024], in0=ab[64:65, 512:1024],
                                                  scalar1=lam[64:65, 0:1], scalar2=None, op0=ALU.mult),
                 reads=[ab, lam], writes=[ab])
            for m in range(2):
                P.op("pe", _mm(pe[0:64, m * 512:(m + 1) * 512], ones65[64:65, 0:64], ab[64:65, m * 512:(m + 1) * 512]),
                     reads=[ones65, ab], writes=[pe])
            P.op("dve", lambda e: e.tensor_tensor(out=o[:, :], in0=osb[0:64, 0:512], in1=pe[0:64, 0:512], op=ALU.mult),
                 reads=[osb, pe], writes=[o])
            P.op("dve", lambda e: e.tensor_tensor(out=o1[:, :], in0=osb[0:64, 512:1024], in1=pe[0:64, 512:1024],
                                                  op=ALU.mult), reads=[osb, pe], writes=[o1])
            P.op("dve", lambda e: e.tensor_tensor(out=o[:, :], in0=o[:, :], in1=o1[:, :], op=ALU.subtract),
                 reads=[o, o1], writes=[o])
            P.op("act", lambda e: e.activation(out=osq[:, :], in_=o[:, :], func=AF.Square), reads=[o], writes=[osq])
            P.op("pe", _mm(pe[0:65, 0:512], ones65[0:64, 0:65], osq[:, :]), reads=[ones65, osq], writes=[pe])
            rsqrt(P, rr, rr[64:65, :], pe, pe[64:65, 0:512], 1.0 / 64, epsb)
            P.op("pe", _mm(pe[0:64, 512:1024], ones65[64:65, 0:64], rr[64:65, :]), reads=[ones65, rr], writes=[pe])
            y = ysb[qi % 2]
            P.op("dve", lambda e, y=y: e.scalar_tensor_tensor(out=y[:, :], in0=o[:, :], scalar=gcol[:, 0:1],
                                                               in1=pe[0:64, 512:1024], op0=ALU.mult, op1=ALU.mult),
                 reads=[o, gcol, pe], writes=[y])
            P.dma("sp", yT.ap[:, q0:q0 + 512], y[:, :], reads=[y], writes=[yT])
        P.finish_outputs([yT])
        P.emit()
    return nc


def iota_cmp(P, name, pattern, base, cm, op, mul, dt_out, parts=128):
    free = 1
    for _, n in pattern:
        free *= n
    it = P.sb(name + "_i", [parts, free], mybir.dt.int32)
    f = P.sb(name + "_f", [parts, free], F32)
    m = P.sb(name, [parts, free], dt_out)
    P.op("pool", lambda e: e.iota(it[:, :], pattern=[list(x) for x in pattern], base=base, channel_multiplier=cm),
         writes=[it])
    P.op("dve", lambda e: e.tensor_copy(out=f[:, :], in_=it[:, :]), reads=[it], writes=[f])
    P.op("dve", lambda e: e.tensor_scalar(out=m[:, :], in0=f[:, :], scalar1=0.0, scalar2=mul, op0=op, op1=ALU.mult),
         reads=[f], writes=[m])
    return m


def build_ml(S):
    CH = 128
    NCH = S // CH
    NG = S // 512
    nc = bass.Bass("TRN2", target_bir_lowering=False)
    with contextlib.ExitStack() as st:
        P = Prog(nc, st)
        qTp = P.dram("qTp", [64, S + 3], F32, "ExternalInput")
        kTp = P.dram("kTp", [64, S + 3], F32, "ExternalInput")
        cwq = P.dram("cwq", [64, 4], F32, "ExternalInput")
        cwk = P.dram("cwk", [64, 4], F32, "ExternalInput")
        vd = P.dram("v", [S, 64], F32, "ExternalInput")
        oTd = P.dram("oT", [64, S], F32, "ExternalInput")
        ifgd = P.dram("ifg", [128, NCH, 2], F32, "ExternalInput")
        gbd = P.dram("gb", [1, 2], F32, "ExternalInput")
        gaind = P.dram("gain", [64, 1], F32, "ExternalInput")
        yT = P.dram("yT", [64, S], F32, "ExternalOutput")

        epsb = const_col(P, "epsb", EPS)
        oneb = const_col(P, "oneb", 1.0)
        ident = make_ident(P)
        id16 = P.sb("id16", [128, 128], BF16)
        P.op("dve", lambda e: e.tensor_copy(out=id16[:, :], in_=ident[:, :]), reads=[ident], writes=[id16])
        tri = iota_cmp(P, "tri", [[1, 128]], 0, -1, ALU.is_ge, 1.0, F32)
        negm4 = iota_cmp(P, "negm4", [[0, 4], [1, 128]], 0, -1, ALU.is_lt, NEG, F32)
        ones = P.sb("ones", [128, 128], F32)
        P.op("pool", lambda e: e.memset(ones[:, :], 1.0), writes=[ones])
        wq = P.sb("wq", [64, 4], F32)
        wk = P.sb("wk", [64, 4], F32)
        gcol = P.sb("gcol", [64, 1], F32)
        gb = P.sb("gbs", [128, 2], F32)
        ifg = P.sb("ifgs", [128, NCH, 2], F32)
        P.dma("sp", wq[:, :], cwq.ap[:, :], writes=[wq])
        P.dma("sp", wk[:, :], cwk.ap[:, :], writes=[wk])
        P.dma("sp", gcol[:, :], gaind.ap[:, :], writes=[gcol])
        P.dma("sp", gb[:, :], gbd.ap.partition_broadcast(128), writes=[gb])
        P.dma("sp", ifg[:, :, :], ifgd.ap[:, :, :], writes=[ifg])
        va = P.sb("va", [128, NCH, 65], BF16)
        vr = vd.ap.rearrange("(t p) d -> p t d", p=128)
        for c0 in range(0, NCH, 32):
            c1 = min(NCH, c0 + 32)
            P.dma("pool", va[:, c0:c1, 0:64], vr[:, c0:c1, :], writes=[va])
        P.op("dve", lambda e: e.memset(va[:, :, 64:65], 1.0), writes=[va])

        pA = P.ps("pA", [128, 512])
        pB = P.ps("pB", [128, 512])
        pC = P.ps("pC", [128, 512])
        pD = P.ps("pD", [128, 512])
        pE = P.ps("pE", [128, 512])
        pT = P.ps("pT", [128, 512], BF16)

        def G(name):
            return P.sb(name, [128, NCH], F32)
        ipre, fpre, ax, l1, logf, bcol, gall, wb, ua, eg = (G(n) for n in
                                                             ("ipre", "fpre", "ax", "l1", "logf", "bcol", "gall", "wb", "ua", "eg"))
        P.op("dve", lambda e: e.tensor_scalar(out=ipre[:, :], in0=ifg[:, :, 0], scalar1=gb[:, 0:1], scalar2=None,
                                              op0=ALU.add), reads=[ifg, gb], writes=[ipre])
        P.op("dve", lambda e: e.tensor_scalar(out=fpre[:, :], in0=ifg[:, :, 1], scalar1=gb[:, 1:2], scalar2=None,
                                              op0=ALU.add), reads=[ifg, gb], writes=[fpre])
        P.op("dve", lambda e: e.scalar_tensor_tensor(out=ax[:, :], in0=fpre[:, :], scalar=-1.0, in1=fpre[:, :],
                                                     op0=ALU.mult, op1=ALU.max),
             reads=[fpre], writes=[ax])
        P.op("act", lambda e: e.activation(out=ax[:, :], in_=ax[:, :], func=AF.Exp, scale=-1.0), reads=[ax], writes=[ax])
        P.op("act", lambda e: e.activation(out=l1[:, :], in_=ax[:, :], func=AF.Ln, bias=oneb[:, 0:1]),
             reads=[ax, oneb], writes=[l1])
        P.op("dve", lambda e: e.tensor_single_scalar(out=logf[:, :], in_=fpre[:, :], scalar=0.0, op=ALU.min),
             reads=[fpre], writes=[logf])
        P.op("dve", lambda e: e.tensor_tensor(out=logf[:, :], in0=logf[:, :], in1=l1[:, :], op=ALU.subtract),
             reads=[logf, l1], writes=[logf])
        rmax = P.sb("rmax", [128, 1], F32)
        mx = P.sb("mx", [1, 128], F32)
        mcol = P.sb("mcol", [128, 1], F32)
        negmc = P.sb("negmc", [128, 1], F32)
        emb = P.sb("emb", [128, 1], F32)
        P.op("dve", lambda e: e.reduce_max(out=rmax[:, :], in_=ipre[:, :], axis=AX.X), reads=[ipre], writes=[rmax])
        P.op("pe", lambda e: e.transpose(out=pA[0:1, 0:128], in_=rmax[:, 0:1], identity=ident[:, :]),
             reads=[rmax, ident], writes=[pA])
        P.op("dve", lambda e: e.reduce_max(out=mx[0:1, 0:1], in_=pA[0:1, 0:128], axis=AX.X), reads=[pA], writes=[mx])
        P.op("pe", _mm(pA[:, 0:1], ones[0:1, :], mx[0:1, 0:1]), reads=[ones, mx], writes=[pA])
        P.op("dve", lambda e: e.tensor_copy(out=mcol[:, :], in_=pA[:, 0:1]), reads=[pA], writes=[mcol])
        P.op("dve", lambda e: e.tensor_scalar(out=negmc[:, :], in0=mcol[:, :], scalar1=-1.0, scalar2=None, op0=ALU.mult),
             reads=[mcol], writes=[negmc])
        P.op("act", lambda e: e.activation(out=emb[:, :], in_=mcol[:, :], func=AF.Exp, scale=-1.0),
             reads=[mcol], writes=[emb])
        P.op("pe", _mm(pB[:, 0:NCH], tri[:, :], logf[:, :]), reads=[tri, logf], writes=[pB])
        P.op("pe", _mm(pC[:, 0:NCH], ones[:, :], logf[:, :]), reads=[ones, logf], writes=[pC])
        P.op("dve", lambda e: e.tensor_copy(out=bcol[:, :], in_=pB[:, 0:NCH]), reads=[pB], writes=[bcol])
        P.op("dve", lambda e: e.tensor_copy(out=gall[:, :], in_=pC[:, 0:NCH]), reads=[pC], writes=[gall])
        P.op("dve", lambda e: e.tensor_tensor(out=wb[:, :], in0=ipre[:, :], in1=bcol[:, :], op=ALU.subtract),
             reads=[ipre, bcol], writes=[wb])
        P.op("dve", lambda e: e.tensor_scalar(out=wb[:, :], in0=wb[:, :], scalar1=negmc[:, 0:1], scalar2=None,
                                              op0=ALU.add), reads=[wb, negmc], writes=[wb])
        P.op("dve", lambda e: e.tensor_tensor(out=ua[:, :], in0=gall[:, :], in1=wb[:, :], op=ALU.add),
             reads=[gall, wb], writes=[ua])
        P.op("act", lambda e: e.activation(out=ua[:, :], in_=ua[:, :], func=AF.Exp), reads=[ua], writes=[ua])
        P.op("act", lambda e: e.activation(out=eg[:, :], in_=gall[:, :], func=AF.Exp), reads=[gall], writes=[eg])

        def conv_silu(xs, w, acc, out_ap, out_buf, n, post_scale=None, tmp=None):
            P.op("dve", lambda e: e.tensor_scalar(out=acc[:, 0:n], in0=xs[:, 0:n], scalar1=w[:, 0:1], scalar2=None,
                                                  op0=ALU.mult), reads=[xs, w], writes=[acc])
            for j in range(1, 4):
                P.op("dve", lambda e, j=j: e.scalar_tensor_tensor(
                    out=acc[:, 0:n], in0=xs[:, j:j + n], scalar=w[:, j:j + 1], in1=acc[:, 0:n], op0=ALU.mult,
                    op1=ALU.add), reads=[xs, w, acc], writes=[acc])
            if post_scale is None:
                P.op("act", lambda e: e.activation(out=out_ap, in_=acc[:, 0:n], func=AF.Silu), reads=[acc],
                     writes=[out_buf])
            else:
                P.op("act", lambda e: e.activation(out=tmp[:, 0:n], in_=acc[:, 0:n], func=AF.Silu), reads=[acc],
                     writes=[tmp])
                P.op("pool", lambda e: e.tensor_scalar(out=out_ap, in0=tmp[:, 0:n], scalar1=post_scale, scalar2=None,
                                                       op0=ALU.mult), reads=[tmp], writes=[out_buf])

        k16 = P.sb("k16", [64, S], BF16)
        KB = min(1024, S)
        xst = [P.sb(f"xst{i}", [64, KB + 3], F32) for i in range(2)]
        acc = P.sb("acc", [64, KB], F32)
        tmpk = P.sb("tmpk", [64, KB], F32)
        for bi in range(S // KB):
            xs = xst[bi % 2]
            P.dma("sp", xs[:, :], kTp.ap[:, bi * KB:bi * KB + KB + 3], writes=[xs])
            conv_silu(xs, wk, acc, k16[:, bi * KB:(bi + 1) * KB], k16, KB, post_scale=ML_DIM ** -0.5, tmp=tmpk)
        ktok = P.sb("ktok", [128, NCH, 64], BF16)
        for c0 in range(0, NCH, 8):
            n = min(8, NCH - c0)
            for i in range(n):
                c = c0 + i
                P.op("pe", lambda e, c=c, i=i: e.transpose(out=pT[:, i * 64:(i + 1) * 64],
                                                           in_=k16[0:64, c * 128:(c + 1) * 128],
                                                           identity=id16[0:64, 0:64]),
                     reads=[k16, id16], writes=[pT])
            P.op("act", lambda e, c0=c0, n=n: e.activation(
                out=ktok[:, c0:c0 + n, :], in_=pT[:, 0:n * 64].rearrange("p (c d) -> p c d", d=64), func=AF.Copy),
                reads=[pT], writes=[ktok])
        vu = P.sb("vu", [128, NCH, 65], BF16)
        for c in range(NCH):
            P.op("pool" if c % 2 else "dve", lambda e, c=c: e.tensor_scalar(
                out=vu[:, c, :], in0=va[:, c, :], scalar1=ua[:, c:c + 1], scalar2=None, op0=ALU.mult),
                reads=[va, ua], writes=[vu])
        dl = P.sb("dl", [64, NCH, 65], F32)
        pds = [pD, pE]
        for gi, c0 in enumerate(range(0, NCH, 7)):
            n = min(7, NCH - c0)
            pd = pds[gi % 2]
            for i in range(n):
                c = c0 + i
                P.op("pe", _mm(pd[0:64, i * 65:(i + 1) * 65], ktok[:, c, :], vu[:, c, :]), reads=[ktok, vu], writes=[pd])
            P.op("act", lambda e, pd=pd, c0=c0, n=n: e.activation(
                out=dl[:, c0:c0 + n, :], in_=pd[0:64, 0:n * 65].rearrange("p (c d) -> p c d", d=65), func=AF.Copy),
                reads=[pd], writes=[dl])
        stt = P.sb("stt", [64, 65], F32)
        sts = P.sb("sts", [64, NCH, 65], BF16)
        P.op("dve", lambda e: e.memset(stt[:, :], 0.0), writes=[stt])
        for c in range(NCH):
            P.op("pool", lambda e, c=c: e.tensor_copy(out=sts[:, c, :], in_=stt[:, :]), reads=[stt], writes=[sts])
            if c + 1 < NCH:
                P.op("dve", lambda e, c=c: e.scalar_tensor_tensor(
                    out=stt[:, :], in0=stt[:, :], scalar=eg[0:64, c:c + 1], in1=dl[:, c, :], op0=ALU.mult, op1=ALU.add),
                    reads=[stt, eg, dl], writes=[stt])

        xq = [P.sb(f"xq{i}", [64, 515], F32) for i in range(2)]
        ot = [P.sb(f"ot{i}", [64, 512], F32) for i in range(2)]
        accq = P.sb("accq", [64, 512], F32)
        q16 = P.sb("q16", [64, 512], BF16)
        qs16 = P.sb("qs16", [64, 512], BF16)
        tl = P.sb("tl", [128, 512], F32)
        ebt = P.sb("ebt", [64, 512], F32)
        wt = P.sb("wt", [128, 512], F32)
        at16 = P.sb("at16", [128, 512], BF16)
        nsb = P.sb("nsb", [65, 512], F32)
        dd = P.sb("dd", [65, 512], F32)
        sg = P.sb("sg", [64, 512], F32)
        wv = P.sb("wv", [64, 512], F32)
        wsq = P.sb("wsq", [64, 512], F32)
        rr = P.sb("rr", [65, 512], F32)
        ysb = [P.sb(f"ysb{i}", [64, 512], F32) for i in range(2)]

        def loadg(gi):
            P.dma("sp", xq[gi % 2][:, :], qTp.ap[:, gi * 512:gi * 512 + 515], writes=[xq[gi % 2]])
            P.dma("sp", ot[gi % 2][:, :], oTd.ap[:, gi * 512:(gi + 1) * 512], writes=[ot[gi % 2]])

        loadg(0)
        for gi in range(NG):
            if gi + 1 < NG:
                loadg(gi + 1)
            t0 = gi * 512
            c0 = gi * 4
            conv_silu(xq[gi % 2], wq, accq, q16[:, :], q16, 512)
            for cl in range(4):
                P.op("pool", lambda e, cl=cl, c0=c0: e.tensor_scalar(
                    out=tl[:, cl * 128:(cl + 1) * 128], in0=tri[:, :], scalar1=logf[:, c0 + cl:c0 + cl + 1],
                    scalar2=None, op0=ALU.mult), reads=[tri, logf], writes=[tl])
            P.op("pe", _mm(pA[:, :], ones[:, :], tl[:, :], start=True, stop=False), reads=[ones, tl], writes=[pA])
            P.op("pe", _mm(pA[:, :], ident[:, :], negm4[:, :], start=False, stop=True), reads=[ident, negm4], writes=[pA])
            P.op("pe", _mm(pB[0:64, :], ones[:, 0:64], tl[:, :]), reads=[ones, tl], writes=[pB])
            P.op("act", lambda e: e.activation(out=ebt[:, :], in_=pB[0:64, :], func=AF.Exp), reads=[pB], writes=[ebt])
            P.op("dve", lambda e: e.tensor_tensor(out=qs16[:, :], in0=q16[:, :], in1=ebt[:, :], op=ALU.mult),
                 reads=[q16, ebt], writes=[qs16])
            for cl in range(4):
                P.op("act", lambda e, cl=cl, c0=c0: e.activation(
                    out=wt[:, cl * 128:(cl + 1) * 128], in_=pA[:, cl * 128:(cl + 1) * 128], func=AF.Exp,
                    bias=wb[:, c0 + cl:c0 + cl + 1]), reads=[pA, wb], writes=[wt])
            for cl in range(4):
                c = c0 + cl
                P.op("pe", _mm(pC[:, cl * 128:(cl + 1) * 128], k16[0:64, c * 128:(c + 1) * 128],
                               q16[0:64, cl * 128:(cl + 1) * 128]), reads=[k16, q16], writes=[pC])
            P.op("dve", lambda e: e.tensor_tensor(out=at16[:, :], in0=pC[:, :], in1=wt[:, :], op=ALU.mult),
                 reads=[pC, wt], writes=[at16])
            for cl in range(4):
                c = c0 + cl
                P.op("pe", _mm(pD[0:65, cl * 128:(cl + 1) * 128], va[:, c, :], at16[:, cl * 128:(cl + 1) * 128],
                               start=True, stop=False), reads=[va, at16], writes=[pD])
                P.op("pe", _mm(pD[0:65, cl * 128:(cl + 1) * 128], sts[:, c, :], qs16[0:64, cl * 128:(cl + 1) * 128],
                               start=False, stop=True), reads=[sts, qs16], writes=[pD])
            P.op("act", lambda e: e.activation(out=nsb[:, :], in_=pD[0:65, :], func=AF.Copy), reads=[pD], writes=[nsb])
            P.op("dve", lambda e: e.scalar_tensor_tensor(out=dd[64:65, :], in0=nsb[64:65, :], scalar=-1.0,
                                                         in1=nsb[64:65, :], op0=ALU.mult, op1=ALU.max),
                 reads=[nsb], writes=[dd])
            P.op("dve", lambda e: e.tensor_scalar(out=dd[64:65, :], in0=dd[64:65, :], scalar1=emb[64:65, 0:1],
                                                  scalar2=None, op0=ALU.max), reads=[dd, emb], writes=[dd])
            P.op("act", lambda e, gi=gi: e.activation(out=sg[:, :], in_=ot[gi % 2][:, :], func=AF.Sigmoid),
                 reads=[ot[gi % 2]], writes=[sg])
            P.op("dve", lambda e: e.tensor_tensor(out=wv[:, :], in0=sg[:, :], in1=nsb[0:64, :], op=ALU.mult),
                 reads=[sg, nsb], writes=[wv])
            P.op("act", lambda e: e.activation(out=wsq[:, :], in_=wv[:, :], func=AF.Square), reads=[wv], writes=[wsq])
            P.op("pe", _mm(pE[0:65, :], ones[0:64, 0:65], wsq[:, :]), reads=[ones, wsq], writes=[pE])
            P.op("dve", lambda e: e.scalar_tensor_tensor(out=rr[64:65, :], in0=dd[64:65, :], scalar=EPS, in1=dd[64:65, :],
                                                         op0=ALU.mult, op1=ALU.mult), reads=[dd], writes=[rr])
            P.op("dve", lambda e: e.scalar_tensor_tensor(out=rr[64:65, :], in0=pE[64:65, :], scalar=1.0 / 64,
                                                         in1=rr[64:65, :], op0=ALU.mult, op1=ALU.add),
                 reads=[pE, rr], writes=[rr])
            P.op("act", lambda e: e.activation(out=rr[64:65, :], in_=rr[64:65, :], func=AF.Sqrt), reads=[rr], writes=[rr])
            P.op("dve", lambda e: e.reciprocal(out=rr[64:65, :], in_=rr[64:65, :]), reads=[rr], writes=[rr])
            P.op("pe", _mm(pE[0:64, :], ones[64:65, 0:64], rr[64:65, :]), reads=[ones, rr], writes=[pE])
            y = ysb[gi % 2]
            P.op("dve", lambda e, y=y: e.scalar_tensor_tensor(out=y[:, :], in0=wv[:, :], scalar=gcol[:, 0:1],
                                                               in1=pE[0:64, :], op0=ALU.mult, op1=ALU.mult),
                 reads=[wv, gcol, pE], writes=[y])
            P.dma("sp", yT.ap[:, t0:t0 + 512], y[:, :], reads=[y], writes=[yT])
        P.finish_outputs([yT])
        P.emit()
    return nc


def build_nsa(S):
    NQ = S // 256
    NKT = S // 128
    NCB = (S - CMP_BLOCK) // CMP_STRIDE + 1
    NCP = ((NCB + 127) // 128) * 128
    NSB = S // SEL_BLOCK
    NJ = (NSB + 127) // 128
    SCW = NJ * 128
    scale = NSA_DIM ** -0.5
    nc = bass.Bass("TRN2", target_bir_lowering=False)
    with contextlib.ExitStack() as st:
        P = Prog(nc, st)
        qB = P.dram("qB", [NQ, 64, 512], F32, "ExternalInput")
        gB = P.dram("gB", [NQ, 1, 1536], F32, "ExternalInput")
        kcT = P.dram("kcT", [64, S], F32, "ExternalInput")
        vcT = P.dram("vcT", [64, S], F32, "ExternalInput")
        ksT = P.dram("ksT", [64, S], F32, "ExternalInput")
        kwT = P.dram("kwT", [64, S], F32, "ExternalInput")
        vsd = P.dram("vs", [S, 64], F32, "ExternalInput")
        vwd = P.dram("vw", [S, 64], F32, "ExternalInput")
        peT = P.dram("peT", [2, 64, 32], F32, "ExternalInput")
        w1d = P.dram("w1", [2, 2048, 256], F32, "ExternalInput")
        w2d = P.dram("w2", [2, 256, 64], F32, "ExternalInput")
        pard = P.dram("par", [1, 1], F32, "ExternalInput")
        yB = P.dram("yB", [NQ, 64, 512], F32, "ExternalOutput")

        ident = make_ident(P)
        id16 = P.sb("id16", [128, 128], BF16)
        P.op("dve", lambda e: e.tensor_copy(out=id16[:, :], in_=ident[:, :]), reads=[ident], writes=[id16])
        ones16 = P.sb("ones16", [64, 128], BF16)
        P.op("pool", lambda e: e.memset(ones16[:, :], 1.0), writes=[ones16])
        ones32 = P.sb("ones32", [65, 64], F32)
        P.op("pool", lambda e: e.memset(ones32[:, :], 1.0), writes=[ones32])
        tinyb = const_col(P, "tinyb", 1e-30)
        pcol = P.sb("pcol", [128, 1], F32)
        p128 = P.sb("p128", [128, 1], F32)
        P.dma("sp", pcol[:, :], pard.ap.partition_broadcast(128), writes=[pcol])
        P.op("dve", lambda e: e.tensor_scalar(out=p128[:, :], in0=pcol[:, :], scalar1=128.0, scalar2=None, op0=ALU.mult),
             reads=[pcol], writes=[p128])
        maskA = P.sb("maskA", [128, 512], BF16)
        maskB = P.sb("maskB", [128, 512], BF16)
        wmasks = {r: P.sb(f"wm{r + 4}", [128, 512], BF16) for r in range(-4, 2)}
        cmask = P.sb("cmask", [128, 32], F32)
        exall = P.sb("exall", [128, 64 * 128], BF16)
        m1w = P.sb("m1w", [128, 5], F32)
        m2w = P.sb("m2w", [128, 5], F32)
        _tscope = P.temp_scope()
        _tscope.__enter__()
        iti = P.sb("iti", [128, 512], mybir.dt.int32)
        tA = P.sb("tA", [128, 512], F32)
        tB = P.sb("tB", [128, 512], F32)

        def iota_to(dst, pattern, base, cm, free):
            P.op("pool", lambda e: e.iota(iti[:, 0:free], pattern=[list(x) for x in pattern], base=base,
                                          channel_multiplier=cm), writes=[iti])
            P.op("dve", lambda e: e.tensor_copy(out=dst[:, 0:free], in_=iti[:, 0:free]), reads=[iti], writes=[dst])

        def mask_to(dst_ap, dst_buf, src, off, psign, free=512, accumulate=False):
            P.op("dve", lambda e: e.tensor_scalar(
                out=tB[:, 0:free], in0=src[:, 0:free], scalar1=p128[:, 0:1], scalar2=float(off),
                op0=(ALU.add if psign > 0 else ALU.subtract), op1=ALU.add), reads=[src, p128], writes=[tB])
            if not accumulate:
                P.op("dve", lambda e: e.tensor_scalar(out=dst_ap, in0=tB[:, 0:free], scalar1=0.0, scalar2=NEG,
                                                      op0=ALU.is_gt, op1=ALU.mult), reads=[tB], writes=[dst_buf])
            else:
                P.op("dve", lambda e: e.tensor_scalar(out=tB[:, 0:free], in0=tB[:, 0:free], scalar1=0.0, scalar2=NEG,
                                                      op0=ALU.is_gt, op1=ALU.mult), reads=[tB], writes=[tB])
                P.op("dve", lambda e: e.tensor_tensor(out=dst_ap, in0=dst_ap, in1=tB[:, 0:free], op=ALU.add),
                     reads=[tB, dst_buf], writes=[dst_buf])

        kq = P.sb("kq", [128, 512], F32)
        nkq = P.sb("nkq", [128, 512], F32)
        iota_to(kq, [[0, 4], [-1, 128]], 0, 1, 512)
        P.op("dve", lambda e: e.tensor_scalar(out=nkq[:, :], in0=kq[:, :], scalar1=-1.0, scalar2=None, op0=ALU.mult),
             reads=[kq], writes=[nkq])
        mask_to(maskA[:, :], maskA, kq, 0, -1)
        mask_to(maskB[:, :], maskB, kq, 128, -1)
        for r in range(-4, 2):
            wm = wmasks[r]
            mask_to(wm[:, :], wm, kq, 128 * r, -1)
            mask_to(wm[:, :], wm, nkq, -128 * r - 511, 1, accumulate=True)
        cq = P.sb("cq", [128, 32], F32)
        iota_to(cq, [[16, 17]], -1, -1, 17)
        mask_to(cmask[:, 0:17], cmask, cq, 0, -1, free=17)
        for part in range(16):
            iota_to(tA, [[-2, 4], [-1, 2], [0, 64]], -8 * part, 1, 512)
            P.op("dve", lambda e, part=part: e.tensor_scalar(
                out=exall[:, part * 512:(part + 1) * 512], in0=tA[:, :], scalar1=0.0, scalar2=None, op0=ALU.is_equal),
                reads=[tA], writes=[exall])
        ge = P.sb("ge", [128, 8], F32)
        iota_to(ge, [[0, 8]], -64, 1, 8)
        P.op("dve", lambda e: e.tensor_scalar(out=ge[:, :], in0=ge[:, :], scalar1=0.0, scalar2=None, op0=ALU.is_ge),
             reads=[ge], writes=[ge])
        mk = P.sb("mk", [128, 4, 5], F32)
        P.op("dve", lambda e: e.memset(mk[:, :, :], 0.0), writes=[mk])

        def lin(dst, a, b):
            P.op("dve", lambda e: e.tensor_scalar(out=dst, in0=ge[:, 0:1], scalar1=float(a), scalar2=float(b),
                                                  op0=ALU.mult, op1=ALU.add), reads=[ge], writes=[mk])
        lin(mk[:, 0, 0:1], 1.0, 0.0)
        lin(mk[:, 1, 0:1], -10001.0, 10001.0)
        lin(mk[:, 1, 1:2], 0.0, 10002.0)
        lin(mk[:, 1, 2:3], 10004.0, -1.0)
        lin(mk[:, 1, 3:4], 0.0, -1.0)
        lin(mk[:, 1, 4:5], 0.0, -1.0)
        lin(mk[:, 2, 0:1], 0.0, 1.0)
        lin(mk[:, 2, 1:2], 0.0, 1.0)
        lin(mk[:, 2, 2:3], 1.0, 0.0)
        lin(mk[:, 3, 2:3], -10003.0, 10003.0)
        lin(mk[:, 3, 3:4], 0.0, 10004.0)
        lin(mk[:, 3, 4:5], 10006.0, -1.0)
        for dst, i0 in ((m1w, 0), (m2w, 1)):
            P.op("dve", lambda e, dst=dst, i0=i0: e.tensor_tensor(out=dst[:, :], in0=mk[:, i0 + 2, :], in1=mk[:, i0, :],
                                                                   op=ALU.subtract), reads=[mk], writes=[dst])
            P.op("dve", lambda e, dst=dst, i0=i0: e.scalar_tensor_tensor(
                out=dst[:, :], in0=dst[:, :], scalar=pcol[:, 0:1], in1=mk[:, i0, :], op0=ALU.mult, op1=ALU.add),
                reads=[dst, pcol, mk], writes=[dst])

        _tscope.__exit__(None, None, None)

        pS = [P.ps(f"pS{i}", [128, 512]) for i in range(2)]
        pO = [P.ps(f"pO{i}", [65, 512]) for i in range(3)]
        pC = P.ps("pC", [128, 1024])
        pT = P.ps("pT", [128, 512], BF16)

        ks16 = P.sb("ks16", [128, S], BF16)
        P.op("pool", lambda e: e.memset(ks16[:, :], 0.0), writes=[ks16])
        P.dma("pool", ks16[0:64, :], ksT.ap[:, :], writes=[ks16])
        P.op("dve", lambda e: e.memset(ks16[64:65, :], 1.0), writes=[ks16])
        vs16 = P.sb("vs16", [128, NKT, 65], BF16)
        vr = vsd.ap.rearrange("(t p) d -> p t d", p=128)
        for c0 in range(0, NKT, 32):
            c1 = min(NKT, c0 + 32)
            P.dma("pool", vs16[:, c0:c1, 0:64], vr[:, c0:c1, :], writes=[vs16])
        P.op("dve", lambda e: e.memset(vs16[:, :, 64:65], 1.0), writes=[vs16])

        CN = 256
        RW = 16 * (CN + 1)
        raw = P.sb("raw", [64, RW], BF16)
        sqt = [P.sb(f"sqt{i}", [64, 512], BF16) for i in range(2)]
        kmx = P.sb("kmx", [128, 64], F32)
        kst = P.sb("kst", [128, 4], F32)
        kn = [0]

        def kmax_chunks(src_buf, src_ap_fn, ncols):
            for c in range((ncols + 511) // 512):
                w = min(512, ncols - c * 512)
                t = sqt[kn[0] % 2]
                ps = pS[kn[0] % 2]
                P.op("pool", lambda e, t=t, c=c, w=w: e.tensor_tensor(
                    out=t[:, 0:w], in0=src_ap_fn(c * 512, w), in1=src_ap_fn(c * 512, w), op=ALU.mult),
                    reads=[src_buf], writes=[t])
                P.op("pe", _mm(ps[:, 0:w], ones16[:, 0:128], t[:, 0:w]), reads=[ones16, t], writes=[ps])
                P.op("dve", lambda e, ps=ps, w=w, k=kn[0]: e.reduce_max(out=kmx[:, k:k + 1], in_=ps[:, 0:w], axis=AX.X),
                     reads=[ps], writes=[kmx])
                kn[0] += 1

        def kmax_finish(col):
            n = kn[0]
            P.op("dve", lambda e: e.reduce_max(out=kst[:, col:col + 1], in_=kmx[:, 0:n], axis=AX.X), reads=[kmx],
                 writes=[kst])
            kn[0] = 0

        kmax_chunks(ks16, lambda c0, w: ks16[0:64, c0:c0 + w], S)
        kmax_finish(0)
        for t0 in range(0, S, 4096):
            tw = min(4096, S - t0)
            P.dma("pool", raw[:, 0:tw], kwT.ap[:, t0:t0 + tw], writes=[raw])
            kmax_chunks(raw, lambda c0, w: raw[0:64, c0:c0 + w], tw)
        kmax_finish(1)

        w1s = P.sb("w1s", [64, 32, 256], BF16)
        w2s = P.sb("w2s", [128, 2, 64], BF16)
        pes = P.sb("pes", [64, 32], BF16)
        hb = P.sb("hb", [128, 2], F32)
        xb = P.sb("xb", [128, 512], F32)
        x2 = P.sb("x2", [128, 512], F32)
        hid = P.sb("hid", [128, 2, NCP], BF16)
        P.op("dve", lambda e: e.memset(hid[:, :, :], 0.0), writes=[hid])
        kcmp = P.sb("kcmp", [64, NCP], BF16)
        P.op("dve", lambda e: e.memset(kcmp[:, :], 0.0), writes=[kcmp])
        vcmp = P.sb("vcmp", [128, NCP // 128, 64], BF16)
        c0g = math.sqrt(2.0 / math.pi)
        for kv, src in ((0, kcT), (1, vcT)):
            for t0 in range(0, 32, 8):
                P.dma("pool", w1s[:, t0:t0 + 8, :],
                      w1d.ap[kv, t0 * 64:(t0 + 8) * 64, :].rearrange("(t d) h -> d t h", d=64), writes=[w1s])
            P.dma("pool", w2s[:, :, :], w2d.ap[kv, :, :].rearrange("(c p) d -> p c d", p=128), writes=[w2s])
            P.dma("pool", pes[:, :], peT.ap[kv, :, :], writes=[pes])
            for hc in range(2):
                for t in range(32):
                    P.op("pe", _mm(pS[0][:, 0:1], w1s[:, t, hc * 128:(hc + 1) * 128], pes[:, t:t + 1],
                                   start=(t == 0), stop=(t == 31)), reads=[w1s, pes], writes=[pS[0]])
                P.op("dve", lambda e, hc=hc: e.tensor_copy(out=hb[:, hc:hc + 1], in_=pS[0][:, 0:1]), reads=[pS[0]],
                     writes=[hb])
            for n0 in range(0, NCB, CN):
                nw = min(CN, NCB - n0)
                P.dma("pool", raw[:, 0:16 * (nw + 1)], src.ap[:, 16 * n0:16 * (n0 + nw + 1)], writes=[raw])
                rawv = raw[:, 0:16 * (nw + 1)].rearrange("p (c s) -> p s c", s=16)
                for hc in range(2):
                    ps = pS[1]
                    for t in range(32):
                        P.op("pe", _mm(ps[:, 0:nw], w1s[:, t, hc * 128:(hc + 1) * 128],
                                       rawv[:, t % 16, t // 16:t // 16 + nw], start=(t == 0), stop=(t == 31)),
                             reads=[w1s, raw], writes=[ps])
                    P.op("act", lambda e, ps=ps, nw=nw, hc=hc: e.activation(
                        out=xb[:, 0:nw], in_=ps[:, 0:nw], func=AF.Identity, bias=hb[:, hc:hc + 1]),
                        reads=[ps, hb], writes=[xb])
                    P.op("act", lambda e, nw=nw: e.activation(out=x2[:, 0:nw], in_=xb[:, 0:nw], func=AF.Square),
                         reads=[xb], writes=[x2])
                    P.op("dve", lambda e, nw=nw: e.tensor_scalar(out=x2[:, 0:nw], in0=x2[:, 0:nw], scalar1=0.044715,
                                                                 scalar2=1.0, op0=ALU.mult, op1=ALU.add),
                         reads=[x2], writes=[x2])
                    P.op("dve", lambda e, nw=nw: e.tensor_tensor(out=x2[:, 0:nw], in0=x2[:, 0:nw], in1=xb[:, 0:nw],
                                                                 op=ALU.mult), reads=[x2, xb], writes=[x2])
                    P.op("act", lambda e, nw=nw: e.activation(out=x2[:, 0:nw], in_=x2[:, 0:nw], func=AF.Sigmoid,
                                                              scale=2.0 * c0g), reads=[x2], writes=[x2])
                    P.op("dve", lambda e, hc=hc, n0=n0, nw=nw: e.tensor_tensor(
                        out=hid[:, hc, n0:n0 + nw], in0=x2[:, 0:nw], in1=xb[:, 0:nw], op=ALU.mult),
                        reads=[x2, xb], writes=[hid])
            if kv == 0:
                for n0 in range(0, NCB, 512):
                    nw = min(512, NCB - n0)
                    ps = pS[0]
                    for hc in range(2):
                        P.op("pe", _mm(ps[0:64, 0:nw], w2s[:, hc, :], hid[:, hc, n0:n0 + nw], start=(hc == 0),
                                       stop=(hc == 1)), reads=[w2s, hid], writes=[ps])
                    P.op("act", lambda e, ps=ps, n0=n0, nw=nw: e.activation(out=kcmp[:, n0:n0 + nw], in_=ps[0:64, 0:nw],
                                                                            func=AF.Copy), reads=[ps], writes=[kcmp])
            else:
                for ct in range(NCP // 128):
                    ps = pS[ct % 2]
                    for hc in range(2):
                        P.op("pe", _mm(ps[:, 0:64], hid[:, hc, ct * 128:(ct + 1) * 128], w2s[:, hc, :], start=(hc == 0),
                                       stop=(hc == 1)), reads=[hid, w2s], writes=[ps])
                    P.op("act", lambda e, ps=ps, ct=ct: e.activation(out=vcmp[:, ct, :], in_=ps[:, 0:64], func=AF.Copy),
                         reads=[ps], writes=[vcmp])
        kmax_chunks(kcmp, lambda c0, w: kcmp[0:64, c0:c0 + w], NCB)
        kmax_finish(2)
        negsw = P.sb("negsw", [128, 1], F32)
        negc = P.sb("negc", [128, 1], F32)
        P.op("dve", lambda e: e.tensor_tensor(out=kst[:, 3:4], in0=kst[:, 0:1], in1=kst[:, 1:2], op=ALU.max),
             reads=[kst], writes=[kst])
        P.op("act", lambda e: e.activation(out=kst[:, 2:4], in_=kst[:, 2:4], func=AF.Sqrt), reads=[kst], writes=[kst])
        P.op("dve", lambda e: e.tensor_scalar(out=negsw[:, :], in0=kst[:, 3:4], scalar1=-1.0, scalar2=None, op0=ALU.mult),
             reads=[kst], writes=[negsw])
        P.op("dve", lambda e: e.tensor_scalar(out=negc[:, :], in0=kst[:, 2:3], scalar1=-scale, scalar2=None, op0=ALU.mult),
             reads=[kst], writes=[negc])

        q16 = [P.sb(f"q16_{i}", [128, 512], BF16) for i in range(2)]
        kwb = [P.sb(f"kwb{i}", [128, 768], BF16) for i in range(2)]
        vwb = [P.sb(f"vwb{i}", [128, 6, 65], BF16) for i in range(2)]
        for i in range(2):
            P.op("pool", lambda e, i=i: e.memset(q16[i][:, :], 0.0), writes=[q16[i]])
            P.op("pool", lambda e, i=i: e.memset(kwb[i][:, :], 0.0), writes=[kwb[i]])
            P.op("dve", lambda e, i=i: e.memset(kwb[i][64:65, :], 1.0), writes=[kwb[i]])
            P.op("dve", lambda e, i=i: e.memset(vwb[i][:, :, 64:65], 1.0), writes=[vwb[i]])
        gsb = [P.sb(f"gsb{i}", [65, 1536], F32) for i in range(2)]
        qsq = P.sb("qsq", [64, 512], BF16)
        qn = P.sb("qn", [65, 512], F32)
        biasA = [P.sb(f"biasA{i}", [128, 4], F32) for i in range(2)]
        den = P.sb("den", [128, 4], F32)
        rden = P.sb("rden", [128, 4], F32)
        p4 = P.sb("p4", [128, 4, NCP], F32)
        pn16 = P.sb("pn16", [128, 4, NCP], BF16)
        P.op("pool", lambda e: e.memset(pn16[:, :, :], 0.0), writes=[pn16])
        impP = P.sb("impP", [128, 4 + NCP + 8], F32)
        P.op("pool", lambda e: e.memset(impP[:, :], 0.0), writes=[impP])
        pslc = P.sb("pslc", [128, NSB], F32)
        sc = P.sb("sc", [128, SCW], F32)
        sc2 = P.sb("sc2", [128, SCW], F32)
        m8a = P.sb("m8a", [128, 8], F32)
        m8b = P.sb("m8b", [128, 8], F32)
        selm = P.sb("selm", [128, SCW], BF16)
        selT = [P.sb(f"selT{i}", [128, NJ, 512], BF16) for i in range(2)]
        pnT = [P.sb(f"pnT{i}", [128, 512], BF16) for i in range(2)]
        pbs = [P.sb(f"pb{i}", [128, 512], BF16) for i in range(3)]
        osb = [P.sb(f"osb{i}", [65, 3, 512], F32) for i in range(2)]
        frow = [P.sb(f"frow{i}", [65, 3, 512], F32) for i in range(2)]
        acc = P.sb("acc", [64, 512], F32)
        acc2 = P.sb("acc2", [64, 512], F32)
        ysb = [P.sb(f"ysb{i}", [64, 512], F32) for i in range(2)]

        def loadq(i):
            P.dma("pool", q16[i % 2][0:64, :], qB.ap[i, :, :], writes=[q16[i % 2]])
            k0 = max(0, 2 * i - 4)
            k1 = 2 * i + 2
            o0 = k0 - (2 * i - 4)
            P.dma("pool", kwb[i % 2][0:64, o0 * 128:6 * 128], kwT.ap[:, k0 * 128:k1 * 128], writes=[kwb[i % 2]])
            P.dma("pool", vwb[i % 2][:, o0:6, 0:64],
                  vwd.ap[k0 * 128:k1 * 128, :].rearrange("(t p) d -> p t d", p=128), writes=[vwb[i % 2]])
            P.dma("sp", gsb[i % 2][64:65, :], gB.ap[i, :, :], writes=[gsb[i % 2]])

        cnt = [0]

        def attend(q, tiles, acc_ps):
            def emit_s(n):
                k_ap, k_buf, _, _, extra = tiles[n]
                ps = pS[(cnt[0] + n) % 2]
                P.op("pe", _mm(ps[:, :], k_ap, q[0:128, :], start=True, stop=False), reads=[k_buf, q], writes=[ps])
                for xi, (l_ap, r_ap, bufs) in enumerate(extra):
                    P.op("pe", _mm(ps[:, :], l_ap, r_ap, start=False, stop=(xi == len(extra) - 1)), reads=bufs,
                         writes=[ps])
            emit_s(0)
            for n, (_, _, v_ap, v_buf, _) in enumerate(tiles):
                ps = pS[(cnt[0] + n) % 2]
                pb = pbs[(cnt[0] + n) % 3]
                P.op("act", lambda e, ps=ps, pb=pb: e.activation(out=pb[:, :], in_=ps[:, :], func=AF.Exp, scale=scale),
                     reads=[ps], writes=[pb])
                if n + 1 < len(tiles):
                    emit_s(n + 1)
                P.op("pe", _mm(acc_ps[0:65, :], v_ap, pb[:, :], start=(n == 0), stop=(n == len(tiles) - 1)),
                     reads=[v_buf, pb], writes=[acc_ps])
            cnt[0] += len(tiles)

        def prep(i):
            q = q16[i % 2]
            bA = biasA[i % 2]
            P.op("pool", lambda e: e.tensor_tensor(out=qsq[:, :], in0=q[0:64, :], in1=q[0:64, :], op=ALU.mult),
                 reads=[q], writes=[qsq])
            P.op("pe", _mm(pC[0:65, 0:512], ones16[:, 0:65], qsq[:, :]), reads=[ones16, qsq], writes=[pC])
            P.op("act", lambda e: e.activation(out=qn[64:65, :], in_=pC[64:65, 0:512], func=AF.Sqrt), reads=[pC],
                 writes=[qn])
            P.op("dve", lambda e: e.tensor_scalar(out=q[64:65, :], in0=qn[64:65, :], scalar1=negsw[64:65, 0:1],
                                                  scalar2=None, op0=ALU.mult), reads=[qn, negsw], writes=[q])
            for h in range(4):
                P.op("pe", _mm(pC[:, 512 + h:513 + h], qsq[:, h * 128:(h + 1) * 128], ones16[:, 0:1]),
                     reads=[qsq, ones16], writes=[pC])
            P.op("act", lambda e: e.activation(out=bA[:, :], in_=pC[:, 512:516], func=AF.Sqrt), reads=[pC], writes=[bA])
            P.op("dve", lambda e: e.tensor_scalar(out=bA[:, :], in0=bA[:, :], scalar1=negc[:, 0:1], scalar2=None,
                                                  op0=ALU.mult), reads=[bA, negc], writes=[bA])

        def ncv_of(i):
            return min(NCB, 16 * i + 15)

        def ncol_of(i):
            return min(NSB, 4 * i + 4)

        def phase_a(i):
            q = q16[i % 2]
            bA = biasA[i % 2]
            g = gsb[i % 2]
            P.op("act", lambda e: e.activation(out=g[64:65, :], in_=g[64:65, :], func=AF.Sigmoid), reads=[g], writes=[g])
            ncv = ncv_of(i)
            mlo = max(0, 16 * i - 2)
            mhi = min(ncv, 16 * i + 15)
            for h in range(4):
                for n0 in range(0, ncv, 512):
                    nw = min(512, ncv - n0)
                    P.op("pe", _mm(pC[:, n0:n0 + nw], q[0:64, h * 128:(h + 1) * 128], kcmp[:, n0:n0 + nw]),
                         reads=[q, kcmp], writes=[pC])
                P.op("dve", lambda e: e.tensor_tensor(
                    out=pC[:, mlo:mhi], in0=pC[:, mlo:mhi], in1=cmask[:, mlo - (16 * i - 2):mhi - (16 * i - 2)],
                    op=ALU.add), reads=[pC, cmask], writes=[pC])
                P.op("act", lambda e, h=h: e.activation(
                    out=p4[:, h, 0:ncv], in_=pC[:, 0:ncv], func=AF.Exp, scale=scale, bias=bA[:, h:h + 1],
                    accum_out=den[:, h:h + 1]), reads=[pC, bA], writes=[p4, den])
            P.op("dve", lambda e: e.tensor_scalar(out=rden[:, :], in0=den[:, :], scalar1=1e-30, scalar2=None, op0=ALU.max),
                 reads=[den], writes=[rden])
            P.op("dve", lambda e: e.reciprocal(out=rden[:, :], in_=rden[:, :]), reads=[rden], writes=[rden])
            for h in range(4):
                P.op("act", lambda e, h=h: e.activation(
                    out=pn16[:, h, 0:ncv], in_=p4[:, h, 0:ncv], func=AF.Copy, scale=rden[:, h:h + 1]),
                    reads=[p4, rden], writes=[pn16])
                if h == 0:
                    P.op("dve", lambda e: e.tensor_scalar(
                        out=impP[:, 4:4 + ncv], in0=p4[:, 0, 0:ncv], scalar1=rden[:, 0:1], scalar2=None, op0=ALU.mult),
                        reads=[p4, rden], writes=[impP])
                else:
                    P.op("dve", lambda e, h=h: e.scalar_tensor_tensor(
                        out=impP[:, 4:4 + ncv], in0=p4[:, h, 0:ncv], scalar=rden[:, h:h + 1], in1=impP[:, 4:4 + ncv],
                        op0=ALU.mult, op1=ALU.add), reads=[p4, rden, impP], writes=[impP])
            ncol = ncol_of(i)
            v0 = impP[:, 3:3 + 4 * (NSB + 1)].rearrange("p (j k) -> p j k", k=4)
            P.op("dve", lambda e: e.reduce_sum(out=pslc[:, :], in_=v0[:, 0:NSB, :], axis=AX.X), reads=[impP],
                 writes=[pslc])
            P.op("dve", lambda e: e.tensor_tensor(out=pslc[:, :], in0=pslc[:, :], in1=v0[:, 1:NSB + 1, 0], op=ALU.add),
                 reads=[pslc, impP], writes=[pslc])
            P.op("pool", lambda e: e.memset(sc[:, :], -1.0), writes=[sc])
            P.op("dve", lambda e: e.tensor_copy(out=sc[:, 0:ncol], in_=pslc[:, 0:ncol]), reads=[pslc], writes=[sc])
            jlo = max(0, 4 * i - 1)
            jhi = min(NSB, 4 * i + 4)
            a0 = jlo - (4 * i - 1)
            a1 = a0 + (jhi - jlo)
            P.op("dve", lambda e: e.tensor_tensor(out=sc[:, jlo:jhi], in0=sc[:, jlo:jhi], in1=m1w[:, a0:a1], op=ALU.mult),
                 reads=[sc, m1w], writes=[sc])
            P.op("dve", lambda e: e.tensor_tensor(out=sc[:, jlo:jhi], in0=sc[:, jlo:jhi], in1=m2w[:, a0:a1], op=ALU.add),
                 reads=[sc, m2w], writes=[sc])
            P.op("dve", lambda e: e.memset(sc[:, 0:1], 10000.0), writes=[sc])
            P.op("dve", lambda e: e.max(out=m8a[:, :], in_=sc[:, :]), reads=[sc], writes=[m8a])
            P.op("dve", lambda e: e.match_replace(out=sc2[:, :], in_to_replace=m8a[:, :], in_values=sc[:, :],
                                                  imm_value=-2.0), reads=[m8a, sc], writes=[sc2])
            P.op("dve", lambda e: e.max(out=m8b[:, :], in_=sc2[:, :]), reads=[sc2], writes=[m8b])
            P.op("dve", lambda e: e.tensor_scalar(out=selm[:, :], in0=sc[:, :], scalar1=m8b[:, 7:8], scalar2=NEG,
                                                  op0=ALU.is_lt, op1=ALU.mult), reads=[sc, m8b], writes=[selm])

        def phase_t(i):
            sT = selT[i % 2]
            for jc in range((ncol_of(i) + 127) // 128):
                for h in range(4):
                    P.op("pe", lambda e, jc=jc, h=h: e.transpose(out=pT[:, h * 128:(h + 1) * 128],
                                                                  in_=selm[:, jc * 128:(jc + 1) * 128],
                                                                  identity=id16[:, :]), reads=[selm, id16], writes=[pT])
                P.op("act", lambda e, jc=jc: e.activation(out=sT[:, jc, :], in_=pT[:, :], func=AF.Copy), reads=[pT],
                     writes=[sT])
            ncc = (ncv_of(i) + 127) // 128
            for cc in range(ncc):
                pt = pnT[cc % 2]
                for h in range(4):
                    P.op("pe", lambda e, cc=cc, h=h: e.transpose(out=pT[:, h * 128:(h + 1) * 128],
                                                                  in_=pn16[:, h, cc * 128:(cc + 1) * 128],
                                                                  identity=id16[:, :]), reads=[pn16, id16], writes=[pT])
                P.op("dve", lambda e, pt=pt: e.tensor_copy(out=pt[:, :], in_=pT[:, :]), reads=[pT], writes=[pt])
                P.op("pe", _mm(pO[2][0:64, :], vcmp[:, cc, :], pt[:, :], start=(cc == 0), stop=(cc == ncc - 1)),
                     reads=[vcmp, pt], writes=[pO[2]])

        def sel(i):
            sT = selT[i % 2]
            tiles = []
            for kt in range(2 * i + 2):
                ex = [(exall[:, (kt % 64) * 128:(kt % 64 + 1) * 128], sT[:, kt // 64, :], [exall, sT])]
                if kt == 2 * i:
                    ex.append((id16[:, :], maskA[:, :], [id16, maskA]))
                if kt == 2 * i + 1:
                    ex.append((id16[:, :], maskB[:, :], [id16, maskB]))
                tiles.append((ks16[0:128, kt * 128:(kt + 1) * 128], ks16, vs16[:, kt, :], vs16, ex))
            attend(q16[i % 2], tiles, pO[0])

        def win(i):
            tiles = []
            kw_, vw_ = kwb[i % 2], vwb[i % 2]
            for r in range(-4, 2):
                if 2 * i + r >= 0:
                    tiles.append((kw_[0:128, (r + 4) * 128:(r + 5) * 128], kw_, vw_[:, r + 4, :], vw_,
                                  [(id16[:, :], wmasks[r][:, :], [id16, wmasks[r]])]))
            attend(q16[i % 2], tiles, pO[1])

        def evac(i):
            ob = osb[i % 2]
            P.op("dve", lambda e: e.tensor_copy(out=ob[:, 0, :], in_=pO[0][0:65, :]), reads=[pO[0]], writes=[ob])
            P.op("dve", lambda e: e.tensor_copy(out=ob[:, 1, :], in_=pO[1][0:65, :]), reads=[pO[1]], writes=[ob])
            P.op("dve", lambda e: e.tensor_copy(out=ob[0:64, 2, :], in_=pO[2][0:64, :]), reads=[pO[2]], writes=[ob])

        def rows(i):
            ob, fr, g = osb[i % 2], frow[i % 2], gsb[i % 2]
            P.op("act", lambda e: e.activation(out=ob[64:65, 0:2, :], in_=ob[64:65, 0:2, :], func=AF.Ln,
                                               bias=tinyb[64:65, 0:1]), reads=[ob, tinyb], writes=[ob])
            P.op("act", lambda e: e.activation(out=ob[64:65, 0:2, :], in_=ob[64:65, 0:2, :], func=AF.Exp, scale=-1.0),
                 reads=[ob], writes=[ob])
            gv = g[64:65, :].rearrange("p (b n) -> p b n", b=3)
            P.op("dve", lambda e: e.tensor_tensor(out=fr[64:65, 1:3, :], in0=gv[:, 1:3, :], in1=ob[64:65, 0:2, :],
                                                  op=ALU.mult), reads=[g, ob], writes=[fr])
            P.op("pool", lambda e: e.tensor_copy(out=fr[64:65, 0, :], in_=gv[:, 0, :]), reads=[g], writes=[fr])

        def finish(i):
            ob, fr = osb[i % 2], frow[i % 2]
            P.op("pe", _mm(pS[0][0:64, :], ones32[64:65, 0:64], fr[64:65, 1, :]), reads=[ones32, fr], writes=[pS[0]])
            P.op("pe", _mm(pS[1][0:64, :], ones32[64:65, 0:64], fr[64:65, 2, :]), reads=[ones32, fr], writes=[pS[1]])
            P.op("dve", lambda e: e.tensor_tensor(out=acc[:, :], in0=ob[0:64, 0, :], in1=pS[0][0:64, :], op=ALU.mult),
                 reads=[ob, pS[0]], writes=[acc])
            P.op("dve", lambda e: e.tensor_tensor(out=acc2[:, :], in0=ob[0:64, 1, :], in1=pS[1][0:64, :], op=ALU.mult),
                 reads=[ob, pS[1]], writes=[acc2])
            P.op("pe", _mm(pS[0][0:64, :], ones32[64:65, 0:64], fr[64:65, 0, :]), reads=[ones32, fr], writes=[pS[0]])
            P.op("dve", lambda e: e.tensor_tensor(out=acc[:, :], in0=acc[:, :], in1=acc2[:, :], op=ALU.add),
                 reads=[acc, acc2], writes=[acc])
            P.op("dve", lambda e: e.tensor_tensor(out=acc2[:, :], in0=ob[0:64, 2, :], in1=pS[0][0:64, :], op=ALU.mult),
                 reads=[ob, pS[0]], writes=[acc2])
            y = ysb[i % 2]
            P.op("dve", lambda e: e.tensor_tensor(out=y[:, :], in0=acc[:, :], in1=acc2[:, :], op=ALU.add),
                 reads=[acc, acc2], writes=[y])
            P.dma("sp", yB.ap[i, :, :], y[:, :], reads=[y], writes=[yB])

        loadq(0)
        prep(0)
        phase_a(0)
        phase_t(0)
        for i in range(NQ):
            if i + 1 < NQ:
                loadq(i + 1)
            if i > 0:
                finish(i - 1)
            win(i)
            if i + 1 < NQ:
                prep(i + 1)
                phase_a(i + 1)
            sel(i)
            evac(i)
            if i + 1 < NQ:
                phase_t(i + 1)
            rows(i)
        finish(NQ - 1)
        P.finish_outputs([yB])
        P.emit()
    return nc


_PROGS = {}


def _prog(key, builder):
    if key not in _PROGS:
        _PROGS[key] = builder()
    return _PROGS[key]


def _run(nc, maps):
    res = run_bass_kernel_spmd(nc, maps, core_ids=list(range(len(maps))))
    return res.results


def _c(a):
    return np.ascontiguousarray(a, dtype=np.float32)


def kernel(x, norm1, w_in, ml_conv, ml_gate_bias, ml_norm, da_lambda, da_norm, nsa_pe, nsa_w1, nsa_w2, w_out, norm2,
           w_ff1, w_ff2, final_norm):
    x = np.asarray(x, np.float32)
    B, S, D = x.shape
    NCORE = 8
    QT = S * B // NCORE
    RPB = S // QT
    for l in range(DEPTH):
        nc = _prog(("l1", QT), lambda: build_l1(QT))
        maps = []
        for c in range(NCORE):
            b, r = divmod(c, RPB)
            xs = x[b, r * QT:(r + 1) * QT]
            maps.append({"x": _c(xs), "xT": _c(xs.T), "g": _c(norm1[l][:, None]), "w": _c(w_in[l])})
        res = _run(nc, maps)
        z = np.stack([np.concatenate([res[b * RPB + r]["z"] for r in range(RPB)], 0) for b in range(B)], 0)
        mixT = np.zeros((B, D_MODEL, S), np.float32)
        nc = _prog(("ml", S), lambda: build_ml(S))
        maps = []
        for c in range(NCORE):
            b, j = divmod(c, ML_HEADS)
            zb = z[b]
            pad = np.zeros((64, 3), np.float32)
            ifg = np.stack([zb[:, 1024 + j], zb[:, 1028 + j]], -1).reshape(S // 128, 128, 2).transpose(1, 0, 2)
            maps.append({
                "qTp": _c(np.concatenate([pad, zb[:, j * 64:(j + 1) * 64].T], 1)),
                "kTp": _c(np.concatenate([pad, zb[:, 256 + j * 64:256 + (j + 1) * 64].T], 1)),
                "cwq": _c(ml_conv[l][:, j * 64:(j + 1) * 64].T),
                "cwk": _c(ml_conv[l][:, 256 + j * 64:256 + (j + 1) * 64].T),
                "v": _c(zb[:, 512 + j * 64:512 + (j + 1) * 64]),
                "oT": _c(zb[:, 768 + j * 64:768 + (j + 1) * 64].T),
                "ifg": _c(ifg),
                "gb": _c(np.asarray([[ml_gate_bias[l][j], ml_gate_bias[l][ML_HEADS + j]]])),
                "gain": _c(ml_norm[l][j * 64:(j + 1) * 64][:, None]),
            })
        res = _run(nc, maps)
        for c in range(NCORE):
            b, j = divmod(c, ML_HEADS)
            mixT[b, j * 64:(j + 1) * 64] = res[c]["yT"]
        nc = _prog(("da", S, l), lambda: build_da(S, l))
        maps = []
        for c in range(NCORE):
            b, j = divmod(c, DA_HEADS)
            zb = z[b]
            maps.append({
                "qT": _c(zb[:, 1032:1288].reshape(S, 4, 2, 32)[:, j].transpose(1, 2, 0)),
                "kT": _c(zb[:, 1288:1544].reshape(S, 4, 2, 32)[:, j].transpose(1, 2, 0)),
                "v": _c(zb[:, 1544 + j * 64:1544 + (j + 1) * 64]),
                "lam4": _c(np.asarray(da_lambda[l]).reshape(1, 128)),
                "gain": _c(da_norm[l][j * 64:(j + 1) * 64][:, None]),
            })
        res = _run(nc, maps)
        for c in range(NCORE):
            b, j = divmod(c, DA_HEADS)
            mixT[b, 256 + j * 64:256 + (j + 1) * 64] = res[c]["yT"]
        nc = _prog(("nsa", S), lambda: build_nsa(S))
        NQ = S // 256
        maps = []
        for c in range(NCORE):
            b, gp = divmod(c, 4)
            g, p = divmod(gp, 2)
            zb = z[b]
            qg = zb[:, 1800:2312].reshape(S, 2, 4, 64)[:, g].reshape(S // 128, 128, 4, 64)[p::2]
            gg = zb[:, 3080:3104].reshape(S, 2, 4, 3)[:, g].reshape(S // 128, 128, 4, 3)[p::2]
            sl = slice(g * 64, (g + 1) * 64)
            maps.append({
                "qB": _c(qg.transpose(0, 3, 2, 1).reshape(NQ, 64, 512)),
                "gB": _c(gg.transpose(0, 3, 2, 1).reshape(NQ, 1, 1536)),
                "kcT": _c(zb[:, 2312:2440][:, sl].T), "vcT": _c(zb[:, 2440:2568][:, sl].T),
                "ksT": _c(zb[:, 2568:2696][:, sl].T), "kwT": _c(zb[:, 2824:2952][:, sl].T),
                "vs": _c(zb[:, 2696:2824][:, sl]), "vw": _c(zb[:, 2952:3080][:, sl]),
                "peT": _c(np.asarray(nsa_pe[l]).transpose(0, 2, 1)), "w1": _c(nsa_w1[l]), "w2": _c(nsa_w2[l]),
                "par": np.array([[float(p)]], np.float32),
            })
        res = _run(nc, maps)
        for c in range(NCORE):
            b, gp = divmod(c, 4)
            g, p = divmod(gp, 2)
            y = res[c]["yB"].reshape(NQ, 64, 4, 128)
            dst = mixT[b, 512 + g * 256:512 + (g + 1) * 256].reshape(4, 64, S // 128, 128)
            dst[:, :, p::2, :] = y.transpose(2, 1, 0, 3)
        final = (l == DEPTH - 1)
        nc = _prog(("l3", QT, final), lambda: build_l3(QT, final))
        maps = []
        for c in range(NCORE):
            b, r = divmod(c, RPB)
            maps.append({"mixT": _c(mixT[b][:, r * QT:(r + 1) * QT]), "x": _c(x[b, r * QT:(r + 1) * QT]),
                         "wo": _c(w_out[l]), "g2": _c(norm2[l][:, None]), "w1": _c(w_ff1[l]), "w2": _c(w_ff2[l]),
                         "gf": _c(np.asarray(final_norm)[None, :])})
        res = _run(nc, maps)
        x = np.stack([np.concatenate([res[b * RPB + r]["out"] for r in range(RPB)], 0) for b in range(B)], 0)
    return x.astype(np.float32)
```
